# Optimizing a Trainium2 kernel written in Bass

```python
import math
import jax, jax.numpy as jnp
from jax import lax
import numpy as np

D_MODEL = 2048
BATCH = 1
SEQ = 16384
DEPTH = 2
DEC_BATCH = 2
DEC_SEQ = 8192
PAST_LEN = 128

GRID_W = 64
N_HEADS = 8
HEAD_DIM = 128
ATT_WIDTH = N_HEADS * HEAD_DIM
KH_MAX = 8
KW = 16
F_GROUPS = 4
F_GROUP_DIM = 256
F_WIDTH = F_GROUPS * F_GROUP_DIM
IN_WIDTH = 3 * ATT_WIDTH + F_WIDTH
D_FF = ((8 * D_MODEL + 3 * 256 - 1) // (3 * 256)) * 256
LN_EPS = 1e-5
DEEPNORM_ALPHA = (2.0 * DEPTH) ** 0.25
DEEPNORM_BETA = (8.0 * DEPTH) ** -0.25

kernel_name = "hybrid_natten_fnet_deepnorm_encoder"


def _layer_norm(x, g, b):
    xf = x.astype(jnp.float32)
    mu = jnp.mean(xf, axis=-1, keepdims=True)
    var = jnp.mean(jnp.square(xf - mu), axis=-1, keepdims=True)
    y = (xf - mu) * lax.rsqrt(var + LN_EPS)
    return (y * g.astype(jnp.float32) + b.astype(jnp.float32)).astype(x.dtype)


def _neighbourhood_attention(q, k, v, rpb):
    B, T, H, Dh = q.shape
    rows = T // GRID_W
    kh = min(KH_MAX, rows)
    qg = q.reshape(B, rows, GRID_W, H, Dh)
    kg = k.reshape(B, rows, GRID_W, H, Dh)
    vg = v.reshape(B, rows, GRID_W, H, Dh)
    cols = np.arange(GRID_W)
    col_start = np.clip(cols - KW // 2, 0, GRID_W - KW)
    col_idx = col_start[:, None] + np.arange(KW)[None, :]
    col_bias_idx = col_idx - cols[:, None] + (KW - 1)
    rpb_cols = rpb[:, :, col_bias_idx].astype(jnp.float32)
    scale = HEAD_DIM ** -0.5

    def one_row(r):
        r_start = jnp.clip(r - kh // 2, 0, rows - kh)
        k_rows = lax.dynamic_slice_in_dim(kg, r_start, kh, axis=1)
        v_rows = lax.dynamic_slice_in_dim(vg, r_start, kh, axis=1)
        k_win = k_rows[:, :, col_idx]
        v_win = v_rows[:, :, col_idx]
        row_bias_idx = r_start - r + (KH_MAX - 1) + jnp.arange(kh)
        bias = jnp.take(rpb_cols, row_bias_idx, axis=1)
        bias = jnp.transpose(bias, (0, 2, 1, 3))
        q_row = lax.dynamic_index_in_dim(qg, r, axis=1, keepdims=False)
        s = jnp.einsum('bchd,bicjhd->bhcij', q_row, k_win).astype(jnp.float32)
        s = s * scale + bias[None]
        p = jax.nn.softmax(s.reshape(B, H, GRID_W, kh * KW), axis=-1)
        p = p.reshape(B, H, GRID_W, kh, KW).astype(v.dtype)
        return jnp.einsum('bhcij,bicjhd->bchd', p, v_win)

    out = lax.map(one_row, jnp.arange(rows))
    return jnp.moveaxis(out, 0, 1).reshape(B, T, H * Dh)


def _fourier_mix(u):
    B, T, C = u.shape
    ug = u.reshape(B, T, F_GROUPS, F_GROUP_DIM).astype(jnp.float32)
    mixed = jnp.fft.fft2(ug, axes=(1, 3), norm="ortho").real
    return mixed.reshape(B, T, C).astype(u.dtype)


def _trunk_layer(x, w_in, rpb, w_att, w_four, w_gate, b_gate, w_out, ln1_g, ln1_b,
                 w_ffn_gate, w_ffn_up, w_ffn_down, ln2_g, ln2_b):
    B, T, _ = x.shape
    proj = x @ w_in
    q, k, v, u = jnp.split(proj, [ATT_WIDTH, 2 * ATT_WIDTH, 3 * ATT_WIDTH], axis=-1)
    q = q.reshape(B, T, N_HEADS, HEAD_DIM)
    k = k.reshape(B, T, N_HEADS, HEAD_DIM)
    v = v.reshape(B, T, N_HEADS, HEAD_DIM)
    a = _neighbourhood_attention(q, k, v, rpb) @ w_att
    f = _fourier_mix(u) @ w_four
    gates = jax.nn.sigmoid(x @ w_gate + b_gate)
    g_a, g_f = jnp.split(gates, 2, axis=-1)
    mixed = (g_a * a + g_f * f) @ w_out
    x = _layer_norm(DEEPNORM_ALPHA * x + mixed, ln1_g, ln1_b)
    h = jax.nn.silu(x @ w_ffn_gate) * (x @ w_ffn_up)
    x = _layer_norm(DEEPNORM_ALPHA * x + h @ w_ffn_down, ln2_g, ln2_b)
    return x


def _trunk(x, ln_in_g, ln_in_b, w_in, rpb, w_att, w_four, w_gate, b_gate, w_out,
           ln1_g, ln1_b, w_ffn_gate, w_ffn_up, w_ffn_down, ln2_g, ln2_b):
    x = _layer_norm(x, ln_in_g, ln_in_b)
    for l in range(DEPTH):
        x = _trunk_layer(x, w_in[l], rpb[l], w_att[l], w_four[l], w_gate[l], b_gate[l],
                         w_out[l], ln1_g[l], ln1_b[l], w_ffn_gate[l], w_ffn_up[l],
                         w_ffn_down[l], ln2_g[l], ln2_b[l])
    return x


def setup_inputs(seed: int = 0) -> dict:
    key = jax.random.key(seed)
    ks = jax.random.split(key, 20)
    nrm = jax.random.normal
    f32 = jnp.float32
    return {
        "x_prompt": nrm(ks[0], (BATCH, SEQ, D_MODEL), f32),
        "x_sample": nrm(ks[1], (DEC_BATCH, DEC_SEQ, D_MODEL), f32),
        "ln_in_g": 1.0 + 0.02 * nrm(ks[2], (D_MODEL,), f32),
        "ln_in_b": 0.02 * nrm(ks[3], (D_MODEL,), f32),
        "w_in": nrm(ks[4], (DEPTH, D_MODEL, IN_WIDTH), f32) * D_MODEL ** -0.5,
        "rpb": 0.1 * nrm(ks[5], (DEPTH, N_HEADS, 2 * KH_MAX - 1, 2 * KW - 1), f32),
        "w_att": nrm(ks[6], (DEPTH, ATT_WIDTH, D_MODEL), f32) * ATT_WIDTH ** -0.5,
        "w_four": nrm(ks[7], (DEPTH, F_WIDTH, D_MODEL), f32) * F_WIDTH ** -0.5,
        "w_gate": nrm(ks[8], (DEPTH, D_MODEL, 2 * D_MODEL), f32) * D_MODEL ** -0.5,
        "b_gate": 0.01 * nrm(ks[9], (DEPTH, 2 * D_MODEL), f32),
        "w_out": nrm(ks[10], (DEPTH, D_MODEL, D_MODEL), f32) * (D_MODEL ** -0.5 * DEEPNORM_BETA),
        "ln1_g": 1.0 + 0.02 * nrm(ks[11], (DEPTH, D_MODEL), f32),
        "ln1_b": 0.02 * nrm(ks[12], (DEPTH, D_MODEL), f32),
        "w_ffn_gate": nrm(ks[13], (DEPTH, D_MODEL, D_FF), f32) * D_MODEL ** -0.5,
        "w_ffn_up": nrm(ks[14], (DEPTH, D_MODEL, D_FF), f32) * D_MODEL ** -0.5,
        "w_ffn_down": nrm(ks[15], (DEPTH, D_FF, D_MODEL), f32) * (D_FF ** -0.5 * DEEPNORM_BETA),
        "ln2_g": 1.0 + 0.02 * nrm(ks[16], (DEPTH, D_MODEL), f32),
        "ln2_b": 0.02 * nrm(ks[17], (DEPTH, D_MODEL), f32),
    }


def reference(x_prompt, x_sample, ln_in_g, ln_in_b, w_in, rpb, w_att, w_four, w_gate,
              b_gate, w_out, ln1_g, ln1_b, w_ffn_gate, w_ffn_up, w_ffn_down, ln2_g, ln2_b):
    y_prompt = _trunk(x_prompt, ln_in_g, ln_in_b, w_in, rpb, w_att, w_four, w_gate, b_gate,
                      w_out, ln1_g, ln1_b, w_ffn_gate, w_ffn_up, w_ffn_down, ln2_g, ln2_b)
    y_sample = _trunk(x_sample, ln_in_g, ln_in_b, w_in, rpb, w_att, w_four, w_gate, b_gate,
                      w_out, ln1_g, ln1_b, w_ffn_gate, w_ffn_up, w_ffn_down, ln2_g, ln2_b)
    return (y_prompt, y_sample)
```

```python
import math
from contextlib import ExitStack

import numpy as np
import ml_dtypes

import concourse.bass as bass
import concourse.mybir as mybir
from concourse.bass_utils import run_bass_kernel_spmd

F32 = mybir.dt.float32
BF16 = mybir.dt.bfloat16
I32 = mybir.dt.int32
AF = mybir.ActivationFunctionType
ALU = mybir.AluOpType
NPBF = ml_dtypes.bfloat16

NCORES = 8
NT = 4096
D = 2048
DFF = 5632
DEPTH = 2
NH = 8
ALPHA = (2.0 * DEPTH) ** 0.25
EPS = 1e-5
SCALE = 128 ** -0.5
NEG = -30000.0
GT = 16384
AGR = 6144

ENGS = ["tensor", "vector", "scalar", "gpsimd", "sync"]
import os as _os
_DEBUG_STOP = int(_os.environ["KSTOP"]) if "KSTOP" in _os.environ else None
_DEBUG_DUMP = ()
_DEBUG_LAYERS = None


class _StopBuild(Exception):
    pass


def _stop(n):
    if _DEBUG_STOP is not None and n >= _DEBUG_STOP:
        raise _StopBuild()


class Buf:
    __slots__ = ("name", "w", "r", "slot", "multi")

    def __init__(self, name, slot=None, multi=False):
        self.name = name
        self.w = {}
        self.r = {}
        self.slot = slot
        self.multi = multi


class Sched:
    def __init__(self, nc, n_dslots=56):
        self.nc = nc
        self.sems = {}
        self.count = {}
        self.ops = {e: [] for e in ENGS}
        self.seen = {e: {} for e in ENGS}
        for e in ENGS:
            self.sems[e] = nc.alloc_semaphore(name="m_" + e)
            self.count[e] = 0
        self.slots = []
        for i in range(n_dslots):
            key = "d%d" % i
            self.sems[key] = nc.alloc_semaphore(name=key)
            self.count[key] = 0
            self.slots.append(key)
        self.next_slot = 0
        self.dyn = {}

    def reset_slots(self):
        self.phase_first = self.next_slot

    def dbuf(self, name, multi=False):
        assert self.next_slot - getattr(self, "phase_first", 0) < len(self.slots), "out of DMA semaphore slots"
        key = self.slots[self.next_slot % len(self.slots)]
        self.next_slot += 1
        return Buf(name, slot=key, multi=multi)

    def buf(self, name, multi=False):
        return Buf(name, multi=multi)

    def _waits(self, eng, reads, writes):
        need = {}

        def add(d):
            for k, v in d.items():
                if need.get(k, 0) < v:
                    need[k] = v
        for b in reads:
            add(b.w)
        for b in writes:
            if not b.multi:
                add(b.w)
            add(b.r)
        out = []
        seen = self.seen[eng]
        for k, v in need.items():
            if seen.get(k, 0) >= v:
                continue
            seen[k] = v
            out.append((k, v))
        return out

    def _record(self, key, val, reads, writes):
        for b in reads:
            if b.r.get(key, 0) < val:
                b.r[key] = val
        for b in writes:
            if b.multi:
                if b.w.get(key, 0) < val:
                    b.w[key] = val
            else:
                b.w = {key: val}
                b.r = {}

    def op(self, eng, fn, reads=(), writes=(), inc=True):
        waits = self._waits(eng, reads, writes)
        if eng == "tensor":
            waits = [(k, v) for (k, v) in waits if k != "tensor"]
        val = self.count[eng] + 1
        if inc:
            self.count[eng] = val
        self._record(eng, val, reads, writes)
        self.ops[eng].append((waits, fn, (eng, 1) if inc else None))

    def dma(self, eng, fn, dbuf, reads=(), writes=()):
        waits = self._waits(eng, reads, writes)
        key = dbuf.slot
        self.count[key] += 16
        self._record(key, self.count[key], reads, writes)
        self.ops[eng].append((waits, fn, (key, 16)))

    def full_barrier(self, skip=()):
        tgt = {k: v for k, v in self.count.items() if v > 0 and k not in skip}
        for e in ENGS:
            waits = []
            for k, v in tgt.items():
                if k == e and e == "tensor":
                    continue
                if self.seen[e].get(k, 0) >= v:
                    continue
                self.seen[e][k] = v
                waits.append((k, v))
            if waits:
                self.ops[e].append((waits, None, None))

    def run(self, block, prologues=None):
        prologues = prologues or {}
        sems = self.sems

        def replay(eng, e):
            for waits, fn, inc in self.ops[eng]:
                for k, v in waits:
                    e.wait_ge(sems[k], v)
                if fn is None:
                    continue
                ins = fn(e)
                if inc is not None:
                    ins.then_inc(sems[inc[0]], inc[1])

        def mk(eng):
            def body(e):
                if eng in prologues:
                    with ExitStack() as st:
                        prologues[eng](e, st)
                        replay(eng, e)
                else:
                    replay(eng, e)
            return body
        block.tensor(mk("tensor"))
        block.vector(mk("vector"))
        block.scalar(mk("scalar"))
        block.gpsimd(mk("gpsimd"))
        block.sync(mk("sync"))


class Rot:
    def __init__(self, items):
        self.items = items
        self.i = 0

    def next(self):
        it = self.items[self.i % len(self.items)]
        self.i += 1
        return it


class Ctx:
    def __init__(self, nc, S):
        self.nc = nc
        self.S = S
        self.off = 17408
        self.uid = 0
        self.base = 17408

    def reset(self, skip=()):
        self.S.full_barrier(skip)
        self.S.reset_slots()
        self.off = self.base

    def tile(self, name, shape, dtype):
        esz = 4 if dtype in (F32, I32) else 2
        n = 1
        for s in shape[1:]:
            n *= s
        nbytes = n * esz
        self.off = (self.off + 63) // 64 * 64
        assert self.off + nbytes <= 224 * 1024, ("SBUF overflow", name, self.off, nbytes)
        self.uid += 1
        t = self.nc.alloc_sbuf_tensor_at("%s_%d" % (name, self.uid), list(shape), dtype, offset=self.off)
        self.off += nbytes
        return t

    def rot(self, name, n, shape, dtype, dma=True):
        items = []
        for i in range(n):
            t = self.tile("%s%d" % (name, i), shape, dtype)
            b = self.S.dbuf("%s%d" % (name, i)) if dma else self.S.buf("%s%d" % (name, i))
            items.append((t, b))
        return Rot(items)


def phase_transpose_in(C, x, xT, ident):
    S = C.S
    C.reset()
    xin = C.rot("xin", 2, [128, D], F32)
    stg = C.rot("tstg", 2, [128, 16, 128], F32)
    xT_v = xT.rearrange("(fc p) t -> p fc t", p=128)
    ev = 0
    for tc in range(NT // 128):
        xt, xb = xin.next()
        S.dma("sync", (lambda xt=xt, tc=tc: lambda e: e.dma_start(out=xt[:], in_=x[tc * 128:(tc + 1) * 128, :]))(),
              xb, writes=[xb])
        st, sb = stg.next()
        for grp in range(4):
            ps, pb = C.psum.next()
            for j in range(4):
                fc = grp * 4 + j
                S.op("tensor", (lambda ps=ps, xt=xt, fc=fc, j=j: lambda e: e.transpose(
                    ps[:, j * 128:(j + 1) * 128], xt[:, fc * 128:(fc + 1) * 128], ident[:]))(),
                    reads=[xb], writes=[pb], inc=(j == 3))
            eng = "vector" if ev % 2 == 0 else "scalar"
            ev += 1
            if eng == "vector":
                fn = (lambda st=st, ps=ps, grp=grp: lambda e: e.tensor_copy(
                    st[:, grp * 4:(grp + 1) * 4, :], ps[:].rearrange("p (a b) -> p a b", a=4)))()
            else:
                fn = (lambda st=st, ps=ps, grp=grp: lambda e: e.activation(
                    out=st[:, grp * 4:(grp + 1) * 4, :], in_=ps[:].rearrange("p (a b) -> p a b", a=4),
                    func=AF.Copy))()
            S.op(eng, fn, reads=[pb], writes=[sb])
        S.dma("gpsimd", (lambda st=st, tc=tc: lambda e: e.dma_start(
            out=xT_v[:, :, tc * 128:(tc + 1) * 128], in_=st[:]))(), sb, reads=[sb])


def phase_ln(C, yT, gcol, bcol, xres, xbf, ones_s, eps_t, final_out=None, ident=None):
    S = C.S
    C.reset()
    yv = yT.rearrange("(kc p) t -> p kc t", p=128)
    ytl = C.rot("lny", 2, [128, 16, 512], F32)
    ybf_t = C.tile("lnybf", [128, 16, 512], BF16)
    ybf_b = S.buf("lnybf")
    ysq_t = C.tile("lnysq", [128, 16, 512], BF16)
    ysq_b = S.buf("lnysq")
    mean_t = C.tile("lnmean", [128, 512], F32)
    mean_b = S.buf("lnmean")
    var_t = C.tile("lnvar", [128, 512], F32)
    var_b = S.buf("lnvar")
    tmp_t = C.tile("lntmp", [128, 512], F32)
    tmp_b = S.buf("lntmp")
    cen = C.rot("lncen", 2, [128, 512], F32, dma=False)
    o32 = C.rot("lno32", 1, [128, 16, 512], F32)
    if final_out is None:
        obf = C.rot("lnobf", 1, [128, 16, 512], BF16)
        xres_v = xres.rearrange("(kc p) t -> p kc t", p=128)
        xbf_v = xbf.rearrange("(kc p) t -> p kc t", p=128)
    else:
        otok = C.rot("lnotok", 2, [128, D], F32)
    for tt in range(NT // 512):
        y, yb = ytl.next()
        S.dma("sync", (lambda y=y, tt=tt: lambda e: e.dma_start(out=y[:], in_=yv[:, :, tt * 512:(tt + 1) * 512]))(),
              yb, writes=[yb])
        S.op("scalar", (lambda y=y: lambda e: e.activation(out=ybf_t[:], in_=y[:], func=AF.Copy))(),
             reads=[yb], writes=[ybf_b])
        S.op("scalar", (lambda y=y: lambda e: e.activation(out=ysq_t[:], in_=y[:], func=AF.Square))(),
             reads=[yb], writes=[ysq_b])
        psm, pmb = C.psum.next()
        psq, pqb = C.psum.next()
        for kc in range(16):
            S.op("tensor", (lambda psm=psm, kc=kc: lambda e: e.matmul(
                psm[:], ones_s[:], ybf_t[:, kc, :], start=(kc == 0), stop=(kc == 15)))(),
                reads=[ybf_b], writes=[pmb], inc=(kc == 15))
        for kc in range(16):
            S.op("tensor", (lambda psq=psq, kc=kc: lambda e: e.matmul(
                psq[:], ones_s[:], ysq_t[:, kc, :], start=(kc == 0), stop=(kc == 15)))(),
                reads=[ysq_b], writes=[pqb], inc=(kc == 15))
        S.op("scalar", (lambda psm=psm: lambda e: e.activation(out=mean_t[:], in_=psm[:], func=AF.Copy))(),
             reads=[pmb], writes=[mean_b])
        S.op("vector", lambda e: e.tensor_tensor(out=tmp_t[:], in0=mean_t[:], in1=mean_t[:], op=ALU.mult),
             reads=[mean_b], writes=[tmp_b])
        S.op("vector", (lambda psq=psq: lambda e: e.tensor_tensor(
            out=var_t[:], in0=psq[:], in1=tmp_t[:], op=ALU.subtract))(),
            reads=[pqb, tmp_b], writes=[var_b])
        S.op("scalar", lambda e: e.activation(out=var_t[:], in_=var_t[:], func=AF.Sqrt, bias=eps_t[:, 0:1], scale=1.0),
             reads=[var_b], writes=[var_b])
        S.op("vector", lambda e: e.reciprocal(out=var_t[:], in_=var_t[:]), reads=[var_b], writes=[var_b])
        o, ob = o32.next()
        for kc in range(16):
            c, cb = cen.next()
            S.op("vector", (lambda c=c, y=y, kc=kc: lambda e: e.tensor_tensor(
                out=c[:], in0=y[:, kc, :], in1=mean_t[:], op=ALU.subtract))(),
                reads=[yb, mean_b], writes=[cb])
            S.op("vector", (lambda c=c: lambda e: e.tensor_tensor(
                out=c[:], in0=c[:], in1=var_t[:], op=ALU.mult))(),
                reads=[cb, var_b], writes=[cb])
            S.op("scalar", (lambda c=c, o=o, kc=kc: lambda e: e.activation(
                out=o[:, kc, :], in_=c[:], func=AF.Identity, bias=bcol[:, kc:kc + 1], scale=gcol[:, kc:kc + 1]))(),
                reads=[cb], writes=[ob])
        if final_out is None:
            ob16, ob16b = obf.next()
            S.op("gpsimd", (lambda o=o, ob16=ob16: lambda e: e.tensor_copy(ob16[:], o[:]))(),
                 reads=[ob], writes=[ob16b])
            S.dma("sync", (lambda o=o, tt=tt: lambda e: e.dma_start(
                out=xres_v[:, :, tt * 512:(tt + 1) * 512], in_=o[:]))(), ob, reads=[ob])
            S.dma("sync", (lambda ob16=ob16, tt=tt: lambda e: e.dma_start(
                out=xbf_v[:, :, tt * 512:(tt + 1) * 512], in_=ob16[:]))(), ob16b, reads=[ob16b])
        else:
            ev = 0
            for ts in range(4):
                ot, otb = otok.next()
                for grp in range(4):
                    ps, pb = C.psum.next()
                    for j in range(4):
                        kc = grp * 4 + j
                        S.op("tensor", (lambda ps=ps, o=o, kc=kc, j=j, ts=ts: lambda e: e.transpose(
                            ps[:, j * 128:(j + 1) * 128], o[:, kc, ts * 128:(ts + 1) * 128], ident[:]))(),
                            reads=[ob], writes=[pb], inc=(j == 3))
                    eng = "vector" if ev % 2 == 0 else "scalar"
                    ev += 1
                    if eng == "vector":
                        fn = (lambda ot=ot, ps=ps, grp=grp: lambda e: e.tensor_copy(
                            ot[:, grp * 512:(grp + 1) * 512], ps[:]))()
                    else:
                        fn = (lambda ot=ot, ps=ps, grp=grp: lambda e: e.activation(
                            out=ot[:, grp * 512:(grp + 1) * 512], in_=ps[:], func=AF.Copy))()
                    S.op(eng, fn, reads=[pb], writes=[otb])
                r0 = tt * 512 + ts * 128
                S.dma("sync", (lambda ot=ot, r0=r0: lambda e: e.dma_start(
                    out=final_out[r0:r0 + 128, :], in_=ot[:]))(), otb, reads=[otb])


class WLoader:
    def __init__(self, C, name, w_ap, KC, stage_rot, n_slab=2, is_bf16=False):
        self.C = C
        self.w = w_ap
        self.KC = KC
        self.stage = stage_rot
        self.is_bf16 = is_bf16
        self.slabs = C.rot(name + "sl", n_slab, [128, KC, 256], BF16, dma=is_bf16)
        self.wv = w_ap.rearrange("(kc p) n -> p kc n", p=128)
        self.cast_i = 0
        if KC <= 16:
            self.pieces = [(0, KC)]
        else:
            assert KC % 11 == 0
            self.pieces = [(i, 11) for i in range(0, KC, 11)]
        self.stages = []
        if not is_bf16:
            for i, (k0, nk) in enumerate(self.pieces):
                t = C.tile("%sst%d" % (name, i), [128, nk, 256], F32)
                self.stages.append((t, C.S.dbuf("%sst%d" % (name, i))))

    def load(self, c0):
        S = self.C.S
        sl, slb = self.slabs.next()
        if self.is_bf16:
            S.dma("sync", (lambda sl=sl, c0=c0: lambda e: e.dma_start(out=sl[:], in_=self.wv[:, :, c0:c0 + 256]))(),
                  slb, writes=[slb])
            return (sl, slb, [])
        parts = []
        for i, (k0, nk) in enumerate(self.pieces):
            st, stb = self.stages[i]
            S.dma("sync", (lambda st=st, k0=k0, nk=nk, c0=c0: lambda e: e.dma_start(
                out=st[:, 0:nk, :], in_=self.wv[:, k0:k0 + nk, c0:c0 + 256]))(), stb, writes=[stb])
            parts.append((st, stb, k0, nk))
        return (sl, slb, parts)

    def cast(self, h):
        S = self.C.S
        sl, slb, parts = h
        for (st, stb, k0, nk) in parts:
            eng = "vector" if self.cast_i % 2 == 0 else "scalar"
            self.cast_i += 1
            if eng == "vector":
                fn = (lambda sl=sl, st=st, k0=k0, nk=nk: lambda e: e.tensor_copy(sl[:, k0:k0 + nk, :], st[:, 0:nk, :]))()
            else:
                fn = (lambda sl=sl, st=st, k0=k0, nk=nk: lambda e: e.activation(
                    out=sl[:, k0:k0 + nk, :], in_=st[:, 0:nk, :], func=AF.Copy))()
            S.op(eng, fn, reads=[stb], writes=[slb])
        return (sl, slb)


def gemm_fm(C, *, T, TB, acts, weights, n_oc, epilogue, prologue_tb=None):
    S = C.S
    C.reset()
    atiles = []
    for i, (a, KC) in enumerate(acts):
        t = C.tile("act%d" % i, [128, KC, TB], BF16)
        b = S.dbuf("act%d" % i)
        atiles.append((t, b, a, KC))
    stage = None
    loaders = []
    for j, (w_ap, KC, ai, col0, isb) in enumerate(weights):
        loaders.append(WLoader(C, "w%d" % j, w_ap, KC, stage, is_bf16=isb))
    ep_state = epilogue("alloc", None, None)
    n_oc2 = n_oc // 2
    for tb in range(T // TB):
        for (t, b, a, KC) in atiles:
            if callable(a):
                fn = (lambda t=t, a=a, tb=tb: lambda e: e.dma_start(out=t[:], in_=a(tb, e)))()
            else:
                src = a.rearrange("(kc p) t -> p kc t", p=128)[:, :, tb * TB:(tb + 1) * TB]
                fn = (lambda t=t, src=src: lambda e: e.dma_start(out=t[:], in_=src))()
            S.dma("scalar" if callable(a) else "sync", fn, b, writes=[b])
        if prologue_tb is not None:
            prologue_tb(tb)
        handles = [ld.load(weights[j][3]) for j, ld in enumerate(loaders)]
        slabs = [ld.cast(h) for ld, h in zip(loaders, handles)]
        for oc2 in range(n_oc2):
            nxt = None
            if oc2 + 1 < n_oc2:
                nxt = [ld.load(weights[j][3] + (oc2 + 1) * 256) for j, ld in enumerate(loaders)]
            for half in range(2):
                oc = oc2 * 2 + half
                for st in range(TB // 512):
                    psums = []
                    for j, (w_ap, KC, ai, col0, isb) in enumerate(weights):
                        ps, pb = C.psum.next()
                        at, ab = atiles[ai][0], atiles[ai][1]
                        sl, slb = slabs[j]
                        for kc in range(KC):
                            S.op("tensor", (lambda ps=ps, sl=sl, at=at, kc=kc, half=half, st=st, KC=KC: lambda e: e.matmul(
                                ps[:], sl[:, kc, half * 128:(half + 1) * 128], at[:, kc, st * 512:(st + 1) * 512],
                                start=(kc == 0), stop=(kc == KC - 1)))(),
                                reads=[slb, ab], writes=[pb], inc=(kc == KC - 1))
                        psums.append((ps, pb))
                    epilogue(oc, tb * TB + st * 512, psums)
                    if half == 0 and st == 0 and nxt is not None:
                        slabs_next = [ld.cast(h) for ld, h in zip(loaders, nxt)]
            if nxt is not None:
                slabs = slabs_next


def gemm_tm(C, *, T, TB, act, KC, w_ap, col0, n_blk, sink):
    S = C.S
    C.reset()
    at = C.tile("tmact", [128, KC, TB], BF16)
    ab = S.dbuf("tmact")
    stage = C.rot("tmst", 3, [128, 8, 512], F32)
    slabs = C.rot("tmsl", 2, [128, KC, 512], BF16, dma=False)
    outs = C.rot("tmout", 3, [128, 512], BF16)
    wv = w_ap.rearrange("(kc p) n -> p kc n", p=128)
    av = act.rearrange("(kc p) t -> p kc t", p=128)
    ci = 0
    ev = 0
    for tb in range(T // TB):
        S.dma("sync", (lambda tb=tb: lambda e: e.dma_start(out=at[:], in_=av[:, :, tb * TB:(tb + 1) * TB]))(),
              ab, writes=[ab])
        for blk in range(n_blk):
            c0 = col0 + blk * 512
            sl, slb = slabs.next()
            for k0 in range(0, KC, 8):
                st, stb = stage.next()
                S.dma("sync", (lambda st=st, k0=k0, c0=c0: lambda e: e.dma_start(
                    out=st[:], in_=wv[:, k0:k0 + 8, c0:c0 + 512]))(), stb, writes=[stb])
                eng = "vector" if ci % 2 == 0 else "scalar"
                ci += 1
                if eng == "vector":
                    fn = (lambda sl=sl, st=st, k0=k0: lambda e: e.tensor_copy(sl[:, k0:k0 + 8, :], st[:]))()
                else:
                    fn = (lambda sl=sl, st=st, k0=k0: lambda e: e.activation(out=sl[:, k0:k0 + 8, :], in_=st[:], func=AF.Copy))()
                S.op(eng, fn, reads=[stb], writes=[slb])
            for tch in range(TB // 128):
                ps, pb = C.psum.next()
                for kc in range(KC):
                    S.op("tensor", (lambda ps=ps, sl=sl, kc=kc, tch=tch: lambda e: e.matmul(
                        ps[:], at[:, kc, tch * 128:(tch + 1) * 128], sl[:, kc, :],
                        start=(kc == 0), stop=(kc == KC - 1)))(),
                        reads=[slb, ab], writes=[pb], inc=(kc == KC - 1))
                o, ob = outs.next()
                eng = "vector" if ev % 2 == 0 else "scalar"
                ev += 1
                if eng == "vector":
                    fn = (lambda o=o, ps=ps: lambda e: e.tensor_copy(o[:], ps[:]))()
                else:
                    fn = (lambda o=o, ps=ps: lambda e: e.activation(out=o[:], in_=ps[:], func=AF.Copy))()
                S.op(eng, fn, reads=[pb], writes=[ob])
                sink(blk, tb * TB + tch * 128, o, ob)


def build_program():
    nc = bass.Bass("TRN2", target_bir_lowering=False)
    def dt(name, shape, dtype, **kw):
        if name in _DEBUG_DUMP and "kind" not in kw:
            kw["kind"] = "ExternalOutput"
        return nc.dram_tensor(name, shape, dtype, **kw)

    def ext_in(name, shape, dtype):
        return dt(name, list(shape), dtype, kind="ExternalInput").ap()

    x = ext_in("x", [NT, D], F32)
    yout = dt("y", [NT, D], F32, kind="ExternalOutput").ap()
    ln_in_g = ext_in("ln_in_g", [D], F32)
    ln_in_b = ext_in("ln_in_b", [D], F32)
    w_in = ext_in("w_in", [DEPTH, D, 4096], F32)
    w_att = ext_in("w_att", [DEPTH, 1024, D], F32)
    w_four = ext_in("w_four", [DEPTH, 1024, D], F32)
    w_gate = ext_in("w_gate", [DEPTH, D, 4096], F32)
    b_gate = ext_in("b_gate", [DEPTH, 4096], F32)
    w_out = ext_in("w_out", [DEPTH, D, D], F32)
    ln1_g = ext_in("ln1_g", [DEPTH, D], F32)
    ln1_b = ext_in("ln1_b", [DEPTH, D], F32)
    w_fg = ext_in("w_ffn_gate", [DEPTH, D, DFF], F32)
    w_fu = ext_in("w_ffn_up", [DEPTH, D, DFF], F32)
    w_fd = ext_in("w_ffn_down", [DEPTH, DFF, D], F32)
    ln2_g = ext_in("ln2_g", [DEPTH, D], F32)
    ln2_b = ext_in("ln2_b", [DEPTH, D], F32)
    c_bf = ext_in("c_bf", [128, 2560], BF16)
    c_f32 = ext_in("c_f32", [128, 1160], F32)
    btile = ext_in("btile", [DEPTH, 128, 8, NH, 256], F32)
    meta = ext_in("meta", [1, 16], I32)

    MB = 1 << 20
    scrA = nc.dram_tensor("scrA", [D * NT], F32).ap()
    nB = (16 + 46 + 48 + 16 + 64) * MB // 2
    scrB = nc.dram_tensor("scrB", [nB], BF16).ap()

    def regB(off_mb, rows, cols):
        o = off_mb * MB // 2
        return scrB[o:o + rows * cols].rearrange("(r c) -> r c", c=cols)

    dump_list = []

    def dbg_or(name, ap, rows, cols, dty):
        if name in _DEBUG_DUMP:
            dump_list.append((name, ap, rows, cols, dty))
        return ap

    xres = dbg_or("xres", scrA.rearrange("(r c) -> r c", c=NT), D, NT, F32)
    xbf = dbg_or("xbf", regB(0, D, NT), D, NT, BF16)
    qT = dbg_or("qT", regB(16, 1024, NT), 1024, NT, BF16)
    kext = dbg_or("kext", regB(24, 1024, 4608), 1024, 4608, BF16)
    vext = dbg_or("vext", regB(33, 4608, 1024), 4608, 1024, BF16)
    agin = regB(42, AGR, 1024)
    w4p = dbg_or("w4p", regB(42, D, D), D, D, BF16)
    attT = dbg_or("attT", regB(54, 1024, NT), 1024, NT, BF16)
    hT = dbg_or("hT", regB(16, DFF, NT), DFF, NT, BF16)
    agout = regB(62, 4 * AGR, 1024)
    mT = dbg_or("mT", regB(62, D, NT), D, NT, BF16)
    agin2 = regB(110, 512, GT)
    agout2 = regB(126, 2048, GT)
    o2 = 126 * MB // 2
    yT = dbg_or("yT", scrB[o2:o2 + 2 * D * NT].bitcast(F32).rearrange("(r c) -> r c", c=NT), D, NT, F32)

    S = Sched(nc)
    C = Ctx(nc, S)

    cbf_t = C.tile("cbf", [128, 2560], BF16)
    cf_t = C.tile("cf32", [128, 1160], F32)
    lnp_t = C.tile("lnp", [128, 10, 16], F32)
    bg_t = C.tile("bgate", [128, DEPTH, 32], F32)
    C.base = (C.off + 63) // 64 * 64
    ones_s = cbf_t[:, 0:128]
    ones1 = cbf_t[:, 128:256]
    Rm = cbf_t[:, 256:768]
    M3 = cbf_t[:, 768:1536].rearrange("p (a b) -> p a b", a=6)
    CS = cbf_t[:, 1536:2560].rearrange("p (r k c) -> p r k c", r=2, k=2)
    ident = cf_t[:, 0:128]
    Twr2 = cf_t[:, 128:640]
    Twi2 = cf_t[:, 640:1152]
    eps_t = cf_t[:, 1152:1153]

    C.psum = Rot([(nc.alloc_psum_tensor("ps%d" % i, [128, 512], F32), S.buf("ps%d" % i)) for i in range(8)])

    cb = S.dbuf("consts")
    S.dma("sync", lambda e: e.dma_start(out=cbf_t[:], in_=c_bf), cb, writes=[cb])
    S.dma("sync", lambda e: e.dma_start(out=cf_t[:], in_=c_f32), cb, writes=[cb])
    lnsrc = [ln_in_g, ln_in_b]
    for l in range(DEPTH):
        lnsrc += [ln1_g[l], ln1_b[l], ln2_g[l], ln2_b[l]]
    for i, src in enumerate(lnsrc):
        S.dma("sync", (lambda i=i, src=src: lambda e: e.dma_start(
            out=lnp_t[:, i, :], in_=src.rearrange("(c p) -> p c", p=128), allow_slow_non_contiguous=True))(),
            cb, writes=[cb])
    for l in range(DEPTH):
        S.dma("sync", (lambda l=l: lambda e: e.dma_start(
            out=bg_t[:, l, :], in_=b_gate[l].rearrange("(c p) -> p c", p=128), allow_slow_non_contiguous=True))(),
            cb, writes=[cb])

    def mk_prologue(items):
        def pro(e, st):
            for (nm, idx, mx) in items:
                r = st.enter_context(e.register("r_" + nm))
                e.reg_load(r, meta[0:1, idx:idx + 1])
                S.dyn[nm] = e.snap(r, donate=True, min_val=0, max_val=mx)
        return pro

    prologues = {
        "sync": mk_prologue([("ex", 1, 3 * 4194304)]),
        "scalar": mk_prologue([("g", 0, 3)]),
        "gpsimd": mk_prologue([("ek_top", 2, (8 * 2048 + 3 * 512) * 1024 + 768), ("ek_bot", 3, (8 * 2048 + 3 * 512) * 1024 + 768),
                               ("ev_top", 4, (11 * 2048 + 3 * 512 + 256) * 1024), ("ev_bot", 5, (11 * 2048 + 3 * 512 + 256) * 1024)]),
    }
    agflat = agout.rearrange("r c -> (r c)")
    aginU = agin[0:4096, :].rearrange("r c -> (r c)").rearrange("(k t c) -> k t c", k=8, c=128)

    def khalo_src(nm):
        return agflat[bass.ds(S.dyn[nm], 2 * 2048 * 1024)].rearrange("(h r w) -> h r w", h=2, w=1024)[:, 0:512, 0:256]

    def vhalo_src(nm):
        return agflat[bass.ds(S.dyn[nm], 256 * 1024)].rearrange("(t c) -> t c", c=1024)

    def _build_body():
        phase_transpose_in(C, x, yT, ident)
        _stop(0)
        phase_ln(C, yT, lnp_t[:, 0, :], lnp_t[:, 1, :], xres, xbf, ones_s, eps_t)
        _stop(1)

        NL = DEPTH if _DEBUG_LAYERS is None else _DEBUG_LAYERS
        for l in range(NL):
            def ep_qk(oc, t0, psums, st={}):
                if oc == "alloc":
                    st["o"] = C.rot("qko", 3, [128, 512], BF16)
                    st["i"] = 0
                    return st
                ps, pb = psums[0]
                o, ob = st["o"].next()
                eng = "vector" if st["i"] % 2 == 0 else "scalar"
                st["i"] += 1
                if eng == "vector":
                    fn = (lambda o=o, ps=ps: lambda e: e.tensor_copy(o[:], ps[:]))()
                else:
                    fn = (lambda o=o, ps=ps: lambda e: e.activation(out=o[:], in_=ps[:], func=AF.Copy))()
                S.op(eng, fn, reads=[pb], writes=[ob])
                if oc < 8:
                    dst = qT[oc * 128:(oc + 1) * 128, t0:t0 + 512]
                else:
                    dst = kext[(oc - 8) * 128:(oc - 7) * 128, 256 + t0:256 + t0 + 512]
                S.dma("gpsimd", (lambda o=o, dst=dst: lambda e: e.dma_start(out=dst, in_=o[:]))(), ob, reads=[ob])

            gemm_fm(C, T=NT, TB=2048, acts=[(xbf, 16)], weights=[(w_in[l], 16, 0, 0, False)], n_oc=16, epilogue=ep_qk)

            _stop(2 + 20 * l)
            def sink_vu(blk, tok0, o, ob):
                if blk < 2:
                    dst = vext[256 + tok0:256 + tok0 + 128, blk * 512:(blk + 1) * 512]
                else:
                    k0 = (blk - 2) * 4
                    dst = aginU[k0:k0 + 4, tok0:tok0 + 128, :].rearrange("k p c -> p k c")
                    S.dma("gpsimd", (lambda o=o, dst=dst: lambda e: e.dma_start(
                        out=dst, in_=o[:].rearrange("p (k c) -> p k c", k=4)))(), ob, reads=[ob])
                    return
                S.dma("gpsimd", (lambda o=o, dst=dst: lambda e: e.dma_start(out=dst, in_=o[:]))(), ob, reads=[ob])

            gemm_tm(C, T=NT, TB=2048, act=xbf, KC=16, w_ap=w_in[l], col0=2048, n_blk=4, sink=sink_vu)

            _stop(3 + 20 * l)
            C.reset()
            bb = S.dbuf("bnd")
            kb_v = agin[4096:5120, :].rearrange("c (q t) -> c q t", q=4)
            quads = [0, 4, 56, 60]
            for qi, r0 in enumerate(quads):
                S.dma("sync", (lambda qi=qi, r0=r0: lambda e: e.dma_start(
                    out=kb_v[:, qi, :], in_=kext[:, 256 + r0 * 64:256 + r0 * 64 + 256]))(), bb, writes=[bb])
                S.dma("sync", (lambda qi=qi, r0=r0: lambda e: e.dma_start(
                    out=agin[5120 + qi * 256:5120 + (qi + 1) * 256, :], in_=vext[256 + r0 * 64:256 + r0 * 64 + 256, :]))(),
                    bb, writes=[bb])
            agb = S.buf("agout", multi=True)
            for k in range(12):
                S.op("gpsimd", (lambda k=k: lambda e: e.collective_compute(
                    "AllGather", ALU.bypass, replica_groups=[[0, 1, 2, 3], [4, 5, 6, 7]],
                    ins=[agin[512 * k:512 * (k + 1), :]], outs=[agout[2048 * k:2048 * (k + 1), :]]))(),
                    reads=[bb], writes=[agb])
            agb.multi = False
            hb = S.dbuf("halo")
            S.dma("gpsimd", lambda e: e.dma_start(
                out=kext[:, 0:256].rearrange("(h c) t -> h c t", h=2), in_=khalo_src("ek_top")),
                hb, reads=[agb], writes=[hb])
            S.dma("gpsimd", lambda e: e.dma_start(
                out=kext[:, 4352:4608].rearrange("(h c) t -> h c t", h=2), in_=khalo_src("ek_bot")),
                hb, reads=[agb], writes=[hb])
            S.dma("gpsimd", lambda e: e.dma_start(out=vext[0:256, :], in_=vhalo_src("ev_top")),
                  hb, reads=[agb], writes=[hb])
            S.dma("gpsimd", lambda e: e.dma_start(out=vext[4352:4608, :], in_=vhalo_src("ev_bot")),
                  hb, reads=[agb], writes=[hb])

            _stop(4 + 20 * l)
            C.reset()
            X = C.tile("fftX", [128, 128, 256], BF16)
            Xb = S.dbuf("fftX")
            for cc in range(2):
                S.dma("sync", (lambda cc=cc: lambda e: e.dma_start(
                    out=X[:, :, cc * 128:(cc + 1) * 128],
                    in_=agflat[cc * 2097152:][bass.ds(S.dyn["ex"], 2097152)].rearrange(
                        "(p t c) -> p t c", p=128, t=128)))(), Xb, writes=[Xb])
            P1r = C.rot("fftp1", 2, [128, 512], F32, dma=False)
            P2r = C.rot("fftp2", 2, [128, 512], F32, dma=False)
            Btr = C.rot("fftB", 2, [128, 32, 2, 256], BF16, dma=False)
            Yr = C.rot("fftY", 2, [128, 32, 2, 128], BF16)
            ag2_v = agin2.rearrange("(ri ch) (t2 t1) -> t2 ri ch t1", ri=2, t2=128)
            ev = 0
            for cbk in range(8):
                Bt, Btb = Btr.next()
                for c in range(32):
                    ch = cbk * 32 + c
                    ps, pb = C.psum.next()
                    S.op("tensor", (lambda ps=ps, ch=ch: lambda e: e.matmul(ps[:], X[:, :, ch], Rm, start=True, stop=True))(),
                         reads=[Xb, cb], writes=[pb])
                    p1, p1b = P1r.next()
                    p2, p2b = P2r.next()
                    S.op("vector", (lambda p1=p1, ps=ps: lambda e: e.tensor_tensor(out=p1[:], in0=ps[:], in1=Twr2, op=ALU.mult))(),
                         reads=[pb], writes=[p1b])
                    S.op("vector", (lambda p2=p2, ps=ps: lambda e: e.tensor_tensor(out=p2[:], in0=ps[:], in1=Twi2, op=ALU.mult))(),
                         reads=[pb], writes=[p2b])
                    S.op("gpsimd", (lambda Bt=Bt, c=c, p1=p1, p2=p2: lambda e: e.tensor_tensor(
                        out=Bt[:, c, 0, :], in0=p1[:, 0:256], in1=p2[:, 256:512], op=ALU.subtract))(),
                        reads=[p1b, p2b], writes=[Btb])
                    S.op("gpsimd", (lambda Bt=Bt, c=c, p1=p1, p2=p2: lambda e: e.tensor_tensor(
                        out=Bt[:, c, 1, :], in0=p2[:, 0:256], in1=p1[:, 256:512], op=ALU.add))(),
                        reads=[p1b, p2b], writes=[Btb])
                if cbk == 0 and "dbgB" in _DEBUG_DUMP and l == 0:
                    dbb = S.dbuf("dbgdma")
                    dbgB = dt("dbgB", [128, 32 * 2 * 256], BF16).ap()
                    dbgX = dt("dbgX", [128, 128 * 256], BF16).ap()
                    S.dma("sync", (lambda Bt=Bt: lambda e: e.dma_start(out=dbgB, in_=Bt[:].rearrange("p a b c -> p (a b c)")))(), dbb, reads=[Btb])
                    S.dma("sync", lambda e: e.dma_start(out=dbgX, in_=X[:].rearrange("p a b -> p (a b)")), dbb, reads=[Xb])
                Y, Yb = Yr.next()
                for q in range(8):
                    psr, prb = C.psum.next()
                    psi, pib = C.psum.next()
                    terms_r = [(0, 0, 0), (1, 0, 1), (2, 1, 0), (3, 1, 1)]
                    terms_i = [(0, 1, 0), (1, 1, 1), (4, 0, 0), (5, 0, 1)]
                    for n, (mi, ri, hf) in enumerate(terms_r):
                        S.op("tensor", (lambda psr=psr, Bt=Bt, q=q, mi=mi, ri=ri, hf=hf, n=n: lambda e: e.matmul(
                            psr[:].rearrange("p (a b) -> p a b", a=4), M3[:, mi, :],
                            Bt[:, 4 * q:4 * q + 4, ri, hf * 128:(hf + 1) * 128], start=(n == 0), stop=(n == 3)))(),
                            reads=[Btb], writes=[prb], inc=(n == 3))
                    for n, (mi, ri, hf) in enumerate(terms_i):
                        S.op("tensor", (lambda psi=psi, Bt=Bt, q=q, mi=mi, ri=ri, hf=hf, n=n: lambda e: e.matmul(
                            psi[:].rearrange("p (a b) -> p a b", a=4), M3[:, mi, :],
                            Bt[:, 4 * q:4 * q + 4, ri, hf * 128:(hf + 1) * 128], start=(n == 0), stop=(n == 3)))(),
                            reads=[Btb], writes=[pib], inc=(n == 3))
                    S.op("scalar", (lambda Y=Y, psr=psr, q=q: lambda e: e.activation(
                        out=Y[:, 4 * q:4 * q + 4, 0, :], in_=psr[:].rearrange("p (a b) -> p a b", a=4), func=AF.Copy))(),
                        reads=[prb], writes=[Yb])
                    S.op("scalar", (lambda Y=Y, psi=psi, q=q: lambda e: e.activation(
                        out=Y[:, 4 * q:4 * q + 4, 1, :], in_=psi[:].rearrange("p (a b) -> p a b", a=4), func=AF.Copy))(),
                        reads=[pib], writes=[Yb])
                if cbk == 0 and "dbgB" in _DEBUG_DUMP and l == 0:
                    dbgY = dt("dbgY", [128, 32 * 2 * 128], BF16).ap()
                    S.dma("sync", (lambda Y=Y: lambda e: e.dma_start(out=dbgY, in_=Y[:].rearrange("p a b c -> p (a b c)")))(), dbb, reads=[Yb])
                for ri in range(2):
                    S.dma("sync", (lambda Y=Y, cbk=cbk, ri=ri: lambda e: e.dma_start(
                        out=ag2_v[:, ri, cbk * 32:(cbk + 1) * 32, :], in_=Y[:, :, ri, :]))(), Yb, reads=[Yb])
            C.reset()
            ag2b = S.buf("agout2", multi=True)
            for k in range(16):
                S.op("gpsimd", (lambda k=k: lambda e: e.collective_compute(
                    "AllGather", ALU.bypass, replica_groups=[[0, 1, 2, 3], [4, 5, 6, 7]],
                    ins=[agin2[32 * k:32 * (k + 1), :]], outs=[agout2[128 * k:128 * (k + 1), :]]))(),
                    writes=[ag2b])

            _stop(5 + 20 * l)
            w4b = C.tile("w4b", [128, 8, D], BF16)
            w4bb = S.buf("w4b")
            w4st = C.rot("w4st", 2, [128, 8, 512], F32)
            w4v = w_four[l].rearrange("(kc p) n -> p kc n", p=128)
            for n4 in range(4):
                st, stb = w4st.next()
                S.dma("sync", (lambda st=st, n4=n4, w4v=w4v: lambda e: e.dma_start(out=st[:], in_=w4v[:, :, n4 * 512:(n4 + 1) * 512]))(),
                      stb, writes=[stb])
                S.op("vector", (lambda st=st, n4=n4, w4b=w4b: lambda e: e.tensor_copy(w4b[:, :, n4 * 512:(n4 + 1) * 512], st[:]))(),
                     reads=[stb], writes=[w4bb])
            w4o = C.rot("w4o", 3, [128, 512], BF16)
            for rr in range(4):
                for ri in range(2):
                    for hc in range(2):
                        R = rr * 4 + ri * 2 + hc
                        for n4 in range(4):
                            ps, pb = C.psum.next()
                            for kk in range(2):
                                S.op("tensor", (lambda ps=ps, ri=ri, kk=kk, hc=hc, rr=rr, n4=n4: lambda e: e.matmul(
                                    ps[:], CS[:, ri, kk, hc * 128:(hc + 1) * 128], w4b[:, rr * 2 + kk, n4 * 512:(n4 + 1) * 512],
                                    start=(kk == 0), stop=(kk == 1)))(), reads=[w4bb, cb], writes=[pb], inc=(kk == 1))
                            o, ob = w4o.next()
                            S.op("scalar", (lambda o=o, ps=ps: lambda e: e.activation(out=o[:], in_=ps[:], func=AF.Copy))(),
                                 reads=[pb], writes=[ob])
                            for pg in range(4):
                                kk2 = 8 * ri + 4 * hc + pg
                                S.dma("sync", (lambda o=o, kk2=kk2, rr=rr, pg=pg, n4=n4: lambda e: e.dma_start(
                                    out=w4p[kk2 * 128 + rr * 32:kk2 * 128 + rr * 32 + 32, n4 * 512:(n4 + 1) * 512],
                                    in_=o[32 * pg:32 * pg + 32, :]))(), ob, reads=[ob])

            _stop(6 + 20 * l)
            C.reset(skip=("gpsimd",))
            aT = C.rot("attaT", 1, [128, 2, NT], BF16)
            qtr = C.rot("attq", 2, [128, 2, NT], BF16)
            ktr = C.rot("attk", 2, [128, 2, 4608], BF16)
            ver = C.rot("attve", 2, [128, 36, 256], BF16)
            vor = C.rot("attvo", 2, [128, 35, 256], BF16)
            btr = C.rot("attbt", 2, [128, 8, 2, 256], F32)
            sbt = C.rot("attsb", 2, [128, 512], F32, dma=False)
            Er = C.rot("attE", 2, [128, 2, 4, 64], BF16, dma=False)
            rdr = C.rot("attrd", 2, [128, 128], F32, dma=False)
            ve_v = vext[0:4608, :].rearrange("(j p) c -> p j c", p=128)
            vo_v = vext[64:64 + 35 * 128, :].rearrange("(j p) c -> p j c", p=128)
            for hp in range(4):
                h0 = 2 * hp
                q_t, q_b = qtr.next()
                k_t, k_b = ktr.next()
                ve_t, ve_b = ver.next()
                vo_t, vo_b = vor.next()
                bt_t, bt_b = btr.next()
                a_t, a_b = aT.next()
                S.dma("sync", (lambda q_t=q_t, h0=h0: lambda e: e.dma_start(
                    out=q_t[:], in_=qT[h0 * 128:(h0 + 2) * 128, :].rearrange("(h p) t -> p h t", p=128)))(), q_b, writes=[q_b])
                S.dma("sync", (lambda k_t=k_t, h0=h0: lambda e: e.dma_start(
                    out=k_t[:], in_=kext[h0 * 128:(h0 + 2) * 128, :].rearrange("(h p) t -> p h t", p=128)))(), k_b, writes=[k_b])
                S.dma("sync", (lambda ve_t=ve_t, h0=h0: lambda e: e.dma_start(
                    out=ve_t[:], in_=ve_v[:, :, h0 * 128:(h0 + 2) * 128]))(), ve_b, writes=[ve_b])
                S.dma("sync", (lambda vo_t=vo_t, h0=h0: lambda e: e.dma_start(
                    out=vo_t[:], in_=vo_v[:, :, h0 * 128:(h0 + 2) * 128]))(), vo_b, writes=[vo_b])
                S.dma("sync", (lambda bt_t=bt_t, h0=h0, l=l: lambda e: e.dma_start(
                    out=bt_t[:], in_=btile[l, :, :, h0:h0 + 2, :]))(), bt_b, writes=[bt_b])
                for r in range(64):
                    var = 0
                    if r < 4:
                        var = 1 + r
                    elif r > 60:
                        var = 5 + (r - 61)
                    pss, psb = C.psum.next()
                    for hh in range(2):
                        for jj in range(4):
                            S.op("tensor", (lambda pss=pss, k_t=k_t, q_t=q_t, hh=hh, r=r, jj=jj: lambda e: e.matmul(
                                pss[:, hh * 256 + jj * 64:hh * 256 + (jj + 1) * 64],
                                k_t[:, hh, 64 * r + 128 * jj:64 * r + 128 * jj + 128],
                                q_t[:, hh, 64 * r:64 * r + 64], start=True, stop=True))(),
                                reads=[k_b, q_b], writes=[psb], inc=(hh == 1 and jj == 3))
                    sb, sbb = sbt.next()
                    S.op("vector", (lambda sb=sb, pss=pss, bt_t=bt_t, var=var: lambda e: e.scalar_tensor_tensor(
                        out=sb[:], in0=pss[:], scalar=SCALE, in1=bt_t[:, var, :, :].rearrange("p a b -> p (a b)"),
                        op0=ALU.mult, op1=ALU.add))(), reads=[psb, bt_b], writes=[sbb])
                    E, Eb = Er.next()
                    S.op("scalar", (lambda E=E, sb=sb: lambda e: e.activation(
                        out=E[:].rearrange("p a b c -> p (a b c)"), in_=sb[:], func=AF.Exp))(),
                        reads=[sbb], writes=[Eb])
                    pso, pob = C.psum.next()
                    psd, pdb = C.psum.next()
                    for hh in range(2):
                        for jj in range(4):
                            if r % 2 == 0:
                                vch = ve_t[:, r // 2 + jj, hh * 128:(hh + 1) * 128]
                                vb = ve_b
                            else:
                                vch = vo_t[:, (r - 1) // 2 + jj, hh * 128:(hh + 1) * 128]
                                vb = vo_b
                            S.op("tensor", (lambda pso=pso, vch=vch, E=E, jj=jj, hh=hh: lambda e: e.matmul(
                                pso[:, hh * 64:(hh + 1) * 64], vch, E[:, hh, jj, :], start=(jj == 0), stop=(jj == 3)))(),
                                reads=[vb, Eb], writes=[pob], inc=(hh == 1 and jj == 3))
                    for jj in range(4):
                        S.op("tensor", (lambda psd=psd, E=E, jj=jj: lambda e: e.matmul(
                            psd[:, 0:128].rearrange("p (a b) -> p a b", a=2), ones1, E[:, :, jj, :],
                            start=(jj == 0), stop=(jj == 3)))(),
                            reads=[Eb, cb], writes=[pdb], inc=(jj == 3))
                    rd, rdb = rdr.next()
                    S.op("vector", (lambda rd=rd, psd=psd: lambda e: e.reciprocal(out=rd[:], in_=psd[:, 0:128]))(),
                         reads=[pdb], writes=[rdb])
                    S.op("vector", (lambda a_t=a_t, pso=pso, rd=rd, r=r: lambda e: e.tensor_tensor(
                        out=a_t[:, :, 64 * r:64 * r + 64], in0=pso[:, 0:128].rearrange("p (a b) -> p a b", a=2),
                        in1=rd[:].rearrange("p (a b) -> p a b", a=2), op=ALU.mult))(),
                        reads=[pob, rdb], writes=[a_b])
                S.dma("sync", (lambda a_t=a_t, h0=h0: lambda e: e.dma_start(
                    out=attT[h0 * 128:(h0 + 2) * 128, :].rearrange("(h p) t -> p h t", p=128), in_=a_t[:]))(), a_b, reads=[a_b])

            _stop(7 + 20 * l)
            def ep_mix(oc, t0, psums, st={}, l=l):
                if oc == "alloc":
                    st["ga"] = C.rot("mxga", 2, [128, 512], F32, dma=False)
                    st["gf"] = C.rot("mxgf", 2, [128, 512], F32, dma=False)
                    st["m1"] = C.rot("mxm1", 2, [128, 512], F32, dma=False)
                    st["m"] = C.rot("mxm", 3, [128, 512], BF16)
                    return st
                (pa, pab), (pf, pfb), (pga, pgab), (pgf, pgfb) = psums
                ga, gab = st["ga"].next()
                gf, gfb = st["gf"].next()
                m1, m1b = st["m1"].next()
                m, mb = st["m"].next()
                S.op("scalar", (lambda ga=ga, pga=pga, oc=oc: lambda e: e.activation(
                    out=ga[:], in_=pga[:], func=AF.Sigmoid, bias=bg_t[:, l, oc:oc + 1], scale=1.0))(), reads=[pgab, cb], writes=[gab])
                S.op("scalar", (lambda gf=gf, pgf=pgf, oc=oc: lambda e: e.activation(
                    out=gf[:], in_=pgf[:], func=AF.Sigmoid, bias=bg_t[:, l, 16 + oc:17 + oc], scale=1.0))(), reads=[pgfb, cb], writes=[gfb])
                S.op("vector", (lambda ga=ga, pa=pa: lambda e: e.tensor_tensor(out=ga[:], in0=ga[:], in1=pa[:], op=ALU.mult))(),
                     reads=[gab, pab], writes=[gab])
                S.op("vector", (lambda gf=gf, pf=pf, m1=m1: lambda e: e.tensor_tensor(out=m1[:], in0=gf[:], in1=pf[:], op=ALU.mult))(),
                     reads=[gfb, pfb], writes=[m1b])
                S.op("vector", (lambda m=m, ga=ga, m1=m1: lambda e: e.tensor_tensor(out=m[:], in0=ga[:], in1=m1[:], op=ALU.add))(),
                     reads=[gab, m1b], writes=[mb])
                S.dma("gpsimd", (lambda m=m, oc=oc, t0=t0: lambda e: e.dma_start(
                    out=mT[oc * 128:(oc + 1) * 128, t0:t0 + 512], in_=m[:]))(), mb, reads=[mb])

            def vc_src(tb, e=None):
                return agout2.rearrange("(kc p) (g t) -> p kc g t", p=128, g=4)[
                    :, :, bass.ds(S.dyn["g"], 1), tb * 1024:(tb + 1) * 1024].rearrange("p kc g t -> p kc (g t)")

            gemm_fm(C, T=NT, TB=1024,
                    acts=[(attT, 8), (vc_src, 16), (xbf, 16)],
                    weights=[(w_att[l], 8, 0, 0, False), (w4p, 16, 1, 0, True),
                             (w_gate[l], 16, 2, 0, False), (w_gate[l], 16, 2, 2048, False)],
                    n_oc=16, epilogue=ep_mix)

            _stop(8 + 20 * l)
            def make_ep_res(srcname):
                def ep_res(oc, t0, psums, st={}):
                    if oc == "alloc":
                        st["x"] = C.rot(srcname + "x", 3, [128, 512], F32)
                        st["y"] = C.rot(srcname + "y", 3, [128, 512], F32)
                        return st
                    ps, pb = psums[0]
                    xt, xb_ = st["x"].next()
                    yt, yb_ = st["y"].next()
                    S.dma("sync", (lambda xt=xt, oc=oc, t0=t0: lambda e: e.dma_start(
                        out=xt[:], in_=xres[oc * 128:(oc + 1) * 128, t0:t0 + 512]))(), xb_, writes=[xb_])
                    S.op("vector", (lambda yt=yt, xt=xt, ps=ps: lambda e: e.scalar_tensor_tensor(
                        out=yt[:], in0=xt[:], scalar=ALPHA, in1=ps[:], op0=ALU.mult, op1=ALU.add))(),
                        reads=[xb_, pb], writes=[yb_])
                    S.dma("gpsimd", (lambda yt=yt, oc=oc, t0=t0: lambda e: e.dma_start(
                        out=yT[oc * 128:(oc + 1) * 128, t0:t0 + 512], in_=yt[:]))(), yb_, reads=[yb_])
                return ep_res

            gemm_fm(C, T=NT, TB=2048, acts=[(mT, 16)], weights=[(w_out[l], 16, 0, 0, False)], n_oc=16,
                    epilogue=make_ep_res("r1"))
            phase_ln(C, yT, lnp_t[:, 2 + 4 * l, :], lnp_t[:, 3 + 4 * l, :], xres, xbf, ones_s, eps_t)

            _stop(9 + 20 * l)
            def ep_ffn(oc, t0, psums, st={}):
                if oc == "alloc":
                    st["s"] = C.rot("ffs", 2, [128, 512], F32, dma=False)
                    st["h"] = C.rot("ffh", 3, [128, 512], BF16)
                    return st
                (pg, pgb), (pu, pub) = psums
                s, sb_ = st["s"].next()
                h, hb_ = st["h"].next()
                S.op("scalar", (lambda s=s, pg=pg: lambda e: e.activation(out=s[:], in_=pg[:], func=AF.Silu))(),
                     reads=[pgb], writes=[sb_])
                S.op("vector", (lambda h=h, s=s, pu=pu: lambda e: e.tensor_tensor(out=h[:], in0=s[:], in1=pu[:], op=ALU.mult))(),
                     reads=[sb_, pub], writes=[hb_])
                S.dma("gpsimd", (lambda h=h, oc=oc, t0=t0: lambda e: e.dma_start(
                    out=hT[oc * 128:(oc + 1) * 128, t0:t0 + 512], in_=h[:]))(), hb_, reads=[hb_])

            gemm_fm(C, T=NT, TB=2048, acts=[(xbf, 16)],
                    weights=[(w_fg[l], 16, 0, 0, False), (w_fu[l], 16, 0, 0, False)], n_oc=44, epilogue=ep_ffn)

            _stop(10 + 20 * l)
            gemm_fm(C, T=NT, TB=1024, acts=[(hT, 44)], weights=[(w_fd[l], 44, 0, 0, False)], n_oc=16,
                    epilogue=make_ep_res("r2"))
            _stop(11 + 20 * l)
            if l < NL - 1:
                phase_ln(C, yT, lnp_t[:, 4 + 4 * l, :], lnp_t[:, 5 + 4 * l, :], xres, xbf, ones_s, eps_t)
            else:
                phase_ln(C, yT, lnp_t[:, 4 + 4 * l, :], lnp_t[:, 5 + 4 * l, :], xres, xbf, ones_s, eps_t,
                         final_out=yout, ident=ident)


    try:
        _build_body()
    except _StopBuild:
        pass

    if dump_list:
        C.reset()
        db_ = S.dbuf("dumpcopy")
        for (name, ap, rows, cols, dty) in dump_list:
            dst = nc.dram_tensor(name, [rows, cols], dty, kind="ExternalOutput").ap()
            nchunk = 8
            rr_ = rows // nchunk
            for i in range(nchunk):
                S.dma("sync", (lambda dst=dst, ap=ap, i=i, rr_=rr_: lambda e: e.dma_start(
                    out=dst[i * rr_:(i + 1) * rr_, :], in_=ap[i * rr_:(i + 1) * rr_, :]))(), db_, writes=[db_])
    C.reset()
    with nc.Block() as block:
        S.run(block, prologues=prologues)
    return nc


def _consts_for_core(c):
    g = c % 4
    groupB = c >= 4
    cbf = np.zeros((128, 2560), np.float32)
    cbf[:, 0:128] = 1.0 / 2048.0
    cbf[:, 128:256] = 1.0
    p = np.arange(128)
    k = np.arange(128)
    R = np.zeros((128, 2, 2, 128), np.float64)
    if not groupB:
        ang = 2 * np.pi * np.outer(p, k) / 128.0
        for hf in range(2):
            R[:, 0, hf, :] = np.cos(ang)
            R[:, 1, hf, :] = -np.sin(ang)
    else:
        for hf in range(2):
            rows = np.arange(64) + 64 * hf
            ang = 2 * np.pi * np.outer(np.arange(64), k) / 64.0
            R[rows, 0, hf, :] = np.cos(ang)
            R[rows, 1, hf, :] = -np.sin(ang)
    cbf[:, 256:768] = R.reshape(128, 512)
    M3 = np.zeros((128, 6, 128), np.float64)
    b = np.arange(64)
    for hf in range(2):
        if not groupB:
            ang = 2 * np.pi * np.outer(p, 64 * hf + b) / 128.0
        else:
            ang = 2 * np.pi * np.outer(p, b) / 64.0
        M3[:, 0 + hf, 64 * hf:64 * hf + 64] = np.cos(ang)
        M3[:, 2 + hf, 64 * hf:64 * hf + 64] = np.sin(ang)
        M3[:, 4 + hf, 64 * hf:64 * hf + 64] = -np.sin(ang)
    cbf[:, 768:1536] = M3.reshape(128, 768)
    T = 8192.0 if groupB else 16384.0
    norm = 1.0 / math.sqrt(T * 256.0)
    CSm = np.zeros((128, 2, 2, 256), np.float64)
    cc = np.arange(256)
    for kk in range(2):
        ang = 2 * np.pi * np.outer(kk * 128 + p, cc) / 256.0
        CSm[:, 0, kk, :] = np.cos(ang) * norm
        CSm[:, 1, kk, :] = np.sin(ang) * norm
    cbf[:, 1536:2560] = CSm.reshape(128, 1024)

    cf = np.zeros((128, 1160), np.float32)
    cf[:, 0:128] = np.eye(128)
    if not groupB:
        ang = 2 * np.pi * np.outer(p, k) / 16384.0
    else:
        ang = 2 * np.pi * np.outer(p, k) / 8192.0
    twr = np.cos(ang)
    twi = -np.sin(ang)
    twr2 = np.concatenate([twr, twr, twr, twr], axis=1)
    twi2 = np.concatenate([twi, twi, twi, twi], axis=1)
    cf[:, 128:640] = twr2
    cf[:, 640:1152] = twi2
    cf[:, 1152] = EPS

    top_seq = (c in (0, 4, 6))
    bot_seq = (c in (3, 5, 7))
    meta = np.zeros((1, 16), np.int32)
    meta[0, 0] = g
    meta[0, 1] = g * 4194304
    rs, q = (g, 1) if top_seq else (g - 1, 3)
    meta[0, 2] = (8 * 2048 + rs * 512) * 1024 + q * 256
    meta[0, 4] = ((10 + q // 2) * 2048 + rs * 512 + (q % 2) * 256) * 1024
    rs, q = (g, 2) if bot_seq else (g + 1, 0)
    meta[0, 3] = (8 * 2048 + rs * 512) * 1024 + q * 256
    meta[0, 5] = ((10 + q // 2) * 2048 + rs * 512 + (q % 2) * 256) * 1024
    return cbf.astype(NPBF), cf, meta, top_seq, bot_seq


def _bias_tiles(rpb, top_seq, bot_seq):
    L = rpb.shape[0]
    out = np.full((L, 128, 8, NH, 4, 64), NEG, np.float32)
    pp = np.arange(128)
    kc = pp % 64
    cq = np.arange(64)
    cs = np.clip(cq - 8, 0, 48)
    valid = (kc[:, None] >= cs[None, :]) & (kc[:, None] < cs[None, :] + 16)
    cidx = np.clip(kc[:, None] - cq[None, :] + 15, 0, 30)
    for var in range(8):
        for jj in range(4):
            i = 2 * jj + pp // 64
            ridx = i + 3
            if var >= 1 and var <= 4 and top_seq:
                r = var - 1
                ridx = np.where(r + i < 4, i + 11, i + 3)
            if var >= 5 and bot_seq:
                r = 61 + (var - 5)
                ridx = np.where(r + i >= 68, i - 5, i + 3)
            ridx = np.clip(ridx, 0, 14)
            vals = rpb[:, :, ridx[:, None], cidx]
            vals = np.where(valid[None, None], vals, NEG)
            out[:, :, var, :, jj, :] = np.transpose(vals, (0, 2, 1, 3))
    return out.reshape(L, 128, 8, NH, 256)


_NC_CACHE = {}


def kernel(x_prompt, x_sample, ln_in_g, ln_in_b, w_in, rpb, w_att, w_four, w_gate, b_gate, w_out,
           ln1_g, ln1_b, w_ffn_gate, w_ffn_up, w_ffn_down, ln2_g, ln2_b):
    f32 = lambda a: np.ascontiguousarray(np.asarray(a, dtype=np.float32))
    xall = np.concatenate([f32(x_prompt).reshape(-1, D), f32(x_sample).reshape(-1, D)], axis=0)
    if "nc" not in _NC_CACHE:
        _NC_CACHE["nc"] = build_program()
    nc = _NC_CACHE["nc"]
    shared = {
        "ln_in_g": f32(ln_in_g), "ln_in_b": f32(ln_in_b), "w_in": f32(w_in), "w_att": f32(w_att),
        "w_four": f32(w_four), "w_gate": f32(w_gate), "b_gate": f32(b_gate), "w_out": f32(w_out),
        "ln1_g": f32(ln1_g), "ln1_b": f32(ln1_b), "w_ffn_gate": f32(w_ffn_gate), "w_ffn_up": f32(w_ffn_up),
        "w_ffn_down": f32(w_ffn_down), "ln2_g": f32(ln2_g), "ln2_b": f32(ln2_b),
    }
    rpb = f32(rpb)
    in_maps = []
    for c in range(NCORES):
        cbf, cf, meta, top_seq, bot_seq = _consts_for_core(c)
        m = dict(shared)
        m["x"] = xall[c * NT:(c + 1) * NT]
        m["c_bf"] = cbf
        m["c_f32"] = cf
        m["meta"] = meta
        m["btile"] = _bias_tiles(rpb, top_seq, bot_seq)
        in_maps.append(m)
    res = run_bass_kernel_spmd(nc, in_maps, core_ids=list(range(NCORES)))
    _NC_CACHE["last"] = res
    yall = np.concatenate([res.results[c]["y"] for c in range(NCORES)], axis=0)
    y_prompt = yall[:16384].reshape(1, 16384, D).astype(np.float32)
    y_sample = yall[16384:].reshape(2, 8192, D).astype(np.float32)
    return (y_prompt, y_sample)
```

```python
import math
from contextlib import ExitStack

import numpy as np
import ml_dtypes

import concourse.bass as bass
import concourse.mybir as mybir
from concourse.bass_utils import run_bass_kernel_spmd

F32 = mybir.dt.float32
BF16 = mybir.dt.bfloat16
I32 = mybir.dt.int32
AF = mybir.ActivationFunctionType
ALU = mybir.AluOpType
NPBF = ml_dtypes.bfloat16

NCORES = 8
NT = 4096
D = 2048
DFF = 5632
DEPTH = 2
NH = 8
ALPHA = (2.0 * DEPTH) ** 0.25
EPS = 1e-5
SCALE = 128 ** -0.5
NEG = -30000.0
GT = 16384
AGR = 6144

ENGS = ["tensor", "vector", "scalar", "gpsimd", "sync"]
import os as _os
_DEBUG_STOP = int(_os.environ["KSTOP"]) if "KSTOP" in _os.environ else None
_DEBUG_DUMP = ()
_DEBUG_LAYERS = None


class _StopBuild(Exception):
    pass


def _stop(n):
    if _DEBUG_STOP is not None and n >= _DEBUG_STOP:
        raise _StopBuild()


class Buf:
    __slots__ = ("name", "w", "r", "slot", "multi")

    def __init__(self, name, slot=None, multi=False):
        self.name = name
        self.w = {}
        self.r = {}
        self.slot = slot
        self.multi = multi


class Sched:
    def __init__(self, nc, n_dslots=56):
        self.nc = nc
        self.sems = {}
        self.count = {}
        self.ops = {e: [] for e in ENGS}
        self.seen = {e: {} for e in ENGS}
        for e in ENGS:
            self.sems[e] = nc.alloc_semaphore(name="m_" + e)
            self.count[e] = 0
        self.slots = []
        for i in range(n_dslots):
            key = "d%d" % i
            self.sems[key] = nc.alloc_semaphore(name=key)
            self.count[key] = 0
            self.slots.append(key)
        self.next_slot = 0
        self.dyn = {}

    def reset_slots(self):
        self.phase_first = self.next_slot

    def dbuf(self, name, multi=False):
        assert self.next_slot - getattr(self, "phase_first", 0) < len(self.slots), "out of DMA semaphore slots"
        key = self.slots[self.next_slot % len(self.slots)]
        self.next_slot += 1
        return Buf(name, slot=key, multi=multi)

    def buf(self, name, multi=False):
        return Buf(name, multi=multi)

    def _waits(self, eng, reads, writes):
        need = {}

        def add(d):
            for k, v in d.items():
                if need.get(k, 0) < v:
                    need[k] = v
        for b in reads:
            add(b.w)
        for b in writes:
            if not b.multi:
                add(b.w)
            add(b.r)
        out = []
        seen = self.seen[eng]
        for k, v in need.items():
            if seen.get(k, 0) >= v:
                continue
            seen[k] = v
            out.append((k, v))
        return out

    def _record(self, key, val, reads, writes):
        for b in reads:
            if b.r.get(key, 0) < val:
                b.r[key] = val
        for b in writes:
            if b.multi:
                if b.w.get(key, 0) < val:
                    b.w[key] = val
            else:
                b.w = {key: val}
                b.r = {}

    def op(self, eng, fn, reads=(), writes=(), inc=True):
        waits = self._waits(eng, reads, writes)
        if eng == "tensor":
            waits = [(k, v) for (k, v) in waits if k != "tensor"]
        val = self.count[eng] + 1
        if inc:
            self.count[eng] = val
        self._record(eng, val, reads, writes)
        self.ops[eng].append((waits, fn, (eng, 1) if inc else None))

    def dma(self, eng, fn, dbuf, reads=(), writes=()):
        waits = self._waits(eng, reads, writes)
        key = dbuf.slot
        self.count[key] += 16
        self._record(key, self.count[key], reads, writes)
        self.ops[eng].append((waits, fn, (key, 16)))

    def full_barrier(self, skip=()):
        tgt = {k: v for k, v in self.count.items() if v > 0 and k not in skip}
        for e in ENGS:
            waits = []
            for k, v in tgt.items():
                if k == e and e == "tensor":
                    continue
                if self.seen[e].get(k, 0) >= v:
                    continue
                self.seen[e][k] = v
                waits.append((k, v))
            if waits:
                self.ops[e].append((waits, None, None))

    def run(self, block, prologues=None):
        prologues = prologues or {}
        sems = self.sems

        def replay(eng, e):
            for waits, fn, inc in self.ops[eng]:
                for k, v in waits:
                    e.wait_ge(sems[k], v)
                if fn is None:
                    continue
                ins = fn(e)
                if inc is not None:
                    ins.then_inc(sems[inc[0]], inc[1])

        def mk(eng):
            def body(e):
                if eng in prologues:
                    with ExitStack() as st:
                        prologues[eng](e, st)
                        replay(eng, e)
                else:
                    replay(eng, e)
            return body
        block.tensor(mk("tensor"))
        block.vector(mk("vector"))
        block.scalar(mk("scalar"))
        block.gpsimd(mk("gpsimd"))
        block.sync(mk("sync"))


class Rot:
    def __init__(self, items):
        self.items = items
        self.i = 0

    def next(self):
        it = self.items[self.i % len(self.items)]
        self.i += 1
        return it


class Ctx:
    def __init__(self, nc, S):
        self.nc = nc
        self.S = S
        self.off = 17408
        self.uid = 0
        self.base = 17408

    def reset(self, skip=()):
        self.S.full_barrier(skip)
        self.S.reset_slots()
        self.off = self.base

    def tile(self, name, shape, dtype):
        esz = 4 if dtype in (F32, I32) else 2
        n = 1
        for s in shape[1:]:
            n *= s
        nbytes = n * esz
        self.off = (self.off + 63) // 64 * 64
        assert self.off + nbytes <= 224 * 1024, ("SBUF overflow", name, self.off, nbytes)
        self.uid += 1
        t = self.nc.alloc_sbuf_tensor_at("%s_%d" % (name, self.uid), list(shape), dtype, offset=self.off)
        self.off += nbytes
        return t

    def rot(self, name, n, shape, dtype, dma=True):
        items = []
        for i in range(n):
            t = self.tile("%s%d" % (name, i), shape, dtype)
            b = self.S.dbuf("%s%d" % (name, i)) if dma else self.S.buf("%s%d" % (name, i))
            items.append((t, b))
        return Rot(items)


def phase_transpose_in(C, x, xT, ident):
    S = C.S
    C.reset()
    xin = C.rot("xin", 2, [128, D], F32)
    stg = C.rot("tstg", 2, [128, 16, 128], F32)
    xT_v = xT.rearrange("(fc p) t -> p fc t", p=128)
    ev = 0
    for tc in range(NT // 128):
        xt, xb = xin.next()
        S.dma("sync", (lambda xt=xt, tc=tc: lambda e: e.dma_start(out=xt[:], in_=x[tc * 128:(tc + 1) * 128, :]))(),
              xb, writes=[xb])
        st, sb = stg.next()
        for grp in range(4):
            ps, pb = C.psum.next()
            for j in range(4):
                fc = grp * 4 + j
                S.op("tensor", (lambda ps=ps, xt=xt, fc=fc, j=j: lambda e: e.transpose(
                    ps[:, j * 128:(j + 1) * 128], xt[:, fc * 128:(fc + 1) * 128], ident[:]))(),
                    reads=[xb], writes=[pb], inc=(j == 3))
            eng = "vector" if ev % 2 == 0 else "scalar"
            ev += 1
            if eng == "vector":
                fn = (lambda st=st, ps=ps, grp=grp: lambda e: e.tensor_copy(
                    st[:, grp * 4:(grp + 1) * 4, :], ps[:].rearrange("p (a b) -> p a b", a=4)))()
            else:
                fn = (lambda st=st, ps=ps, grp=grp: lambda e: e.activation(
                    out=st[:, grp * 4:(grp + 1) * 4, :], in_=ps[:].rearrange("p (a b) -> p a b", a=4),
                    func=AF.Copy))()
            S.op(eng, fn, reads=[pb], writes=[sb])
        S.dma("gpsimd", (lambda st=st, tc=tc: lambda e: e.dma_start(
            out=xT_v[:, :, tc * 128:(tc + 1) * 128], in_=st[:]))(), sb, reads=[sb])


def phase_ln(C, yT, gcol, bcol, xres, xbf, ones_s, eps_t, final_out=None, ident=None):
    S = C.S
    C.reset()
    yv = yT.rearrange("(kc p) t -> p kc t", p=128)
    ytl = C.rot("lny", 2, [128, 16, 512], F32)
    ybf_t = C.tile("lnybf", [128, 16, 512], BF16)
    ybf_b = S.buf("lnybf")
    ysq_t = C.tile("lnysq", [128, 16, 512], BF16)
    ysq_b = S.buf("lnysq")
    mean_t = C.tile("lnmean", [128, 512], F32)
    mean_b = S.buf("lnmean")
    var_t = C.tile("lnvar", [128, 512], F32)
    var_b = S.buf("lnvar")
    tmp_t = C.tile("lntmp", [128, 512], F32)
    tmp_b = S.buf("lntmp")
    cen = C.rot("lncen", 2, [128, 512], F32, dma=False)
    o32 = C.rot("lno32", 1, [128, 16, 512], F32)
    if final_out is None:
        obf = C.rot("lnobf", 1, [128, 16, 512], BF16)
        xres_v = xres.rearrange("(kc p) t -> p kc t", p=128)
        xbf_v = xbf.rearrange("(kc p) t -> p kc t", p=128)
    else:
        otok = C.rot("lnotok", 2, [128, D], F32)
    for tt in range(NT // 512):
        y, yb = ytl.next()
        S.dma("sync", (lambda y=y, tt=tt: lambda e: e.dma_start(out=y[:], in_=yv[:, :, tt * 512:(tt + 1) * 512]))(),
              yb, writes=[yb])
        S.op("scalar", (lambda y=y: lambda e: e.activation(out=ybf_t[:], in_=y[:], func=AF.Copy))(),
             reads=[yb], writes=[ybf_b])
        S.op("scalar", (lambda y=y: lambda e: e.activation(out=ysq_t[:], in_=y[:], func=AF.Square))(),
             reads=[yb], writes=[ysq_b])
        psm, pmb = C.psum.next()
        psq, pqb = C.psum.next()
        for kc in range(16):
            S.op("tensor", (lambda psm=psm, kc=kc: lambda e: e.matmul(
                psm[:], ones_s[:], ybf_t[:, kc, :], start=(kc == 0), stop=(kc == 15)))(),
                reads=[ybf_b], writes=[pmb], inc=(kc == 15))
        for kc in range(16):
            S.op("tensor", (lambda psq=psq, kc=kc: lambda e: e.matmul(
                psq[:], ones_s[:], ysq_t[:, kc, :], start=(kc == 0), stop=(kc == 15)))(),
                reads=[ysq_b], writes=[pqb], inc=(kc == 15))
        S.op("scalar", (lambda psm=psm: lambda e: e.activation(out=mean_t[:], in_=psm[:], func=AF.Copy))(),
             reads=[pmb], writes=[mean_b])
        S.op("vector", lambda e: e.tensor_tensor(out=tmp_t[:], in0=mean_t[:], in1=mean_t[:], op=ALU.mult),
             reads=[mean_b], writes=[tmp_b])
        S.op("vector", (lambda psq=psq: lambda e: e.tensor_tensor(
            out=var_t[:], in0=psq[:], in1=tmp_t[:], op=ALU.subtract))(),
            reads=[pqb, tmp_b], writes=[var_b])
        S.op("scalar", lambda e: e.activation(out=var_t[:], in_=var_t[:], func=AF.Sqrt, bias=eps_t[:, 0:1], scale=1.0),
             reads=[var_b], writes=[var_b])
        S.op("vector", lambda e: e.reciprocal(out=var_t[:], in_=var_t[:]), reads=[var_b], writes=[var_b])
        o, ob = o32.next()
        for kc in range(16):
            c, cb = cen.next()
            S.op("vector", (lambda c=c, y=y, kc=kc: lambda e: e.tensor_tensor(
                out=c[:], in0=y[:, kc, :], in1=mean_t[:], op=ALU.subtract))(),
                reads=[yb, mean_b], writes=[cb])
            S.op("vector", (lambda c=c: lambda e: e.tensor_tensor(
                out=c[:], in0=c[:], in1=var_t[:], op=ALU.mult))(),
                reads=[cb, var_b], writes=[cb])
            S.op("scalar", (lambda c=c, o=o, kc=kc: lambda e: e.activation(
                out=o[:, kc, :], in_=c[:], func=AF.Identity, bias=bcol[:, kc:kc + 1], scale=gcol[:, kc:kc + 1]))(),
                reads=[cb], writes=[ob])
        if final_out is None:
            ob16, ob16b = obf.next()
            S.op("gpsimd", (lambda o=o, ob16=ob16: lambda e: e.tensor_copy(ob16[:], o[:]))(),
                 reads=[ob], writes=[ob16b])
            S.dma("sync", (lambda o=o, tt=tt: lambda e: e.dma_start(
                out=xres_v[:, :, tt * 512:(tt + 1) * 512], in_=o[:]))(), ob, reads=[ob])
            S.dma("sync", (lambda ob16=ob16, tt=tt: lambda e: e.dma_start(
                out=xbf_v[:, :, tt * 512:(tt + 1) * 512], in_=ob16[:]))(), ob16b, reads=[ob16b])
        else:
            ev = 0
            for ts in range(4):
                ot, otb = otok.next()
                for grp in range(4):
                    ps, pb = C.psum.next()
                    for j in range(4):
                        kc = grp * 4 + j
                        S.op("tensor", (lambda ps=ps, o=o, kc=kc, j=j, ts=ts: lambda e: e.transpose(
                            ps[:, j * 128:(j + 1) * 128], o[:, kc, ts * 128:(ts + 1) * 128], ident[:]))(),
                            reads=[ob], writes=[pb], inc=(j == 3))
                    eng = "vector" if ev % 2 == 0 else "scalar"
                    ev += 1
                    if eng == "vector":
                        fn = (lambda ot=ot, ps=ps, grp=grp: lambda e: e.tensor_copy(
                            ot[:, grp * 512:(grp + 1) * 512], ps[:]))()
                    else:
                        fn = (lambda ot=ot, ps=ps, grp=grp: lambda e: e.activation(
                            out=ot[:, grp * 512:(grp + 1) * 512], in_=ps[:], func=AF.Copy))()
                    S.op(eng, fn, reads=[pb], writes=[otb])
                r0 = tt * 512 + ts * 128
                S.dma("sync", (lambda ot=ot, r0=r0: lambda e: e.dma_start(
                    out=final_out[r0:r0 + 128, :], in_=ot[:]))(), otb, reads=[otb])


class WLoader:
    qi = 0

    def __init__(self, C, name, w_ap, KC, stage_rot, n_slab=2, is_bf16=False):
        self.C = C
        self.w = w_ap
        self.KC = KC
        self.stage = stage_rot
        self.is_bf16 = is_bf16
        self.slabs = C.rot(name + "sl", n_slab, [128, KC, 256], BF16, dma=is_bf16)
        self.wv = w_ap.rearrange("(kc p) n -> p kc n", p=128)
        self.cast_i = 0
        if KC <= 16:
            self.pieces = [(0, KC)]
        else:
            assert KC % 11 == 0
            self.pieces = [(i, 11) for i in range(0, KC, 11)]
        self.stages = []
        if not is_bf16:
            for i, (k0, nk) in enumerate(self.pieces):
                t = C.tile("%sst%d" % (name, i), [128, nk, 256], F32)
                self.stages.append((t, C.S.dbuf("%sst%d" % (name, i))))

    def load(self, c0):
        S = self.C.S
        sl, slb = self.slabs.next()
        if self.is_bf16:
            S.dma("sync", (lambda sl=sl, c0=c0: lambda e: e.dma_start(out=sl[:], in_=self.wv[:, :, c0:c0 + 256]))(),
                  slb, writes=[slb])
            return (sl, slb, [])
        parts = []
        for i, (k0, nk) in enumerate(self.pieces):
            st, stb = self.stages[i]
            WLoader.qi += 1
            S.dma("sync" if WLoader.qi % 2 == 0 else "scalar", (lambda st=st, k0=k0, nk=nk, c0=c0: lambda e: e.dma_start(
                out=st[:, 0:nk, :], in_=self.wv[:, k0:k0 + nk, c0:c0 + 256]))(), stb, writes=[stb])
            parts.append((st, stb, k0, nk))
        return (sl, slb, parts)

    def cast(self, h):
        S = self.C.S
        sl, slb, parts = h
        for (st, stb, k0, nk) in parts:
            eng = "vector" if self.cast_i % 2 == 0 else "scalar"
            self.cast_i += 1
            if eng == "vector":
                fn = (lambda sl=sl, st=st, k0=k0, nk=nk: lambda e: e.tensor_copy(sl[:, k0:k0 + nk, :], st[:, 0:nk, :]))()
            else:
                fn = (lambda sl=sl, st=st, k0=k0, nk=nk: lambda e: e.activation(
                    out=sl[:, k0:k0 + nk, :], in_=st[:, 0:nk, :], func=AF.Copy))()
            S.op(eng, fn, reads=[stb], writes=[slb])
        return (sl, slb)


def gemm_fm(C, *, T, TB, acts, weights, n_oc, epilogue, prologue_tb=None, reset_skip=()):
    S = C.S
    C.reset(skip=reset_skip)
    atiles = []
    for i, (a, KC) in enumerate(acts):
        t = C.tile("act%d" % i, [128, KC, TB], BF16)
        b = S.dbuf("act%d" % i)
        atiles.append((t, b, a, KC))
    stage = None
    loaders = []
    for j, (w_ap, KC, ai, col0, isb) in enumerate(weights):
        loaders.append(WLoader(C, "w%d" % j, w_ap, KC, stage, is_bf16=isb))
    ep_state = epilogue("alloc", None, None)
    n_oc2 = n_oc // 2
    for tb in range(T // TB):
        for (t, b, a, KC) in atiles:
            if callable(a):
                fn = (lambda t=t, a=a, tb=tb: lambda e: e.dma_start(out=t[:], in_=a(tb, e)))()
            else:
                src = a.rearrange("(kc p) t -> p kc t", p=128)[:, :, tb * TB:(tb + 1) * TB]
                fn = (lambda t=t, src=src: lambda e: e.dma_start(out=t[:], in_=src))()
            S.dma("scalar" if callable(a) else "sync", fn, b, writes=[b])
        if prologue_tb is not None:
            prologue_tb(tb)
        handles = [ld.load(weights[j][3]) for j, ld in enumerate(loaders)]
        slabs = [ld.cast(h) for ld, h in zip(loaders, handles)]
        for oc2 in range(n_oc2):
            nxt = None
            if oc2 + 1 < n_oc2:
                nxt = [ld.load(weights[j][3] + (oc2 + 1) * 256) for j, ld in enumerate(loaders)]
            for half in range(2):
                oc = oc2 * 2 + half
                for st in range(TB // 512):
                    psums = []
                    for j, (w_ap, KC, ai, col0, isb) in enumerate(weights):
                        ps, pb = C.psum.next()
                        at, ab = atiles[ai][0], atiles[ai][1]
                        sl, slb = slabs[j]
                        for kc in range(KC):
                            S.op("tensor", (lambda ps=ps, sl=sl, at=at, kc=kc, half=half, st=st, KC=KC: lambda e: e.matmul(
                                ps[:], sl[:, kc, half * 128:(half + 1) * 128], at[:, kc, st * 512:(st + 1) * 512],
                                start=(kc == 0), stop=(kc == KC - 1)))(),
                                reads=[slb, ab], writes=[pb], inc=(kc == KC - 1))
                        psums.append((ps, pb))
                    epilogue(oc, tb * TB + st * 512, psums)
                    if half == 0 and st == 0 and nxt is not None:
                        slabs_next = [ld.cast(h) for ld, h in zip(loaders, nxt)]
            if nxt is not None:
                slabs = slabs_next


def gemm_tm(C, *, T, TB, act, KC, w_ap, col0, n_blk, sink):
    S = C.S
    C.reset()
    at = C.tile("tmact", [128, KC, TB], BF16)
    ab = S.dbuf("tmact")
    stage = C.rot("tmst", 3, [128, 8, 512], F32)
    slabs = C.rot("tmsl", 2, [128, KC, 512], BF16, dma=False)
    outs = C.rot("tmout", 3, [128, 512], BF16)
    wv = w_ap.rearrange("(kc p) n -> p kc n", p=128)
    av = act.rearrange("(kc p) t -> p kc t", p=128)
    ci = 0
    ev = 0
    for tb in range(T // TB):
        S.dma("sync", (lambda tb=tb: lambda e: e.dma_start(out=at[:], in_=av[:, :, tb * TB:(tb + 1) * TB]))(),
              ab, writes=[ab])
        for blk in range(n_blk):
            c0 = col0 + blk * 512
            sl, slb = slabs.next()
            for k0 in range(0, KC, 8):
                st, stb = stage.next()
                S.dma("sync", (lambda st=st, k0=k0, c0=c0: lambda e: e.dma_start(
                    out=st[:], in_=wv[:, k0:k0 + 8, c0:c0 + 512]))(), stb, writes=[stb])
                eng = "vector" if ci % 2 == 0 else "scalar"
                ci += 1
                if eng == "vector":
                    fn = (lambda sl=sl, st=st, k0=k0: lambda e: e.tensor_copy(sl[:, k0:k0 + 8, :], st[:]))()
                else:
                    fn = (lambda sl=sl, st=st, k0=k0: lambda e: e.activation(out=sl[:, k0:k0 + 8, :], in_=st[:], func=AF.Copy))()
                S.op(eng, fn, reads=[stb], writes=[slb])
            for tch in range(TB // 128):
                ps, pb = C.psum.next()
                for kc in range(KC):
                    S.op("tensor", (lambda ps=ps, sl=sl, kc=kc, tch=tch: lambda e: e.matmul(
                        ps[:], at[:, kc, tch * 128:(tch + 1) * 128], sl[:, kc, :],
                        start=(kc == 0), stop=(kc == KC - 1)))(),
                        reads=[slb, ab], writes=[pb], inc=(kc == KC - 1))
                o, ob = outs.next()
                eng = "vector" if ev % 2 == 0 else "scalar"
                ev += 1
                if eng == "vector":
                    fn = (lambda o=o, ps=ps: lambda e: e.tensor_copy(o[:], ps[:]))()
                else:
                    fn = (lambda o=o, ps=ps: lambda e: e.activation(out=o[:], in_=ps[:], func=AF.Copy))()
                S.op(eng, fn, reads=[pb], writes=[ob])
                sink(blk, tb * TB + tch * 128, o, ob)


def build_program():
    nc = bass.Bass("TRN2", target_bir_lowering=False)
    def dt(name, shape, dtype, **kw):
        if name in _DEBUG_DUMP and "kind" not in kw:
            kw["kind"] = "ExternalOutput"
        return nc.dram_tensor(name, shape, dtype, **kw)

    def ext_in(name, shape, dtype):
        return dt(name, list(shape), dtype, kind="ExternalInput").ap()

    x = ext_in("x", [NT, D], F32)
    yout = dt("y", [NT, D], F32, kind="ExternalOutput").ap()
    ln_in_g = ext_in("ln_in_g", [D], F32)
    ln_in_b = ext_in("ln_in_b", [D], F32)
    w_in = ext_in("w_in", [DEPTH, D, 4096], F32)
    w_att = ext_in("w_att", [DEPTH, 1024, D], F32)
    w_four = ext_in("w_four", [DEPTH, 1024, D], F32)
    w_gate = ext_in("w_gate", [DEPTH, D, 4096], F32)
    b_gate = ext_in("b_gate", [DEPTH, 4096], F32)
    w_out = ext_in("w_out", [DEPTH, D, D], F32)
    ln1_g = ext_in("ln1_g", [DEPTH, D], F32)
    ln1_b = ext_in("ln1_b", [DEPTH, D], F32)
    w_fg = ext_in("w_ffn_gate", [DEPTH, D, DFF], F32)
    w_fu = ext_in("w_ffn_up", [DEPTH, D, DFF], F32)
    w_fd = ext_in("w_ffn_down", [DEPTH, DFF, D], F32)
    ln2_g = ext_in("ln2_g", [DEPTH, D], F32)
    ln2_b = ext_in("ln2_b", [DEPTH, D], F32)
    c_bf = ext_in("c_bf", [128, 2560], BF16)
    c_f32 = ext_in("c_f32", [128, 1160], F32)
    btile = ext_in("btile", [DEPTH, 128, 8, NH, 256], F32)
    meta = ext_in("meta", [1, 16], I32)

    MB = 1 << 20
    scrA = nc.dram_tensor("scrA", [D * NT], F32).ap()
    nB = (16 + 46 + 48 + 16 + 64) * MB // 2
    scrB = nc.dram_tensor("scrB", [nB], BF16).ap()

    def regB(off_mb, rows, cols):
        o = off_mb * MB // 2
        return scrB[o:o + rows * cols].rearrange("(r c) -> r c", c=cols)

    dump_list = []

    def dbg_or(name, ap, rows, cols, dty):
        if name in _DEBUG_DUMP:
            dump_list.append((name, ap, rows, cols, dty))
        return ap

    xres = dbg_or("xres", scrA.rearrange("(r c) -> r c", c=NT), D, NT, F32)
    xbf = dbg_or("xbf", regB(0, D, NT), D, NT, BF16)
    qT = dbg_or("qT", regB(16, 1024, NT), 1024, NT, BF16)
    kext = dbg_or("kext", regB(24, 1024, 4608), 1024, 4608, BF16)
    vext = dbg_or("vext", regB(33, 4608, 1024), 4608, 1024, BF16)
    agin = regB(42, AGR, 1024)
    w4p = dbg_or("w4p", regB(42, D, D), D, D, BF16)
    attT = dbg_or("attT", regB(54, 1024, NT), 1024, NT, BF16)
    hT = dbg_or("hT", regB(16, DFF, NT), DFF, NT, BF16)
    agout = regB(62, 4 * AGR, 1024)
    mT = dbg_or("mT", regB(62, D, NT), D, NT, BF16)
    agin2 = regB(110, 512, GT)
    agout2 = regB(126, 2048, GT)
    o2 = 126 * MB // 2
    yT = dbg_or("yT", scrB[o2:o2 + 2 * D * NT].bitcast(F32).rearrange("(r c) -> r c", c=NT), D, NT, F32)

    S = Sched(nc)
    C = Ctx(nc, S)

    cbf_t = C.tile("cbf", [128, 2560], BF16)
    cf_t = C.tile("cf32", [128, 1160], F32)
    lnp_t = C.tile("lnp", [128, 10, 16], F32)
    bg_t = C.tile("bgate", [128, DEPTH, 32], F32)
    C.base = (C.off + 63) // 64 * 64
    ones_s = cbf_t[:, 0:128]
    ones1 = cbf_t[:, 128:256]
    Rm = cbf_t[:, 256:768]
    M3 = cbf_t[:, 768:1536].rearrange("p (a b) -> p a b", a=6)
    CS = cbf_t[:, 1536:2560].rearrange("p (r k c) -> p r k c", r=2, k=2)
    ident = cf_t[:, 0:128]
    Twr2 = cf_t[:, 128:640]
    Twi2 = cf_t[:, 640:1152]
    eps_t = cf_t[:, 1152:1153]

    C.psum = Rot([(nc.alloc_psum_tensor("ps%d" % i, [128, 512], F32), S.buf("ps%d" % i)) for i in range(8)])

    cb = S.dbuf("consts")
    S.dma("sync", lambda e: e.dma_start(out=cbf_t[:], in_=c_bf), cb, writes=[cb])
    S.dma("sync", lambda e: e.dma_start(out=cf_t[:], in_=c_f32), cb, writes=[cb])
    lnsrc = [ln_in_g, ln_in_b]
    for l in range(DEPTH):
        lnsrc += [ln1_g[l], ln1_b[l], ln2_g[l], ln2_b[l]]
    for i, src in enumerate(lnsrc):
        S.dma("sync", (lambda i=i, src=src: lambda e: e.dma_start(
            out=lnp_t[:, i, :], in_=src.rearrange("(c p) -> p c", p=128), allow_slow_non_contiguous=True))(),
            cb, writes=[cb])
    for l in range(DEPTH):
        S.dma("sync", (lambda l=l: lambda e: e.dma_start(
            out=bg_t[:, l, :], in_=b_gate[l].rearrange("(c p) -> p c", p=128), allow_slow_non_contiguous=True))(),
            cb, writes=[cb])

    def mk_prologue(items):
        def pro(e, st):
            for (nm, idx, mx) in items:
                r = st.enter_context(e.register("r_" + nm))
                e.reg_load(r, meta[0:1, idx:idx + 1])
                S.dyn[nm] = e.snap(r, donate=True, min_val=0, max_val=mx)
        return pro

    prologues = {
        "sync": mk_prologue([("ex", 1, 3 * 4194304)]),
        "scalar": mk_prologue([("g", 0, 3)]),
        "gpsimd": mk_prologue([("ek_top", 2, (8 * 2048 + 3 * 512) * 1024 + 768), ("ek_bot", 3, (8 * 2048 + 3 * 512) * 1024 + 768),
                               ("ev_top", 4, (11 * 2048 + 3 * 512 + 256) * 1024), ("ev_bot", 5, (11 * 2048 + 3 * 512 + 256) * 1024)]),
    }
    agflat = agout.rearrange("r c -> (r c)")
    aginU = agin[0:4096, :].rearrange("r c -> (r c)").rearrange("(k t c) -> k t c", k=8, c=128)

    def khalo_src(nm):
        return agflat[bass.ds(S.dyn[nm], 2 * 2048 * 1024)].rearrange("(h r w) -> h r w", h=2, w=1024)[:, 0:512, 0:256]

    def vhalo_src(nm):
        return agflat[bass.ds(S.dyn[nm], 256 * 1024)].rearrange("(t c) -> t c", c=1024)

    def _build_body():
        phase_transpose_in(C, x, yT, ident)
        _stop(0)
        phase_ln(C, yT, lnp_t[:, 0, :], lnp_t[:, 1, :], xres, xbf, ones_s, eps_t)
        _stop(1)

        NL = DEPTH if _DEBUG_LAYERS is None else _DEBUG_LAYERS
        for l in range(NL):
            def sink_vu(blk, tok0, o, ob):
                if blk < 2:
                    dst = vext[256 + tok0:256 + tok0 + 128, blk * 512:(blk + 1) * 512]
                else:
                    k0 = (blk - 2) * 4
                    dst = aginU[k0:k0 + 4, tok0:tok0 + 128, :].rearrange("k p c -> p k c")
                    S.dma("gpsimd", (lambda o=o, dst=dst: lambda e: e.dma_start(
                        out=dst, in_=o[:].rearrange("p (k c) -> p k c", k=4)))(), ob, reads=[ob])
                    return
                S.dma("gpsimd", (lambda o=o, dst=dst: lambda e: e.dma_start(out=dst, in_=o[:]))(), ob, reads=[ob])

            gemm_tm(C, T=NT, TB=2048, act=xbf, KC=16, w_ap=w_in[l], col0=2048, n_blk=4, sink=sink_vu)

            C.reset()
            bb = S.dbuf("bnd")
            for qi, r0 in enumerate([0, 4, 56, 60]):
                S.dma("sync", (lambda qi=qi, r0=r0: lambda e: e.dma_start(
                    out=agin[5120 + qi * 256:5120 + (qi + 1) * 256, :], in_=vext[256 + r0 * 64:256 + r0 * 64 + 256, :]))(),
                    bb, writes=[bb])
            agb = S.buf("agout", multi=True)
            for k in (10, 11, 0, 1, 2, 3, 4, 5, 6, 7):
                S.op("gpsimd", (lambda k=k: lambda e: e.collective_compute(
                    "AllGather", ALU.bypass, replica_groups=[[0, 1, 2, 3], [4, 5, 6, 7]],
                    ins=[agin[512 * k:512 * (k + 1), :]], outs=[agout[2048 * k:2048 * (k + 1), :]]))(),
                    reads=[bb], writes=[agb])
            _stop(2 + 20 * l)
            def ep_qk(oc, t0, psums, st={}):
                if oc == "alloc":
                    st["o"] = C.rot("qko", 3, [128, 512], BF16)
                    st["i"] = 0
                    return st
                ps, pb = psums[0]
                o, ob = st["o"].next()
                eng = "vector" if st["i"] % 2 == 0 else "scalar"
                st["i"] += 1
                if eng == "vector":
                    fn = (lambda o=o, ps=ps: lambda e: e.tensor_copy(o[:], ps[:]))()
                else:
                    fn = (lambda o=o, ps=ps: lambda e: e.activation(out=o[:], in_=ps[:], func=AF.Copy))()
                S.op(eng, fn, reads=[pb], writes=[ob])
                if oc < 8:
                    dst = qT[oc * 128:(oc + 1) * 128, t0:t0 + 512]
                else:
                    dst = kext[(oc - 8) * 128:(oc - 7) * 128, 256 + t0:256 + t0 + 512]
                S.dma("gpsimd", (lambda o=o, dst=dst: lambda e: e.dma_start(out=dst, in_=o[:]))(), ob, reads=[ob])

            gemm_fm(C, T=NT, TB=2048, acts=[(xbf, 16)], weights=[(w_in[l], 16, 0, 0, False)], n_oc=16, epilogue=ep_qk,
                    reset_skip=("gpsimd",))

            _stop(3 + 20 * l)
            C.reset(skip=("gpsimd",))
            bb2 = S.dbuf("bndk")
            kb_v = agin[4096:5120, :].rearrange("c (q t) -> c q t", q=4)
            for qi, r0 in enumerate([0, 4, 56, 60]):
                S.dma("sync", (lambda qi=qi, r0=r0: lambda e: e.dma_start(
                    out=kb_v[:, qi, :], in_=kext[:, 256 + r0 * 64:256 + r0 * 64 + 256]))(), bb2, writes=[bb2])
            for k in (8, 9):
                S.op("gpsimd", (lambda k=k: lambda e: e.collective_compute(
                    "AllGather", ALU.bypass, replica_groups=[[0, 1, 2, 3], [4, 5, 6, 7]],
                    ins=[agin[512 * k:512 * (k + 1), :]], outs=[agout[2048 * k:2048 * (k + 1), :]]))(),
                    reads=[bb2], writes=[agb])
            hb = S.dbuf("halo")
            S.dma("gpsimd", lambda e: e.dma_start(
                out=kext[:, 0:256].rearrange("(h c) t -> h c t", h=2), in_=khalo_src("ek_top")),
                hb, reads=[agb], writes=[hb])
            S.dma("gpsimd", lambda e: e.dma_start(
                out=kext[:, 4352:4608].rearrange("(h c) t -> h c t", h=2), in_=khalo_src("ek_bot")),
                hb, reads=[agb], writes=[hb])
            S.dma("gpsimd", lambda e: e.dma_start(out=vext[0:256, :], in_=vhalo_src("ev_top")),
                  hb, reads=[agb], writes=[hb])
            S.dma("gpsimd", lambda e: e.dma_start(out=vext[4352:4608, :], in_=vhalo_src("ev_bot")),
                  hb, reads=[agb], writes=[hb])

            _stop(4 + 20 * l)
            C.reset()
            X = C.tile("fftX", [128, 128, 256], BF16)
            Xb = S.dbuf("fftX")
            for cc in range(2):
                S.dma("sync", (lambda cc=cc: lambda e: e.dma_start(
                    out=X[:, :, cc * 128:(cc + 1) * 128],
                    in_=agflat[cc * 2097152:][bass.ds(S.dyn["ex"], 2097152)].rearrange(
                        "(p t c) -> p t c", p=128, t=128)))(), Xb, writes=[Xb])
            P1r = C.rot("fftp1", 2, [128, 512], F32, dma=False)
            P2r = C.rot("fftp2", 2, [128, 512], F32, dma=False)
            Btr = C.rot("fftB", 2, [128, 32, 2, 256], BF16, dma=False)
            Yr = C.rot("fftY", 2, [128, 32, 2, 128], BF16)
            ag2_v = agin2.rearrange("(ri ch) (t2 t1) -> t2 ri ch t1", ri=2, t2=128)
            ev = 0
            for cbk in range(8):
                Bt, Btb = Btr.next()
                for c in range(32):
                    ch = cbk * 32 + c
                    ps, pb = C.psum.next()
                    S.op("tensor", (lambda ps=ps, ch=ch: lambda e: e.matmul(ps[:], X[:, :, ch], Rm, start=True, stop=True))(),
                         reads=[Xb, cb], writes=[pb])
                    p1, p1b = P1r.next()
                    p2, p2b = P2r.next()
                    S.op("vector", (lambda p1=p1, ps=ps: lambda e: e.tensor_tensor(out=p1[:], in0=ps[:], in1=Twr2, op=ALU.mult))(),
                         reads=[pb], writes=[p1b])
                    S.op("vector", (lambda p2=p2, ps=ps: lambda e: e.tensor_tensor(out=p2[:], in0=ps[:], in1=Twi2, op=ALU.mult))(),
                         reads=[pb], writes=[p2b])
                    S.op("gpsimd", (lambda Bt=Bt, c=c, p1=p1, p2=p2: lambda e: e.tensor_tensor(
                        out=Bt[:, c, 0, :], in0=p1[:, 0:256], in1=p2[:, 256:512], op=ALU.subtract))(),
                        reads=[p1b, p2b], writes=[Btb])
                    S.op("gpsimd", (lambda Bt=Bt, c=c, p1=p1, p2=p2: lambda e: e.tensor_tensor(
                        out=Bt[:, c, 1, :], in0=p2[:, 0:256], in1=p1[:, 256:512], op=ALU.add))(),
                        reads=[p1b, p2b], writes=[Btb])
                if cbk == 0 and "dbgB" in _DEBUG_DUMP and l == 0:
                    dbb = S.dbuf("dbgdma")
                    dbgB = dt("dbgB", [128, 32 * 2 * 256], BF16).ap()
                    dbgX = dt("dbgX", [128, 128 * 256], BF16).ap()
                    S.dma("sync", (lambda Bt=Bt: lambda e: e.dma_start(out=dbgB, in_=Bt[:].rearrange("p a b c -> p (a b c)")))(), dbb, reads=[Btb])
                    S.dma("sync", lambda e: e.dma_start(out=dbgX, in_=X[:].rearrange("p a b -> p (a b)")), dbb, reads=[Xb])
                Y, Yb = Yr.next()
                for q in range(8):
                    psr, prb = C.psum.next()
                    psi, pib = C.psum.next()
                    terms_r = [(0, 0, 0), (1, 0, 1), (2, 1, 0), (3, 1, 1)]
                    terms_i = [(0, 1, 0), (1, 1, 1), (4, 0, 0), (5, 0, 1)]
                    for n, (mi, ri, hf) in enumerate(terms_r):
                        S.op("tensor", (lambda psr=psr, Bt=Bt, q=q, mi=mi, ri=ri, hf=hf, n=n: lambda e: e.matmul(
                            psr[:].rearrange("p (a b) -> p a b", a=4), M3[:, mi, :],
                            Bt[:, 4 * q:4 * q + 4, ri, hf * 128:(hf + 1) * 128], start=(n == 0), stop=(n == 3)))(),
                            reads=[Btb], writes=[prb], inc=(n == 3))
                    for n, (mi, ri, hf) in enumerate(terms_i):
                        S.op("tensor", (lambda psi=psi, Bt=Bt, q=q, mi=mi, ri=ri, hf=hf, n=n: lambda e: e.matmul(
                            psi[:].rearrange("p (a b) -> p a b", a=4), M3[:, mi, :],
                            Bt[:, 4 * q:4 * q + 4, ri, hf * 128:(hf + 1) * 128], start=(n == 0), stop=(n == 3)))(),
                            reads=[Btb], writes=[pib], inc=(n == 3))
                    S.op("scalar", (lambda Y=Y, psr=psr, q=q: lambda e: e.activation(
                        out=Y[:, 4 * q:4 * q + 4, 0, :], in_=psr[:].rearrange("p (a b) -> p a b", a=4), func=AF.Copy))(),
                        reads=[prb], writes=[Yb])
                    S.op("scalar", (lambda Y=Y, psi=psi, q=q: lambda e: e.activation(
                        out=Y[:, 4 * q:4 * q + 4, 1, :], in_=psi[:].rearrange("p (a b) -> p a b", a=4), func=AF.Copy))(),
                        reads=[pib], writes=[Yb])
                if cbk == 0 and "dbgB" in _DEBUG_DUMP and l == 0:
                    dbgY = dt("dbgY", [128, 32 * 2 * 128], BF16).ap()
                    S.dma("sync", (lambda Y=Y: lambda e: e.dma_start(out=dbgY, in_=Y[:].rearrange("p a b c -> p (a b c)")))(), dbb, reads=[Yb])
                for ri in range(2):
                    S.dma("sync", (lambda Y=Y, cbk=cbk, ri=ri: lambda e: e.dma_start(
                        out=ag2_v[:, ri, cbk * 32:(cbk + 1) * 32, :], in_=Y[:, :, ri, :]))(), Yb, reads=[Yb])
            C.reset()
            ag2b = S.buf("agout2", multi=True)
            for k in range(16):
                S.op("gpsimd", (lambda k=k: lambda e: e.collective_compute(
                    "AllGather", ALU.bypass, replica_groups=[[0, 1, 2, 3], [4, 5, 6, 7]],
                    ins=[agin2[32 * k:32 * (k + 1), :]], outs=[agout2[128 * k:128 * (k + 1), :]]))(),
                    writes=[ag2b])

            _stop(5 + 20 * l)
            w4b = C.tile("w4b", [128, 8, D], BF16)
            w4bb = S.buf("w4b")
            w4st = C.rot("w4st", 2, [128, 8, 512], F32)
            w4v = w_four[l].rearrange("(kc p) n -> p kc n", p=128)
            for n4 in range(4):
                st, stb = w4st.next()
                S.dma("sync", (lambda st=st, n4=n4, w4v=w4v: lambda e: e.dma_start(out=st[:], in_=w4v[:, :, n4 * 512:(n4 + 1) * 512]))(),
                      stb, writes=[stb])
                S.op("vector", (lambda st=st, n4=n4, w4b=w4b: lambda e: e.tensor_copy(w4b[:, :, n4 * 512:(n4 + 1) * 512], st[:]))(),
                     reads=[stb], writes=[w4bb])
            w4o = C.rot("w4o", 3, [128, 512], BF16)
            for rr in range(4):
                for ri in range(2):
                    for hc in range(2):
                        R = rr * 4 + ri * 2 + hc
                        for n4 in range(4):
                            ps, pb = C.psum.next()
                            for kk in range(2):
                                S.op("tensor", (lambda ps=ps, ri=ri, kk=kk, hc=hc, rr=rr, n4=n4: lambda e: e.matmul(
                                    ps[:], CS[:, ri, kk, hc * 128:(hc + 1) * 128], w4b[:, rr * 2 + kk, n4 * 512:(n4 + 1) * 512],
                                    start=(kk == 0), stop=(kk == 1)))(), reads=[w4bb, cb], writes=[pb], inc=(kk == 1))
                            o, ob = w4o.next()
                            S.op("scalar", (lambda o=o, ps=ps: lambda e: e.activation(out=o[:], in_=ps[:], func=AF.Copy))(),
                                 reads=[pb], writes=[ob])
                            for pg in range(4):
                                kk2 = 8 * ri + 4 * hc + pg
                                S.dma("sync", (lambda o=o, kk2=kk2, rr=rr, pg=pg, n4=n4: lambda e: e.dma_start(
                                    out=w4p[kk2 * 128 + rr * 32:kk2 * 128 + rr * 32 + 32, n4 * 512:(n4 + 1) * 512],
                                    in_=o[32 * pg:32 * pg + 32, :]))(), ob, reads=[ob])

            _stop(6 + 20 * l)
            C.reset(skip=("gpsimd",))
            aT = C.rot("attaT", 1, [128, 2, NT], BF16)
            qtr = C.rot("attq", 2, [128, 2, NT], BF16)
            ktr = C.rot("attk", 2, [128, 2, 4608], BF16)
            ver = C.rot("attve", 2, [128, 36, 256], BF16)
            vor = C.rot("attvo", 2, [128, 35, 256], BF16)
            btr = C.rot("attbt", 2, [128, 8, 2, 256], F32)
            sbt = C.rot("attsb", 2, [128, 512], F32, dma=False)
            Er = C.rot("attE", 2, [128, 2, 4, 64], BF16, dma=False)
            rdr = C.rot("attrd", 2, [128, 128], F32, dma=False)
            ve_v = vext[0:4608, :].rearrange("(j p) c -> p j c", p=128)
            vo_v = vext[64:64 + 35 * 128, :].rearrange("(j p) c -> p j c", p=128)
            for hp in range(4):
                h0 = 2 * hp
                q_t, q_b = qtr.next()
                k_t, k_b = ktr.next()
                ve_t, ve_b = ver.next()
                vo_t, vo_b = vor.next()
                bt_t, bt_b = btr.next()
                a_t, a_b = aT.next()
                S.dma("sync", (lambda q_t=q_t, h0=h0: lambda e: e.dma_start(
                    out=q_t[:], in_=qT[h0 * 128:(h0 + 2) * 128, :].rearrange("(h p) t -> p h t", p=128)))(), q_b, writes=[q_b])
                S.dma("sync", (lambda k_t=k_t, h0=h0: lambda e: e.dma_start(
                    out=k_t[:], in_=kext[h0 * 128:(h0 + 2) * 128, :].rearrange("(h p) t -> p h t", p=128)))(), k_b, writes=[k_b])
                S.dma("sync", (lambda ve_t=ve_t, h0=h0: lambda e: e.dma_start(
                    out=ve_t[:], in_=ve_v[:, :, h0 * 128:(h0 + 2) * 128]))(), ve_b, writes=[ve_b])
                S.dma("sync", (lambda vo_t=vo_t, h0=h0: lambda e: e.dma_start(
                    out=vo_t[:], in_=vo_v[:, :, h0 * 128:(h0 + 2) * 128]))(), vo_b, writes=[vo_b])
                S.dma("sync", (lambda bt_t=bt_t, h0=h0, l=l: lambda e: e.dma_start(
                    out=bt_t[:], in_=btile[l, :, :, h0:h0 + 2, :]))(), bt_b, writes=[bt_b])
                for r in range(64):
                    var = 0
                    if r < 4:
                        var = 1 + r
                    elif r > 60:
                        var = 5 + (r - 61)
                    pss, psb = C.psum.next()
                    for hh in range(2):
                        for jj in range(4):
                            S.op("tensor", (lambda pss=pss, k_t=k_t, q_t=q_t, hh=hh, r=r, jj=jj: lambda e: e.matmul(
                                pss[:, hh * 256 + jj * 64:hh * 256 + (jj + 1) * 64],
                                k_t[:, hh, 64 * r + 128 * jj:64 * r + 128 * jj + 128],
                                q_t[:, hh, 64 * r:64 * r + 64], start=True, stop=True))(),
                                reads=[k_b, q_b], writes=[psb], inc=(hh == 1 and jj == 3))
                    sb, sbb = sbt.next()
                    S.op("vector", (lambda sb=sb, pss=pss, bt_t=bt_t, var=var: lambda e: e.scalar_tensor_tensor(
                        out=sb[:], in0=pss[:], scalar=SCALE, in1=bt_t[:, var, :, :].rearrange("p a b -> p (a b)"),
                        op0=ALU.mult, op1=ALU.add))(), reads=[psb, bt_b], writes=[sbb])
                    E, Eb = Er.next()
                    S.op("scalar", (lambda E=E, sb=sb: lambda e: e.activation(
                        out=E[:].rearrange("p a b c -> p (a b c)"), in_=sb[:], func=AF.Exp))(),
                        reads=[sbb], writes=[Eb])
                    pso, pob = C.psum.next()
                    psd, pdb = C.psum.next()
                    for hh in range(2):
                        for jj in range(4):
                            if r % 2 == 0:
                                vch = ve_t[:, r // 2 + jj, hh * 128:(hh + 1) * 128]
                                vb = ve_b
                            else:
                                vch = vo_t[:, (r - 1) // 2 + jj, hh * 128:(hh + 1) * 128]
                                vb = vo_b
                            S.op("tensor", (lambda pso=pso, vch=vch, E=E, jj=jj, hh=hh: lambda e: e.matmul(
                                pso[:, hh * 64:(hh + 1) * 64], vch, E[:, hh, jj, :], start=(jj == 0), stop=(jj == 3)))(),
                                reads=[vb, Eb], writes=[pob], inc=(hh == 1 and jj == 3))
                    for jj in range(4):
                        S.op("tensor", (lambda psd=psd, E=E, jj=jj: lambda e: e.matmul(
                            psd[:, 0:128].rearrange("p (a b) -> p a b", a=2), ones1, E[:, :, jj, :],
                            start=(jj == 0), stop=(jj == 3)))(),
                            reads=[Eb, cb], writes=[pdb], inc=(jj == 3))
                    rd, rdb = rdr.next()
                    S.op("vector", (lambda rd=rd, psd=psd: lambda e: e.reciprocal(out=rd[:], in_=psd[:, 0:128]))(),
                         reads=[pdb], writes=[rdb])
                    S.op("vector", (lambda a_t=a_t, pso=pso, rd=rd, r=r: lambda e: e.tensor_tensor(
                        out=a_t[:, :, 64 * r:64 * r + 64], in0=pso[:, 0:128].rearrange("p (a b) -> p a b", a=2),
                        in1=rd[:].rearrange("p (a b) -> p a b", a=2), op=ALU.mult))(),
                        reads=[pob, rdb], writes=[a_b])
                S.dma("sync", (lambda a_t=a_t, h0=h0: lambda e: e.dma_start(
                    out=attT[h0 * 128:(h0 + 2) * 128, :].rearrange("(h p) t -> p h t", p=128), in_=a_t[:]))(), a_b, reads=[a_b])

            _stop(7 + 20 * l)
            def ep_mix(oc, t0, psums, st={}, l=l):
                if oc == "alloc":
                    st["ga"] = C.rot("mxga", 2, [128, 512], F32, dma=False)
                    st["gf"] = C.rot("mxgf", 2, [128, 512], F32, dma=False)
                    st["m1"] = C.rot("mxm1", 2, [128, 512], F32, dma=False)
                    st["m"] = C.rot("mxm", 3, [128, 512], BF16)
                    return st
                (pa, pab), (pf, pfb), (pga, pgab), (pgf, pgfb) = psums
                ga, gab = st["ga"].next()
                gf, gfb = st["gf"].next()
                m1, m1b = st["m1"].next()
                m, mb = st["m"].next()
                S.op("scalar", (lambda ga=ga, pga=pga, oc=oc: lambda e: e.activation(
                    out=ga[:], in_=pga[:], func=AF.Sigmoid, bias=bg_t[:, l, oc:oc + 1], scale=1.0))(), reads=[pgab, cb], writes=[gab])
                S.op("scalar", (lambda gf=gf, pgf=pgf, oc=oc: lambda e: e.activation(
                    out=gf[:], in_=pgf[:], func=AF.Sigmoid, bias=bg_t[:, l, 16 + oc:17 + oc], scale=1.0))(), reads=[pgfb, cb], writes=[gfb])
                S.op("vector", (lambda ga=ga, pa=pa: lambda e: e.tensor_tensor(out=ga[:], in0=ga[:], in1=pa[:], op=ALU.mult))(),
                     reads=[gab, pab], writes=[gab])
                S.op("vector", (lambda gf=gf, pf=pf, m1=m1: lambda e: e.tensor_tensor(out=m1[:], in0=gf[:], in1=pf[:], op=ALU.mult))(),
                     reads=[gfb, pfb], writes=[m1b])
                S.op("vector", (lambda m=m, ga=ga, m1=m1: lambda e: e.tensor_tensor(out=m[:], in0=ga[:], in1=m1[:], op=ALU.add))(),
                     reads=[gab, m1b], writes=[mb])
                S.dma("gpsimd", (lambda m=m, oc=oc, t0=t0: lambda e: e.dma_start(
                    out=mT[oc * 128:(oc + 1) * 128, t0:t0 + 512], in_=m[:]))(), mb, reads=[mb])

            def vc_src(tb, e=None):
                return agout2.rearrange("(kc p) (g t) -> p kc g t", p=128, g=4)[
                    :, :, bass.ds(S.dyn["g"], 1), tb * 1024:(tb + 1) * 1024].rearrange("p kc g t -> p kc (g t)")

            gemm_fm(C, T=NT, TB=1024,
                    acts=[(attT, 8), (vc_src, 16), (xbf, 16)],
                    weights=[(w_att[l], 8, 0, 0, False), (w4p, 16, 1, 0, True),
                             (w_gate[l], 16, 2, 0, False), (w_gate[l], 16, 2, 2048, False)],
                    n_oc=16, epilogue=ep_mix)

            _stop(8 + 20 * l)
            def make_ep_res(srcname):
                def ep_res(oc, t0, psums, st={}):
                    if oc == "alloc":
                        st["x"] = C.rot(srcname + "x", 3, [128, 512], F32)
                        st["y"] = C.rot(srcname + "y", 3, [128, 512], F32)
                        return st
                    ps, pb = psums[0]
                    xt, xb_ = st["x"].next()
                    yt, yb_ = st["y"].next()
                    S.dma("sync", (lambda xt=xt, oc=oc, t0=t0: lambda e: e.dma_start(
                        out=xt[:], in_=xres[oc * 128:(oc + 1) * 128, t0:t0 + 512]))(), xb_, writes=[xb_])
                    S.op("vector", (lambda yt=yt, xt=xt, ps=ps: lambda e: e.scalar_tensor_tensor(
                        out=yt[:], in0=xt[:], scalar=ALPHA, in1=ps[:], op0=ALU.mult, op1=ALU.add))(),
                        reads=[xb_, pb], writes=[yb_])
                    S.dma("gpsimd", (lambda yt=yt, oc=oc, t0=t0: lambda e: e.dma_start(
                        out=yT[oc * 128:(oc + 1) * 128, t0:t0 + 512], in_=yt[:]))(), yb_, reads=[yb_])
                return ep_res

            gemm_fm(C, T=NT, TB=2048, acts=[(mT, 16)], weights=[(w_out[l], 16, 0, 0, False)], n_oc=16,
                    epilogue=make_ep_res("r1"))
            phase_ln(C, yT, lnp_t[:, 2 + 4 * l, :], lnp_t[:, 3 + 4 * l, :], xres, xbf, ones_s, eps_t)

            _stop(9 + 20 * l)
            def ep_ffn(oc, t0, psums, st={}):
                if oc == "alloc":
                    st["s"] = C.rot("ffs", 2, [128, 512], F32, dma=False)
                    st["h"] = C.rot("ffh", 3, [128, 512], BF16)
                    return st
                (pg, pgb), (pu, pub) = psums
                s, sb_ = st["s"].next()
                h, hb_ = st["h"].next()
                S.op("scalar", (lambda s=s, pg=pg: lambda e: e.activation(out=s[:], in_=pg[:], func=AF.Silu))(),
                     reads=[pgb], writes=[sb_])
                S.op("vector", (lambda h=h, s=s, pu=pu: lambda e: e.tensor_tensor(out=h[:], in0=s[:], in1=pu[:], op=ALU.mult))(),
                     reads=[sb_, pub], writes=[hb_])
                S.dma("gpsimd", (lambda h=h, oc=oc, t0=t0: lambda e: e.dma_start(
                    out=hT[oc * 128:(oc + 1) * 128, t0:t0 + 512], in_=h[:]))(), hb_, reads=[hb_])

            gemm_fm(C, T=NT, TB=2048, acts=[(xbf, 16)],
                    weights=[(w_fg[l], 16, 0, 0, False), (w_fu[l], 16, 0, 0, False)], n_oc=44, epilogue=ep_ffn)

            _stop(10 + 20 * l)
            gemm_fm(C, T=NT, TB=1024, acts=[(hT, 44)], weights=[(w_fd[l], 44, 0, 0, False)], n_oc=16,
                    epilogue=make_ep_res("r2"))
            _stop(11 + 20 * l)
            if l < NL - 1:
                phase_ln(C, yT, lnp_t[:, 4 + 4 * l, :], lnp_t[:, 5 + 4 * l, :], xres, xbf, ones_s, eps_t)
            else:
                phase_ln(C, yT, lnp_t[:, 4 + 4 * l, :], lnp_t[:, 5 + 4 * l, :], xres, xbf, ones_s, eps_t,
                         final_out=yout, ident=ident)


    try:
        _build_body()
    except _StopBuild:
        pass

    if dump_list:
        C.reset()
        db_ = S.dbuf("dumpcopy")
        for (name, ap, rows, cols, dty) in dump_list:
            dst = nc.dram_tensor(name, [rows, cols], dty, kind="ExternalOutput").ap()
            nchunk = 8
            rr_ = rows // nchunk
            for i in range(nchunk):
                S.dma("sync", (lambda dst=dst, ap=ap, i=i, rr_=rr_: lambda e: e.dma_start(
                    out=dst[i * rr_:(i + 1) * rr_, :], in_=ap[i * rr_:(i + 1) * rr_, :]))(), db_, writes=[db_])
    C.reset()
    with nc.Block() as block:
        S.run(block, prologues=prologues)
    return nc


def _consts_for_core(c):
    g = c % 4
    groupB = c >= 4
    cbf = np.zeros((128, 2560), np.float32)
    cbf[:, 0:128] = 1.0 / 2048.0
    cbf[:, 128:256] = 1.0
    p = np.arange(128)
    k = np.arange(128)
    R = np.zeros((128, 2, 2, 128), np.float64)
    if not groupB:
        ang = 2 * np.pi * np.outer(p, k) / 128.0
        for hf in range(2):
            R[:, 0, hf, :] = np.cos(ang)
            R[:, 1, hf, :] = -np.sin(ang)
    else:
        for hf in range(2):
            rows = np.arange(64) + 64 * hf
            ang = 2 * np.pi * np.outer(np.arange(64), k) / 64.0
            R[rows, 0, hf, :] = np.cos(ang)
            R[rows, 1, hf, :] = -np.sin(ang)
    cbf[:, 256:768] = R.reshape(128, 512)
    M3 = np.zeros((128, 6, 128), np.float64)
    b = np.arange(64)
    for hf in range(2):
        if not groupB:
            ang = 2 * np.pi * np.outer(p, 64 * hf + b) / 128.0
        else:
            ang = 2 * np.pi * np.outer(p, b) / 64.0
        M3[:, 0 + hf, 64 * hf:64 * hf + 64] = np.cos(ang)
        M3[:, 2 + hf, 64 * hf:64 * hf + 64] = np.sin(ang)
        M3[:, 4 + hf, 64 * hf:64 * hf + 64] = -np.sin(ang)
    cbf[:, 768:1536] = M3.reshape(128, 768)
    T = 8192.0 if groupB else 16384.0
    norm = 1.0 / math.sqrt(T * 256.0)
    CSm = np.zeros((128, 2, 2, 256), np.float64)
    cc = np.arange(256)
    for kk in range(2):
        ang = 2 * np.pi * np.outer(kk * 128 + p, cc) / 256.0
        CSm[:, 0, kk, :] = np.cos(ang) * norm
        CSm[:, 1, kk, :] = np.sin(ang) * norm
    cbf[:, 1536:2560] = CSm.reshape(128, 1024)

    cf = np.zeros((128, 1160), np.float32)
    cf[:, 0:128] = np.eye(128)
    if not groupB:
        ang = 2 * np.pi * np.outer(p, k) / 16384.0
    else:
        ang = 2 * np.pi * np.outer(p, k) / 8192.0
    twr = np.cos(ang)
    twi = -np.sin(ang)
    twr2 = np.concatenate([twr, twr, twr, twr], axis=1)
    twi2 = np.concatenate([twi, twi, twi, twi], axis=1)
    cf[:, 128:640] = twr2
    cf[:, 640:1152] = twi2
    cf[:, 1152] = EPS

    top_seq = (c in (0, 4, 6))
    bot_seq = (c in (3, 5, 7))
    meta = np.zeros((1, 16), np.int32)
    meta[0, 0] = g
    meta[0, 1] = g * 4194304
    rs, q = (g, 1) if top_seq else (g - 1, 3)
    meta[0, 2] = (8 * 2048 + rs * 512) * 1024 + q * 256
    meta[0, 4] = ((10 + q // 2) * 2048 + rs * 512 + (q % 2) * 256) * 1024
    rs, q = (g, 2) if bot_seq else (g + 1, 0)
    meta[0, 3] = (8 * 2048 + rs * 512) * 1024 + q * 256
    meta[0, 5] = ((10 + q // 2) * 2048 + rs * 512 + (q % 2) * 256) * 1024
    return cbf.astype(NPBF), cf, meta, top_seq, bot_seq


def _bias_tiles(rpb, top_seq, bot_seq):
    L = rpb.shape[0]
    out = np.full((L, 128, 8, NH, 4, 64), NEG, np.float32)
    pp = np.arange(128)
    kc = pp % 64
    cq = np.arange(64)
    cs = np.clip(cq - 8, 0, 48)
    valid = (kc[:, None] >= cs[None, :]) & (kc[:, None] < cs[None, :] + 16)
    cidx = np.clip(kc[:, None] - cq[None, :] + 15, 0, 30)
    for var in range(8):
        for jj in range(4):
            i = 2 * jj + pp // 64
            ridx = i + 3
            if var >= 1 and var <= 4 and top_seq:
                r = var - 1
                ridx = np.where(r + i < 4, i + 11, i + 3)
            if var >= 5 and bot_seq:
                r = 61 + (var - 5)
                ridx = np.where(r + i >= 68, i - 5, i + 3)
            ridx = np.clip(ridx, 0, 14)
            vals = rpb[:, :, ridx[:, None], cidx]
            vals = np.where(valid[None, None], vals, NEG)
            out[:, :, var, :, jj, :] = np.transpose(vals, (0, 2, 1, 3))
    return out.reshape(L, 128, 8, NH, 256)


_NC_CACHE = {}


def kernel(x_prompt, x_sample, ln_in_g, ln_in_b, w_in, rpb, w_att, w_four, w_gate, b_gate, w_out,
           ln1_g, ln1_b, w_ffn_gate, w_ffn_up, w_ffn_down, ln2_g, ln2_b):
    f32 = lambda a: np.ascontiguousarray(np.asarray(a, dtype=np.float32))
    xall = np.concatenate([f32(x_prompt).reshape(-1, D), f32(x_sample).reshape(-1, D)], axis=0)
    if "nc" not in _NC_CACHE:
        _NC_CACHE["nc"] = build_program()
    nc = _NC_CACHE["nc"]
    shared = {
        "ln_in_g": f32(ln_in_g), "ln_in_b": f32(ln_in_b), "w_in": f32(w_in), "w_att": f32(w_att),
        "w_four": f32(w_four), "w_gate": f32(w_gate), "b_gate": f32(b_gate), "w_out": f32(w_out),
        "ln1_g": f32(ln1_g), "ln1_b": f32(ln1_b), "w_ffn_gate": f32(w_ffn_gate), "w_ffn_up": f32(w_ffn_up),
        "w_ffn_down": f32(w_ffn_down), "ln2_g": f32(ln2_g), "ln2_b": f32(ln2_b),
    }
    rpb = f32(rpb)
    in_maps = []
    for c in range(NCORES):
        cbf, cf, meta, top_seq, bot_seq = _consts_for_core(c)
        m = dict(shared)
        m["x"] = xall[c * NT:(c + 1) * NT]
        m["c_bf"] = cbf
        m["c_f32"] = cf
        m["meta"] = meta
        m["btile"] = _bias_tiles(rpb, top_seq, bot_seq)
        in_maps.append(m)
    res = run_bass_kernel_spmd(nc, in_maps, core_ids=list(range(NCORES)))
    _NC_CACHE["last"] = res
    yall = np.concatenate([res.results[c]["y"] for c in range(NCORES)], axis=0)
    y_prompt = yall[:16384].reshape(1, 16384, D).astype(np.float32)
    y_sample = yall[16384:].reshape(2, 8192, D).astype(np.float32)
    return (y_prompt, y_sample)
```

```python
import math
from contextlib import ExitStack

import numpy as np
import ml_dtypes

import concourse.bass as bass
import concourse.mybir as mybir
from concourse.bass_utils import run_bass_kernel_spmd

F32 = mybir.dt.float32
BF16 = mybir.dt.bfloat16
I32 = mybir.dt.int32
AF = mybir.ActivationFunctionType
ALU = mybir.AluOpType
NPBF = ml_dtypes.bfloat16

NCORES = 8
NT = 4096
D = 2048
DFF = 5632
DEPTH = 2
NH = 8
ALPHA = (2.0 * DEPTH) ** 0.25
EPS = 1e-5
SCALE = 128 ** -0.5
NEG = -30000.0
GT = 16384
AGR = 6144

ENGS = ["tensor", "vector", "scalar", "gpsimd", "sync"]
import os as _os
_DEBUG_STOP = int(_os.environ["KSTOP"]) if "KSTOP" in _os.environ else None
_DEBUG_DUMP = ()
_DEBUG_LAYERS = None


class _StopBuild(Exception):
    pass


def _stop(n):
    if _DEBUG_STOP is not None and n >= _DEBUG_STOP:
        raise _StopBuild()


class Buf:
    __slots__ = ("name", "w", "r", "slot", "multi")

    def __init__(self, name, slot=None, multi=False):
        self.name = name
        self.w = {}
        self.r = {}
        self.slot = slot
        self.multi = multi


class Sched:
    def __init__(self, nc, n_dslots=56):
        self.nc = nc
        self.sems = {}
        self.count = {}
        self.ops = {e: [] for e in ENGS}
        self.seen = {e: {} for e in ENGS}
        for e in ENGS:
            self.sems[e] = nc.alloc_semaphore(name="m_" + e)
            self.count[e] = 0
        self.slots = []
        for i in range(n_dslots):
            key = "d%d" % i
            self.sems[key] = nc.alloc_semaphore(name=key)
            self.count[key] = 0
            self.slots.append(key)
        self.next_slot = 0
        self.dyn = {}

    def reset_slots(self):
        self.phase_first = self.next_slot

    def dbuf(self, name, multi=False):
        assert self.next_slot - getattr(self, "phase_first", 0) < len(self.slots), "out of DMA semaphore slots"
        key = self.slots[self.next_slot % len(self.slots)]
        self.next_slot += 1
        return Buf(name, slot=key, multi=multi)

    def buf(self, name, multi=False):
        return Buf(name, multi=multi)

    def _waits(self, eng, reads, writes):
        need = {}

        def add(d):
            for k, v in d.items():
                if need.get(k, 0) < v:
                    need[k] = v
        for b in reads:
            add(b.w)
        for b in writes:
            if not b.multi:
                add(b.w)
            add(b.r)
        out = []
        seen = self.seen[eng]
        for k, v in need.items():
            if seen.get(k, 0) >= v:
                continue
            seen[k] = v
            out.append((k, v))
        return out

    def _record(self, key, val, reads, writes):
        for b in reads:
            if b.r.get(key, 0) < val:
                b.r[key] = val
        for b in writes:
            if b.multi:
                if b.w.get(key, 0) < val:
                    b.w[key] = val
            else:
                b.w = {key: val}
                b.r = {}

    def op(self, eng, fn, reads=(), writes=(), inc=True):
        waits = self._waits(eng, reads, writes)
        if eng == "tensor":
            waits = [(k, v) for (k, v) in waits if k != "tensor"]
        val = self.count[eng] + 1
        if inc:
            self.count[eng] = val
        self._record(eng, val, reads, writes)
        self.ops[eng].append((waits, fn, (eng, 1) if inc else None))

    def dma(self, eng, fn, dbuf, reads=(), writes=()):
        waits = self._waits(eng, reads, writes)
        key = dbuf.slot
        self.count[key] += 16
        self._record(key, self.count[key], reads, writes)
        self.ops[eng].append((waits, fn, (key, 16)))

    def full_barrier(self, skip=()):
        tgt = {k: v for k, v in self.count.items() if v > 0 and k not in skip}
        for e in ENGS:
            waits = []
            for k, v in tgt.items():
                if k == e and e == "tensor":
                    continue
                if self.seen[e].get(k, 0) >= v:
                    continue
                self.seen[e][k] = v
                waits.append((k, v))
            if waits:
                self.ops[e].append((waits, None, None))

    def run(self, block, prologues=None):
        prologues = prologues or {}
        sems = self.sems

        def replay(eng, e):
            for waits, fn, inc in self.ops[eng]:
                for k, v in waits:
                    e.wait_ge(sems[k], v)
                if fn is None:
                    continue
                ins = fn(e)
                if inc is not None:
                    ins.then_inc(sems[inc[0]], inc[1])

        def mk(eng):
            def body(e):
                if eng in prologues:
                    with ExitStack() as st:
                        prologues[eng](e, st)
                        replay(eng, e)
                else:
                    replay(eng, e)
            return body
        block.tensor(mk("tensor"))
        block.vector(mk("vector"))
        block.scalar(mk("scalar"))
        block.gpsimd(mk("gpsimd"))
        block.sync(mk("sync"))


class Rot:
    def __init__(self, items):
        self.items = items
        self.i = 0

    def next(self):
        it = self.items[self.i % len(self.items)]
        self.i += 1
        return it


class Ctx:
    def __init__(self, nc, S):
        self.nc = nc
        self.S = S
        self.off = 17408
        self.uid = 0
        self.base = 17408

    def reset(self, skip=()):
        self.S.full_barrier(skip)
        self.S.reset_slots()
        self.off = self.base

    def tile(self, name, shape, dtype):
        esz = 4 if dtype in (F32, I32) else 2
        n = 1
        for s in shape[1:]:
            n *= s
        nbytes = n * esz
        self.off = (self.off + 63) // 64 * 64
        assert self.off + nbytes <= 224 * 1024, ("SBUF overflow", name, self.off, nbytes)
        self.uid += 1
        t = self.nc.alloc_sbuf_tensor_at("%s_%d" % (name, self.uid), list(shape), dtype, offset=self.off)
        self.off += nbytes
        return t

    def rot(self, name, n, shape, dtype, dma=True):
        items = []
        for i in range(n):
            t = self.tile("%s%d" % (name, i), shape, dtype)
            b = self.S.dbuf("%s%d" % (name, i)) if dma else self.S.buf("%s%d" % (name, i))
            items.append((t, b))
        return Rot(items)


def phase_transpose_in(C, x, xT, ident):
    S = C.S
    C.reset()
    xin = C.rot("xin", 2, [128, D], F32)
    stg = C.rot("tstg", 2, [128, 16, 128], F32)
    xT_v = xT.rearrange("(fc p) t -> p fc t", p=128)
    ev = 0
    for tc in range(NT // 128):
        xt, xb = xin.next()
        S.dma("sync", (lambda xt=xt, tc=tc: lambda e: e.dma_start(out=xt[:], in_=x[tc * 128:(tc + 1) * 128, :]))(),
              xb, writes=[xb])
        st, sb = stg.next()
        for grp in range(4):
            ps, pb = C.psum.next()
            for j in range(4):
                fc = grp * 4 + j
                S.op("tensor", (lambda ps=ps, xt=xt, fc=fc, j=j: lambda e: e.transpose(
                    ps[:, j * 128:(j + 1) * 128], xt[:, fc * 128:(fc + 1) * 128], ident[:]))(),
                    reads=[xb], writes=[pb], inc=(j == 3))
            eng = "vector" if ev % 2 == 0 else "scalar"
            ev += 1
            if eng == "vector":
                fn = (lambda st=st, ps=ps, grp=grp: lambda e: e.tensor_copy(
                    st[:, grp * 4:(grp + 1) * 4, :], ps[:].rearrange("p (a b) -> p a b", a=4)))()
            else:
                fn = (lambda st=st, ps=ps, grp=grp: lambda e: e.activation(
                    out=st[:, grp * 4:(grp + 1) * 4, :], in_=ps[:].rearrange("p (a b) -> p a b", a=4),
                    func=AF.Copy))()
            S.op(eng, fn, reads=[pb], writes=[sb])
        S.dma("gpsimd", (lambda st=st, tc=tc: lambda e: e.dma_start(
            out=xT_v[:, :, tc * 128:(tc + 1) * 128], in_=st[:]))(), sb, reads=[sb])


def phase_ln(C, yT, gcol, bcol, xres, xbf, ones_s, eps_t, final_out=None, ident=None):
    S = C.S
    C.reset()
    yv = yT.rearrange("(kc p) t -> p kc t", p=128)
    ytl = C.rot("lny", 2, [128, 16, 512], F32)
    ybf_t = C.tile("lnybf", [128, 16, 512], BF16)
    ybf_b = S.buf("lnybf")
    ysq_t = C.tile("lnysq", [128, 16, 512], BF16)
    ysq_b = S.buf("lnysq")
    mean_t = C.tile("lnmean", [128, 512], F32)
    mean_b = S.buf("lnmean")
    var_t = C.tile("lnvar", [128, 512], F32)
    var_b = S.buf("lnvar")
    tmp_t = C.tile("lntmp", [128, 512], F32)
    tmp_b = S.buf("lntmp")
    cen = C.rot("lncen", 2, [128, 512], F32, dma=False)
    o32 = C.rot("lno32", 1, [128, 16, 512], F32)
    if final_out is None:
        obf = C.rot("lnobf", 1, [128, 16, 512], BF16)
        xres_v = xres.rearrange("(kc p) t -> p kc t", p=128)
        xbf_v = xbf.rearrange("(kc p) t -> p kc t", p=128)
    else:
        otok = C.rot("lnotok", 2, [128, D], F32)
    for tt in range(NT // 512):
        y, yb = ytl.next()
        S.dma("sync", (lambda y=y, tt=tt: lambda e: e.dma_start(out=y[:], in_=yv[:, :, tt * 512:(tt + 1) * 512]))(),
              yb, writes=[yb])
        S.op("scalar", (lambda y=y: lambda e: e.activation(out=ybf_t[:], in_=y[:], func=AF.Copy))(),
             reads=[yb], writes=[ybf_b])
        S.op("scalar", (lambda y=y: lambda e: e.activation(out=ysq_t[:], in_=y[:], func=AF.Square))(),
             reads=[yb], writes=[ysq_b])
        psm, pmb = C.psum.next()
        psq, pqb = C.psum.next()
        for kc in range(16):
            S.op("tensor", (lambda psm=psm, kc=kc: lambda e: e.matmul(
                psm[:], ones_s[:], ybf_t[:, kc, :], start=(kc == 0), stop=(kc == 15)))(),
                reads=[ybf_b], writes=[pmb], inc=(kc == 15))
        for kc in range(16):
            S.op("tensor", (lambda psq=psq, kc=kc: lambda e: e.matmul(
                psq[:], ones_s[:], ysq_t[:, kc, :], start=(kc == 0), stop=(kc == 15)))(),
                reads=[ysq_b], writes=[pqb], inc=(kc == 15))
        S.op("scalar", (lambda psm=psm: lambda e: e.activation(out=mean_t[:], in_=psm[:], func=AF.Copy))(),
             reads=[pmb], writes=[mean_b])
        S.op("vector", lambda e: e.tensor_tensor(out=tmp_t[:], in0=mean_t[:], in1=mean_t[:], op=ALU.mult),
             reads=[mean_b], writes=[tmp_b])
        S.op("vector", (lambda psq=psq: lambda e: e.tensor_tensor(
            out=var_t[:], in0=psq[:], in1=tmp_t[:], op=ALU.subtract))(),
            reads=[pqb, tmp_b], writes=[var_b])
        S.op("scalar", lambda e: e.activation(out=var_t[:], in_=var_t[:], func=AF.Sqrt, bias=eps_t[:, 0:1], scale=1.0),
             reads=[var_b], writes=[var_b])
        S.op("vector", lambda e: e.reciprocal(out=var_t[:], in_=var_t[:]), reads=[var_b], writes=[var_b])
        o, ob = o32.next()
        for kc in range(16):
            c, cb = cen.next()
            S.op("vector", (lambda c=c, y=y, kc=kc: lambda e: e.tensor_tensor(
                out=c[:], in0=y[:, kc, :], in1=mean_t[:], op=ALU.subtract))(),
                reads=[yb, mean_b], writes=[cb])
            S.op("vector", (lambda c=c: lambda e: e.tensor_tensor(
                out=c[:], in0=c[:], in1=var_t[:], op=ALU.mult))(),
                reads=[cb, var_b], writes=[cb])
            S.op("scalar", (lambda c=c, o=o, kc=kc: lambda e: e.activation(
                out=o[:, kc, :], in_=c[:], func=AF.Identity, bias=bcol[:, kc:kc + 1], scale=gcol[:, kc:kc + 1]))(),
                reads=[cb], writes=[ob])
        if final_out is None:
            ob16, ob16b = obf.next()
            S.op("gpsimd", (lambda o=o, ob16=ob16: lambda e: e.tensor_copy(ob16[:], o[:]))(),
                 reads=[ob], writes=[ob16b])
            S.dma("sync", (lambda o=o, tt=tt: lambda e: e.dma_start(
                out=xres_v[:, :, tt * 512:(tt + 1) * 512], in_=o[:]))(), ob, reads=[ob])
            S.dma("sync", (lambda ob16=ob16, tt=tt: lambda e: e.dma_start(
                out=xbf_v[:, :, tt * 512:(tt + 1) * 512], in_=ob16[:]))(), ob16b, reads=[ob16b])
        else:
            ev = 0
            for ts in range(4):
                ot, otb = otok.next()
                for grp in range(4):
                    ps, pb = C.psum.next()
                    for j in range(4):
                        kc = grp * 4 + j
                        S.op("tensor", (lambda ps=ps, o=o, kc=kc, j=j, ts=ts: lambda e: e.transpose(
                            ps[:, j * 128:(j + 1) * 128], o[:, kc, ts * 128:(ts + 1) * 128], ident[:]))(),
                            reads=[ob], writes=[pb], inc=(j == 3))
                    eng = "vector" if ev % 2 == 0 else "scalar"
                    ev += 1
                    if eng == "vector":
                        fn = (lambda ot=ot, ps=ps, grp=grp: lambda e: e.tensor_copy(
                            ot[:, grp * 512:(grp + 1) * 512], ps[:]))()
                    else:
                        fn = (lambda ot=ot, ps=ps, grp=grp: lambda e: e.activation(
                            out=ot[:, grp * 512:(grp + 1) * 512], in_=ps[:], func=AF.Copy))()
                    S.op(eng, fn, reads=[pb], writes=[otb])
                r0 = tt * 512 + ts * 128
                S.dma("sync", (lambda ot=ot, r0=r0: lambda e: e.dma_start(
                    out=final_out[r0:r0 + 128, :], in_=ot[:]))(), otb, reads=[otb])


class WLoader:
    qi = 0

    def __init__(self, C, name, w_ap, KC, stage_rot, n_slab=2, is_bf16=False):
        self.C = C
        self.w = w_ap
        self.KC = KC
        self.stage = stage_rot
        self.is_bf16 = is_bf16
        self.slabs = C.rot(name + "sl", n_slab, [128, KC, 256], BF16, dma=is_bf16)
        self.wv = w_ap.rearrange("(kc p) n -> p kc n", p=128)
        self.cast_i = 0
        if KC <= 16:
            self.pieces = [(0, KC)]
        else:
            assert KC % 11 == 0
            self.pieces = [(i, 11) for i in range(0, KC, 11)]
        self.stages = []
        if not is_bf16:
            for i, (k0, nk) in enumerate(self.pieces):
                t = C.tile("%sst%d" % (name, i), [128, nk, 256], F32)
                self.stages.append((t, C.S.dbuf("%sst%d" % (name, i))))

    def load(self, c0):
        S = self.C.S
        sl, slb = self.slabs.next()
        if self.is_bf16:
            S.dma("sync", (lambda sl=sl, c0=c0: lambda e: e.dma_start(out=sl[:], in_=self.wv[:, :, c0:c0 + 256]))(),
                  slb, writes=[slb])
            return (sl, slb, [])
        parts = []
        for i, (k0, nk) in enumerate(self.pieces):
            st, stb = self.stages[i]
            WLoader.qi += 1
            S.dma("sync" if WLoader.qi % 2 == 0 else "scalar", (lambda st=st, k0=k0, nk=nk, c0=c0: lambda e: e.dma_start(
                out=st[:, 0:nk, :], in_=self.wv[:, k0:k0 + nk, c0:c0 + 256]))(), stb, writes=[stb])
            parts.append((st, stb, k0, nk))
        return (sl, slb, parts)

    def cast(self, h):
        S = self.C.S
        sl, slb, parts = h
        for (st, stb, k0, nk) in parts:
            eng = "vector" if self.cast_i % 2 == 0 else "scalar"
            self.cast_i += 1
            if eng == "vector":
                fn = (lambda sl=sl, st=st, k0=k0, nk=nk: lambda e: e.tensor_copy(sl[:, k0:k0 + nk, :], st[:, 0:nk, :]))()
            else:
                fn = (lambda sl=sl, st=st, k0=k0, nk=nk: lambda e: e.activation(
                    out=sl[:, k0:k0 + nk, :], in_=st[:, 0:nk, :], func=AF.Copy))()
            S.op(eng, fn, reads=[stb], writes=[slb])
        return (sl, slb)


def gemm_fm(C, *, T, TB, acts, weights, n_oc, epilogue, prologue_tb=None, reset_skip=()):
    S = C.S
    C.reset(skip=reset_skip)
    atiles = []
    for i, (a, KC) in enumerate(acts):
        t = C.tile("act%d" % i, [128, KC, TB], BF16)
        b = S.dbuf("act%d" % i)
        atiles.append((t, b, a, KC))
    stage = None
    loaders = []
    for j, (w_ap, KC, ai, col0, isb) in enumerate(weights):
        loaders.append(WLoader(C, "w%d" % j, w_ap, KC, stage, is_bf16=isb))
    ep_state = epilogue("alloc", None, None)
    n_oc2 = n_oc // 2
    for tb in range(T // TB):
        for (t, b, a, KC) in atiles:
            if callable(a):
                fn = (lambda t=t, a=a, tb=tb: lambda e: e.dma_start(out=t[:], in_=a(tb, e)))()
            else:
                src = a.rearrange("(kc p) t -> p kc t", p=128)[:, :, tb * TB:(tb + 1) * TB]
                fn = (lambda t=t, src=src: lambda e: e.dma_start(out=t[:], in_=src))()
            S.dma("scalar" if callable(a) else "sync", fn, b, writes=[b])
        if prologue_tb is not None:
            prologue_tb(tb)
        handles = [ld.load(weights[j][3]) for j, ld in enumerate(loaders)]
        slabs = [ld.cast(h) for ld, h in zip(loaders, handles)]
        for oc2 in range(n_oc2):
            nxt = None
            if oc2 + 1 < n_oc2:
                nxt = [ld.load(weights[j][3] + (oc2 + 1) * 256) for j, ld in enumerate(loaders)]
            for half in range(2):
                oc = oc2 * 2 + half
                for st in range(TB // 512):
                    psums = []
                    for j, (w_ap, KC, ai, col0, isb) in enumerate(weights):
                        ps, pb = C.psum.next()
                        at, ab = atiles[ai][0], atiles[ai][1]
                        sl, slb = slabs[j]
                        for kc in range(KC):
                            S.op("tensor", (lambda ps=ps, sl=sl, at=at, kc=kc, half=half, st=st, KC=KC: lambda e: e.matmul(
                                ps[:], sl[:, kc, half * 128:(half + 1) * 128], at[:, kc, st * 512:(st + 1) * 512],
                                start=(kc == 0), stop=(kc == KC - 1)))(),
                                reads=[slb, ab], writes=[pb], inc=(kc == KC - 1))
                        psums.append((ps, pb))
                    epilogue(oc, tb * TB + st * 512, psums)
                    if half == 0 and st == 0 and nxt is not None:
                        slabs_next = [ld.cast(h) for ld, h in zip(loaders, nxt)]
            if nxt is not None:
                slabs = slabs_next


def gemm_tm(C, *, T, TB, act, KC, w_ap, col0, n_blk, sink):
    S = C.S
    C.reset()
    at = C.tile("tmact", [128, KC, TB], BF16)
    ab = S.dbuf("tmact")
    stage = C.rot("tmst", 3, [128, 8, 512], F32)
    slabs = C.rot("tmsl", 2, [128, KC, 512], BF16, dma=False)
    outs = C.rot("tmout", 3, [128, 512], BF16)
    wv = w_ap.rearrange("(kc p) n -> p kc n", p=128)
    av = act.rearrange("(kc p) t -> p kc t", p=128)
    ci = 0
    ev = 0
    for tb in range(T // TB):
        S.dma("sync", (lambda tb=tb: lambda e: e.dma_start(out=at[:], in_=av[:, :, tb * TB:(tb + 1) * TB]))(),
              ab, writes=[ab])
        for blk in range(n_blk):
            c0 = col0 + blk * 512
            sl, slb = slabs.next()
            for k0 in range(0, KC, 8):
                st, stb = stage.next()
                S.dma("sync", (lambda st=st, k0=k0, c0=c0: lambda e: e.dma_start(
                    out=st[:], in_=wv[:, k0:k0 + 8, c0:c0 + 512]))(), stb, writes=[stb])
                eng = "vector" if ci % 2 == 0 else "scalar"
                ci += 1
                if eng == "vector":
                    fn = (lambda sl=sl, st=st, k0=k0: lambda e: e.tensor_copy(sl[:, k0:k0 + 8, :], st[:]))()
                else:
                    fn = (lambda sl=sl, st=st, k0=k0: lambda e: e.activation(out=sl[:, k0:k0 + 8, :], in_=st[:], func=AF.Copy))()
                S.op(eng, fn, reads=[stb], writes=[slb])
            for tch in range(TB // 128):
                ps, pb = C.psum.next()
                for kc in range(KC):
                    S.op("tensor", (lambda ps=ps, sl=sl, kc=kc, tch=tch: lambda e: e.matmul(
                        ps[:], at[:, kc, tch * 128:(tch + 1) * 128], sl[:, kc, :],
                        start=(kc == 0), stop=(kc == KC - 1)))(),
                        reads=[slb, ab], writes=[pb], inc=(kc == KC - 1))
                o, ob = outs.next()
                eng = "vector" if ev % 2 == 0 else "scalar"
                ev += 1
                if eng == "vector":
                    fn = (lambda o=o, ps=ps: lambda e: e.tensor_copy(o[:], ps[:]))()
                else:
                    fn = (lambda o=o, ps=ps: lambda e: e.activation(out=o[:], in_=ps[:], func=AF.Copy))()
                S.op(eng, fn, reads=[pb], writes=[ob])
                sink(blk, tb * TB + tch * 128, o, ob)


def build_program():
    nc = bass.Bass("TRN2", target_bir_lowering=False)
    def dt(name, shape, dtype, **kw):
        if name in _DEBUG_DUMP and "kind" not in kw:
            kw["kind"] = "ExternalOutput"
        return nc.dram_tensor(name, shape, dtype, **kw)

    def ext_in(name, shape, dtype):
        return dt(name, list(shape), dtype, kind="ExternalInput").ap()

    x = ext_in("x", [NT, D], F32)
    yout = dt("y", [NT, D], F32, kind="ExternalOutput").ap()
    ln_in_g = ext_in("ln_in_g", [D], F32)
    ln_in_b = ext_in("ln_in_b", [D], F32)
    w_in = ext_in("w_in", [DEPTH, D, 4096], F32)
    w_att = ext_in("w_att", [DEPTH, 1024, D], F32)
    w_four = ext_in("w_four", [DEPTH, 1024, D], F32)
    w_gate = ext_in("w_gate", [DEPTH, D, 4096], F32)
    b_gate = ext_in("b_gate", [DEPTH, 4096], F32)
    w_out = ext_in("w_out", [DEPTH, D, D], F32)
    ln1_g = ext_in("ln1_g", [DEPTH, D], F32)
    ln1_b = ext_in("ln1_b", [DEPTH, D], F32)
    w_fg = ext_in("w_ffn_gate", [DEPTH, D, DFF], F32)
    w_fu = ext_in("w_ffn_up", [DEPTH, D, DFF], F32)
    w_fd = ext_in("w_ffn_down", [DEPTH, DFF, D], F32)
    ln2_g = ext_in("ln2_g", [DEPTH, D], F32)
    ln2_b = ext_in("ln2_b", [DEPTH, D], F32)
    c_bf = ext_in("c_bf", [128, 2560], BF16)
    c_f32 = ext_in("c_f32", [128, 1160], F32)
    btile = ext_in("btile", [DEPTH, 128, 8, NH, 256], F32)
    meta = ext_in("meta", [1, 16], I32)

    MB = 1 << 20
    scrA = nc.dram_tensor("scrA", [D * NT], F32).ap()
    nB = (16 + 46 + 48 + 16 + 64) * MB // 2
    scrB = nc.dram_tensor("scrB", [nB], BF16).ap()

    def regB(off_mb, rows, cols):
        o = off_mb * MB // 2
        return scrB[o:o + rows * cols].rearrange("(r c) -> r c", c=cols)

    dump_list = []

    def dbg_or(name, ap, rows, cols, dty):
        if name in _DEBUG_DUMP:
            dump_list.append((name, ap, rows, cols, dty))
        return ap

    xres = dbg_or("xres", scrA.rearrange("(r c) -> r c", c=NT), D, NT, F32)
    xbf = dbg_or("xbf", regB(0, D, NT), D, NT, BF16)
    qT = dbg_or("qT", regB(16, 1024, NT), 1024, NT, BF16)
    kext = dbg_or("kext", regB(24, 1024, 4608), 1024, 4608, BF16)
    vext = dbg_or("vext", regB(33, 4608, 1024), 4608, 1024, BF16)
    agin = regB(42, AGR, 1024)
    w4p = dbg_or("w4p", regB(42, D, D), D, D, BF16)
    attT = dbg_or("attT", regB(54, 1024, NT), 1024, NT, BF16)
    hT = dbg_or("hT", regB(16, DFF, NT), DFF, NT, BF16)
    agout = regB(62, 4 * AGR, 1024)
    mT = dbg_or("mT", regB(62, D, NT), D, NT, BF16)
    agin2 = regB(110, 512, GT)
    agout2 = regB(126, 2048, GT)
    o2 = 126 * MB // 2
    yT = dbg_or("yT", scrB[o2:o2 + 2 * D * NT].bitcast(F32).rearrange("(r c) -> r c", c=NT), D, NT, F32)

    S = Sched(nc)
    C = Ctx(nc, S)

    cbf_t = C.tile("cbf", [128, 2560], BF16)
    cf_t = C.tile("cf32", [128, 1160], F32)
    lnp_t = C.tile("lnp", [128, 10, 16], F32)
    bg_t = C.tile("bgate", [128, DEPTH, 32], F32)
    C.base = (C.off + 63) // 64 * 64
    ones_s = cbf_t[:, 0:128]
    ones1 = cbf_t[:, 128:256]
    Rm = cbf_t[:, 256:768]
    M3 = cbf_t[:, 768:1536].rearrange("p (a b) -> p a b", a=6)
    CS = cbf_t[:, 1536:2560].rearrange("p (r k c) -> p r k c", r=2, k=2)
    ident = cf_t[:, 0:128]
    Twr2 = cf_t[:, 128:640]
    Twi2 = cf_t[:, 640:1152]
    eps_t = cf_t[:, 1152:1153]

    C.psum = Rot([(nc.alloc_psum_tensor("ps%d" % i, [128, 512], F32), S.buf("ps%d" % i)) for i in range(8)])

    cb = S.dbuf("consts")
    S.dma("sync", lambda e: e.dma_start(out=cbf_t[:], in_=c_bf), cb, writes=[cb])
    S.dma("sync", lambda e: e.dma_start(out=cf_t[:], in_=c_f32), cb, writes=[cb])
    lnsrc = [ln_in_g, ln_in_b]
    for l in range(DEPTH):
        lnsrc += [ln1_g[l], ln1_b[l], ln2_g[l], ln2_b[l]]
    for i, src in enumerate(lnsrc):
        S.dma("sync", (lambda i=i, src=src: lambda e: e.dma_start(
            out=lnp_t[:, i, :], in_=src.rearrange("(c p) -> p c", p=128), allow_slow_non_contiguous=True))(),
            cb, writes=[cb])
    for l in range(DEPTH):
        S.dma("sync", (lambda l=l: lambda e: e.dma_start(
            out=bg_t[:, l, :], in_=b_gate[l].rearrange("(c p) -> p c", p=128), allow_slow_non_contiguous=True))(),
            cb, writes=[cb])

    def mk_prologue(items):
        def pro(e, st):
            for (nm, idx, mx) in items:
                r = st.enter_context(e.register("r_" + nm))
                e.reg_load(r, meta[0:1, idx:idx + 1])
                S.dyn[nm] = e.snap(r, donate=True, min_val=0, max_val=mx)
        return pro

    prologues = {
        "sync": mk_prologue([("ex", 1, 3 * 4194304)]),
        "scalar": mk_prologue([("g", 0, 3)]),
        "gpsimd": mk_prologue([("ek_top", 2, (8 * 2048 + 3 * 512) * 1024 + 768), ("ek_bot", 3, (8 * 2048 + 3 * 512) * 1024 + 768),
                               ("ev_top", 4, (11 * 2048 + 3 * 512 + 256) * 1024), ("ev_bot", 5, (11 * 2048 + 3 * 512 + 256) * 1024)]),
    }
    agflat = agout.rearrange("r c -> (r c)")
    aginU = agin[0:4096, :].rearrange("r c -> (r c)").rearrange("(k t c) -> k t c", k=8, c=128)

    def khalo_src(nm):
        return agflat[bass.ds(S.dyn[nm], 2 * 2048 * 1024)].rearrange("(h r w) -> h r w", h=2, w=1024)[:, 0:512, 0:256]

    def vhalo_src(nm):
        return agflat[bass.ds(S.dyn[nm], 256 * 1024)].rearrange("(t c) -> t c", c=1024)

    def _build_body():
        phase_transpose_in(C, x, yT, ident)
        _stop(0)
        phase_ln(C, yT, lnp_t[:, 0, :], lnp_t[:, 1, :], xres, xbf, ones_s, eps_t)
        _stop(1)

        NL = DEPTH if _DEBUG_LAYERS is None else _DEBUG_LAYERS
        for l in range(NL):
            def sink_vu(blk, tok0, o, ob):
                if blk < 2:
                    dst = vext[256 + tok0:256 + tok0 + 128, blk * 512:(blk + 1) * 512]
                else:
                    k0 = (blk - 2) * 4
                    dst = aginU[k0:k0 + 4, tok0:tok0 + 128, :].rearrange("k p c -> p k c")
                    S.dma("gpsimd", (lambda o=o, dst=dst: lambda e: e.dma_start(
                        out=dst, in_=o[:].rearrange("p (k c) -> p k c", k=4)))(), ob, reads=[ob])
                    return
                S.dma("gpsimd", (lambda o=o, dst=dst: lambda e: e.dma_start(out=dst, in_=o[:]))(), ob, reads=[ob])

            gemm_tm(C, T=NT, TB=2048, act=xbf, KC=16, w_ap=w_in[l], col0=2048, n_blk=4, sink=sink_vu)

            C.reset()
            bb = S.dbuf("bnd")
            for qi, r0 in enumerate([0, 4, 56, 60]):
                S.dma("sync", (lambda qi=qi, r0=r0: lambda e: e.dma_start(
                    out=agin[5120 + qi * 256:5120 + (qi + 1) * 256, :], in_=vext[256 + r0 * 64:256 + r0 * 64 + 256, :]))(),
                    bb, writes=[bb])
            agb = S.buf("agout", multi=True)
            for k in (10, 11, 0, 1, 2, 3, 4, 5, 6, 7):
                S.op("gpsimd", (lambda k=k: lambda e: e.collective_compute(
                    "AllGather", ALU.bypass, replica_groups=[[0, 1, 2, 3], [4, 5, 6, 7]],
                    ins=[agin[512 * k:512 * (k + 1), :]], outs=[agout[2048 * k:2048 * (k + 1), :]]))(),
                    reads=[bb], writes=[agb])
            _stop(2 + 20 * l)
            def ep_qk(oc, t0, psums, st={}):
                if oc == "alloc":
                    st["o"] = C.rot("qko", 3, [128, 512], BF16)
                    st["i"] = 0
                    return st
                ps, pb = psums[0]
                o, ob = st["o"].next()
                eng = "vector" if st["i"] % 2 == 0 else "scalar"
                st["i"] += 1
                if eng == "vector":
                    fn = (lambda o=o, ps=ps: lambda e: e.tensor_copy(o[:], ps[:]))()
                else:
                    fn = (lambda o=o, ps=ps: lambda e: e.activation(out=o[:], in_=ps[:], func=AF.Copy))()
                S.op(eng, fn, reads=[pb], writes=[ob])
                if oc < 8:
                    dst = qT[oc * 128:(oc + 1) * 128, t0:t0 + 512]
                else:
                    dst = kext[(oc - 8) * 128:(oc - 7) * 128, 256 + t0:256 + t0 + 512]
                S.dma("gpsimd", (lambda o=o, dst=dst: lambda e: e.dma_start(out=dst, in_=o[:]))(), ob, reads=[ob])

            gemm_fm(C, T=NT, TB=2048, acts=[(xbf, 16)], weights=[(w_in[l], 16, 0, 0, False)], n_oc=16, epilogue=ep_qk,
                    reset_skip=("gpsimd",))

            _stop(3 + 20 * l)
            C.reset(skip=("gpsimd",))
            bb2 = S.dbuf("bndk")
            kb_v = agin[4096:5120, :].rearrange("c (q t) -> c q t", q=4)
            for qi, r0 in enumerate([0, 4, 56, 60]):
                S.dma("sync", (lambda qi=qi, r0=r0: lambda e: e.dma_start(
                    out=kb_v[:, qi, :], in_=kext[:, 256 + r0 * 64:256 + r0 * 64 + 256]))(), bb2, writes=[bb2])
            for k in (8, 9):
                S.op("gpsimd", (lambda k=k: lambda e: e.collective_compute(
                    "AllGather", ALU.bypass, replica_groups=[[0, 1, 2, 3], [4, 5, 6, 7]],
                    ins=[agin[512 * k:512 * (k + 1), :]], outs=[agout[2048 * k:2048 * (k + 1), :]]))(),
                    reads=[bb2], writes=[agb])
            hb = S.dbuf("halo")
            S.dma("gpsimd", lambda e: e.dma_start(
                out=kext[:, 0:256].rearrange("(h c) t -> h c t", h=2), in_=khalo_src("ek_top")),
                hb, reads=[agb], writes=[hb])
            S.dma("gpsimd", lambda e: e.dma_start(
                out=kext[:, 4352:4608].rearrange("(h c) t -> h c t", h=2), in_=khalo_src("ek_bot")),
                hb, reads=[agb], writes=[hb])
            S.dma("gpsimd", lambda e: e.dma_start(out=vext[0:256, :], in_=vhalo_src("ev_top")),
                  hb, reads=[agb], writes=[hb])
            S.dma("gpsimd", lambda e: e.dma_start(out=vext[4352:4608, :], in_=vhalo_src("ev_bot")),
                  hb, reads=[agb], writes=[hb])

            _stop(4 + 20 * l)
            C.reset()
            X = C.tile("fftX", [128, 128, 256], BF16)
            Xb = S.dbuf("fftX")
            for cc in range(2):
                S.dma("sync", (lambda cc=cc: lambda e: e.dma_start(
                    out=X[:, :, cc * 128:(cc + 1) * 128],
                    in_=agflat[cc * 2097152:][bass.ds(S.dyn["ex"], 2097152)].rearrange(
                        "(p t c) -> p t c", p=128, t=128)))(), Xb, writes=[Xb])
            P1r = C.rot("fftp1", 2, [128, 512], F32, dma=False)
            P2r = C.rot("fftp2", 2, [128, 512], F32, dma=False)
            Btr = C.rot("fftB", 2, [128, 32, 2, 256], BF16, dma=False)
            Yr = C.rot("fftY", 2, [128, 32, 2, 128], BF16)
            ag2_v = agin2.rearrange("(ri ch) (t2 t1) -> t2 ri ch t1", ri=2, t2=128)
            ev = 0
            for cbk in range(8):
                Bt, Btb = Btr.next()
                for c in range(32):
                    ch = cbk * 32 + c
                    ps, pb = C.psum.next()
                    S.op("tensor", (lambda ps=ps, ch=ch: lambda e: e.matmul(ps[:], X[:, :, ch], Rm, start=True, stop=True))(),
                         reads=[Xb, cb], writes=[pb])
                    p1, p1b = P1r.next()
                    p2, p2b = P2r.next()
                    S.op("vector", (lambda p1=p1, ps=ps: lambda e: e.tensor_tensor(out=p1[:], in0=ps[:], in1=Twr2, op=ALU.mult))(),
                         reads=[pb], writes=[p1b])
                    S.op("vector", (lambda p2=p2, ps=ps: lambda e: e.tensor_tensor(out=p2[:], in0=ps[:], in1=Twi2, op=ALU.mult))(),
                         reads=[pb], writes=[p2b])
                    S.op("gpsimd", (lambda Bt=Bt, c=c, p1=p1, p2=p2: lambda e: e.tensor_tensor(
                        out=Bt[:, c, 0, :], in0=p1[:, 0:256], in1=p2[:, 256:512], op=ALU.subtract))(),
                        reads=[p1b, p2b], writes=[Btb])
                    S.op("gpsimd", (lambda Bt=Bt, c=c, p1=p1, p2=p2: lambda e: e.tensor_tensor(
                        out=Bt[:, c, 1, :], in0=p2[:, 0:256], in1=p1[:, 256:512], op=ALU.add))(),
                        reads=[p1b, p2b], writes=[Btb])
                if cbk == 0 and "dbgB" in _DEBUG_DUMP and l == 0:
                    dbb = S.dbuf("dbgdma")
                    dbgB = dt("dbgB", [128, 32 * 2 * 256], BF16).ap()
                    dbgX = dt("dbgX", [128, 128 * 256], BF16).ap()
                    S.dma("sync", (lambda Bt=Bt: lambda e: e.dma_start(out=dbgB, in_=Bt[:].rearrange("p a b c -> p (a b c)")))(), dbb, reads=[Btb])
                    S.dma("sync", lambda e: e.dma_start(out=dbgX, in_=X[:].rearrange("p a b -> p (a b)")), dbb, reads=[Xb])
                Y, Yb = Yr.next()
                for q in range(8):
                    psr, prb = C.psum.next()
                    psi, pib = C.psum.next()
                    terms_r = [(0, 0, 0), (1, 0, 1), (2, 1, 0), (3, 1, 1)]
                    terms_i = [(0, 1, 0), (1, 1, 1), (4, 0, 0), (5, 0, 1)]
                    for n, (mi, ri, hf) in enumerate(terms_r):
                        S.op("tensor", (lambda psr=psr, Bt=Bt, q=q, mi=mi, ri=ri, hf=hf, n=n: lambda e: e.matmul(
                            psr[:].rearrange("p (a b) -> p a b", a=4), M3[:, mi, :],
                            Bt[:, 4 * q:4 * q + 4, ri, hf * 128:(hf + 1) * 128], start=(n == 0), stop=(n == 3)))(),
                            reads=[Btb], writes=[prb], inc=(n == 3))
                    for n, (mi, ri, hf) in enumerate(terms_i):
                        S.op("tensor", (lambda psi=psi, Bt=Bt, q=q, mi=mi, ri=ri, hf=hf, n=n: lambda e: e.matmul(
                            psi[:].rearrange("p (a b) -> p a b", a=4), M3[:, mi, :],
                            Bt[:, 4 * q:4 * q + 4, ri, hf * 128:(hf + 1) * 128], start=(n == 0), stop=(n == 3)))(),
                            reads=[Btb], writes=[pib], inc=(n == 3))
                    S.op("scalar", (lambda Y=Y, psr=psr, q=q: lambda e: e.activation(
                        out=Y[:, 4 * q:4 * q + 4, 0, :], in_=psr[:].rearrange("p (a b) -> p a b", a=4), func=AF.Copy))(),
                        reads=[prb], writes=[Yb])
                    S.op("scalar", (lambda Y=Y, psi=psi, q=q: lambda e: e.activation(
                        out=Y[:, 4 * q:4 * q + 4, 1, :], in_=psi[:].rearrange("p (a b) -> p a b", a=4), func=AF.Copy))(),
                        reads=[pib], writes=[Yb])
                if cbk == 0 and "dbgB" in _DEBUG_DUMP and l == 0:
                    dbgY = dt("dbgY", [128, 32 * 2 * 128], BF16).ap()
                    S.dma("sync", (lambda Y=Y: lambda e: e.dma_start(out=dbgY, in_=Y[:].rearrange("p a b c -> p (a b c)")))(), dbb, reads=[Yb])
                for ri in range(2):
                    S.dma("sync", (lambda Y=Y, cbk=cbk, ri=ri: lambda e: e.dma_start(
                        out=ag2_v[:, ri, cbk * 32:(cbk + 1) * 32, :], in_=Y[:, :, ri, :]))(), Yb, reads=[Yb])
            C.reset()
            ag2b = S.buf("agout2", multi=True)
            for k in range(16):
                S.op("gpsimd", (lambda k=k: lambda e: e.collective_compute(
                    "AllGather", ALU.bypass, replica_groups=[[0, 1, 2, 3], [4, 5, 6, 7]],
                    ins=[agin2[32 * k:32 * (k + 1), :]], outs=[agout2[128 * k:128 * (k + 1), :]]))(),
                    writes=[ag2b])

            _stop(5 + 20 * l)
            w4b = C.tile("w4b", [128, 8, D], BF16)
            w4bb = S.buf("w4b")
            w4st = C.rot("w4st", 2, [128, 8, 512], F32)
            w4v = w_four[l].rearrange("(kc p) n -> p kc n", p=128)
            for n4 in range(4):
                st, stb = w4st.next()
                S.dma("sync", (lambda st=st, n4=n4, w4v=w4v: lambda e: e.dma_start(out=st[:], in_=w4v[:, :, n4 * 512:(n4 + 1) * 512]))(),
                      stb, writes=[stb])
                S.op("vector", (lambda st=st, n4=n4, w4b=w4b: lambda e: e.tensor_copy(w4b[:, :, n4 * 512:(n4 + 1) * 512], st[:]))(),
                     reads=[stb], writes=[w4bb])
            w4o = C.rot("w4o", 3, [128, 512], BF16)
            for rr in range(4):
                for ri in range(2):
                    for hc in range(2):
                        R = rr * 4 + ri * 2 + hc
                        for n4 in range(4):
                            ps, pb = C.psum.next()
                            for kk in range(2):
                                S.op("tensor", (lambda ps=ps, ri=ri, kk=kk, hc=hc, rr=rr, n4=n4: lambda e: e.matmul(
                                    ps[:], CS[:, ri, kk, hc * 128:(hc + 1) * 128], w4b[:, rr * 2 + kk, n4 * 512:(n4 + 1) * 512],
                                    start=(kk == 0), stop=(kk == 1)))(), reads=[w4bb, cb], writes=[pb], inc=(kk == 1))
                            o, ob = w4o.next()
                            S.op("scalar", (lambda o=o, ps=ps: lambda e: e.activation(out=o[:], in_=ps[:], func=AF.Copy))(),
                                 reads=[pb], writes=[ob])
                            for pg in range(4):
                                kk2 = 8 * ri + 4 * hc + pg
                                S.dma("sync", (lambda o=o, kk2=kk2, rr=rr, pg=pg, n4=n4: lambda e: e.dma_start(
                                    out=w4p[kk2 * 128 + rr * 32:kk2 * 128 + rr * 32 + 32, n4 * 512:(n4 + 1) * 512],
                                    in_=o[32 * pg:32 * pg + 32, :]))(), ob, reads=[ob])

            _stop(6 + 20 * l)
            C.reset(skip=("gpsimd",))
            aT = C.rot("attaT", 1, [128, 2, NT], BF16)
            qtr = C.rot("attq", 2, [128, 2, NT], BF16)
            ktr = C.rot("attk", 2, [128, 2, 4608], BF16)
            ver = C.rot("attve", 2, [128, 36, 256], BF16)
            vor = C.rot("attvo", 2, [128, 35, 256], BF16)
            btr = C.rot("attbt", 2, [128, 8, 2, 256], F32)
            sbt = C.rot("attsb", 2, [128, 512], F32, dma=False)
            Er = C.rot("attE", 3, [128, 2, 4, 64], BF16, dma=False)
            rdr = C.rot("attrd", 2, [128, 128], F32, dma=False)
            ve_v = vext[0:4608, :].rearrange("(j p) c -> p j c", p=128)
            vo_v = vext[64:64 + 35 * 128, :].rearrange("(j p) c -> p j c", p=128)
            for hp in range(4):
                h0 = 2 * hp
                q_t, q_b = qtr.next()
                k_t, k_b = ktr.next()
                ve_t, ve_b = ver.next()
                vo_t, vo_b = vor.next()
                bt_t, bt_b = btr.next()
                a_t, a_b = aT.next()
                S.dma("sync", (lambda q_t=q_t, h0=h0: lambda e: e.dma_start(
                    out=q_t[:], in_=qT[h0 * 128:(h0 + 2) * 128, :].rearrange("(h p) t -> p h t", p=128)))(), q_b, writes=[q_b])
                S.dma("sync", (lambda k_t=k_t, h0=h0: lambda e: e.dma_start(
                    out=k_t[:], in_=kext[h0 * 128:(h0 + 2) * 128, :].rearrange("(h p) t -> p h t", p=128)))(), k_b, writes=[k_b])
                S.dma("sync", (lambda ve_t=ve_t, h0=h0: lambda e: e.dma_start(
                    out=ve_t[:], in_=ve_v[:, :, h0 * 128:(h0 + 2) * 128]))(), ve_b, writes=[ve_b])
                S.dma("sync", (lambda vo_t=vo_t, h0=h0: lambda e: e.dma_start(
                    out=vo_t[:], in_=vo_v[:, :, h0 * 128:(h0 + 2) * 128]))(), vo_b, writes=[vo_b])
                S.dma("sync", (lambda bt_t=bt_t, h0=h0, l=l: lambda e: e.dma_start(
                    out=bt_t[:], in_=btile[l, :, :, h0:h0 + 2, :]))(), bt_b, writes=[bt_b])
                def stage_a(r):
                    var = 0
                    if r < 4:
                        var = 1 + r
                    elif r > 60:
                        var = 5 + (r - 61)
                    pss, psb = C.psum.next()
                    for hh in range(2):
                        for jj in range(4):
                            S.op("tensor", (lambda pss=pss, hh=hh, r=r, jj=jj, k_t=k_t, q_t=q_t: lambda e: e.matmul(
                                pss[:, hh * 256 + jj * 64:hh * 256 + (jj + 1) * 64],
                                k_t[:, hh, 64 * r + 128 * jj:64 * r + 128 * jj + 128],
                                q_t[:, hh, 64 * r:64 * r + 64], start=True, stop=True))(),
                                reads=[k_b, q_b], writes=[psb], inc=(hh == 1 and jj == 3))
                    sb, sbb = sbt.next()
                    S.op("vector", (lambda sb=sb, pss=pss, var=var, bt_t=bt_t: lambda e: e.scalar_tensor_tensor(
                        out=sb[:], in0=pss[:], scalar=SCALE, in1=bt_t[:, var, :, :].rearrange("p a b -> p (a b)"),
                        op0=ALU.mult, op1=ALU.add))(), reads=[psb, bt_b], writes=[sbb])
                    E, Eb = Er.next()
                    S.op("scalar", (lambda E=E, sb=sb: lambda e: e.activation(
                        out=E[:].rearrange("p a b c -> p (a b c)"), in_=sb[:], func=AF.Exp))(),
                        reads=[sbb], writes=[Eb])
                    return (r, E, Eb)

                def stage_b(st_):
                    r, E, Eb = st_
                    pso, pob = C.psum.next()
                    psd, pdb = C.psum.next()
                    for hh in range(2):
                        for jj in range(4):
                            if r % 2 == 0:
                                vch = ve_t[:, r // 2 + jj, hh * 128:(hh + 1) * 128]
                                vb = ve_b
                            else:
                                vch = vo_t[:, (r - 1) // 2 + jj, hh * 128:(hh + 1) * 128]
                                vb = vo_b
                            S.op("tensor", (lambda pso=pso, vch=vch, E=E, jj=jj, hh=hh: lambda e: e.matmul(
                                pso[:, hh * 64:(hh + 1) * 64], vch, E[:, hh, jj, :], start=(jj == 0), stop=(jj == 3)))(),
                                reads=[vb, Eb], writes=[pob], inc=(hh == 1 and jj == 3))
                    for jj in range(4):
                        S.op("tensor", (lambda psd=psd, E=E, jj=jj: lambda e: e.matmul(
                            psd[:, 0:128].rearrange("p (a b) -> p a b", a=2), ones1, E[:, :, jj, :],
                            start=(jj == 0), stop=(jj == 3)))(),
                            reads=[Eb, cb], writes=[pdb], inc=(jj == 3))
                    rd, rdb = rdr.next()
                    S.op("vector", (lambda rd=rd, psd=psd: lambda e: e.reciprocal(out=rd[:], in_=psd[:, 0:128]))(),
                         reads=[pdb], writes=[rdb])
                    S.op("vector", (lambda pso=pso, rd=rd, r=r, a_t=a_t: lambda e: e.tensor_tensor(
                        out=a_t[:, :, 64 * r:64 * r + 64], in0=pso[:, 0:128].rearrange("p (a b) -> p a b", a=2),
                        in1=rd[:].rearrange("p (a b) -> p a b", a=2), op=ALU.mult))(),
                        reads=[pob, rdb], writes=[a_b])

                prev_ = None
                for r in range(65):
                    cur_ = stage_a(r) if r < 64 else None
                    if prev_ is not None:
                        stage_b(prev_)
                    prev_ = cur_
                S.dma("sync", (lambda a_t=a_t, h0=h0: lambda e: e.dma_start(
                    out=attT[h0 * 128:(h0 + 2) * 128, :].rearrange("(h p) t -> p h t", p=128), in_=a_t[:]))(), a_b, reads=[a_b])

            _stop(7 + 20 * l)
            def ep_mix(oc, t0, psums, st={}, l=l):
                if oc == "alloc":
                    st["ga"] = C.rot("mxga", 2, [128, 512], F32, dma=False)
                    st["gf"] = C.rot("mxgf", 2, [128, 512], F32, dma=False)
                    st["m1"] = C.rot("mxm1", 2, [128, 512], F32, dma=False)
                    st["m"] = C.rot("mxm", 3, [128, 512], BF16)
                    return st
                (pa, pab), (pf, pfb), (pga, pgab), (pgf, pgfb) = psums
                ga, gab = st["ga"].next()
                gf, gfb = st["gf"].next()
                m1, m1b = st["m1"].next()
                m, mb = st["m"].next()
                S.op("scalar", (lambda ga=ga, pga=pga, oc=oc: lambda e: e.activation(
                    out=ga[:], in_=pga[:], func=AF.Sigmoid, bias=bg_t[:, l, oc:oc + 1], scale=1.0))(), reads=[pgab, cb], writes=[gab])
                S.op("scalar", (lambda gf=gf, pgf=pgf, oc=oc: lambda e: e.activation(
                    out=gf[:], in_=pgf[:], func=AF.Sigmoid, bias=bg_t[:, l, 16 + oc:17 + oc], scale=1.0))(), reads=[pgfb, cb], writes=[gfb])
                S.op("vector", (lambda ga=ga, pa=pa: lambda e: e.tensor_tensor(out=ga[:], in0=ga[:], in1=pa[:], op=ALU.mult))(),
                     reads=[gab, pab], writes=[gab])
                S.op("vector", (lambda gf=gf, pf=pf, m1=m1: lambda e: e.tensor_tensor(out=m1[:], in0=gf[:], in1=pf[:], op=ALU.mult))(),
                     reads=[gfb, pfb], writes=[m1b])
                S.op("vector", (lambda m=m, ga=ga, m1=m1: lambda e: e.tensor_tensor(out=m[:], in0=ga[:], in1=m1[:], op=ALU.add))(),
                     reads=[gab, m1b], writes=[mb])
                S.dma("gpsimd", (lambda m=m, oc=oc, t0=t0: lambda e: e.dma_start(
                    out=mT[oc * 128:(oc + 1) * 128, t0:t0 + 512], in_=m[:]))(), mb, reads=[mb])

            def vc_src(tb, e=None):
                return agout2.rearrange("(kc p) (g t) -> p kc g t", p=128, g=4)[
                    :, :, bass.ds(S.dyn["g"], 1), tb * 1024:(tb + 1) * 1024].rearrange("p kc g t -> p kc (g t)")

            gemm_fm(C, T=NT, TB=1024,
                    acts=[(attT, 8), (vc_src, 16), (xbf, 16)],
                    weights=[(w_att[l], 8, 0, 0, False), (w4p, 16, 1, 0, True),
                             (w_gate[l], 16, 2, 0, False), (w_gate[l], 16, 2, 2048, False)],
                    n_oc=16, epilogue=ep_mix)

            _stop(8 + 20 * l)
            def make_ep_res(srcname):
                def ep_res(oc, t0, psums, st={}):
                    if oc == "alloc":
                        st["x"] = C.rot(srcname + "x", 3, [128, 512], F32)
                        st["y"] = C.rot(srcname + "y", 3, [128, 512], F32)
                        return st
                    ps, pb = psums[0]
                    xt, xb_ = st["x"].next()
                    yt, yb_ = st["y"].next()
                    S.dma("sync", (lambda xt=xt, oc=oc, t0=t0: lambda e: e.dma_start(
                        out=xt[:], in_=xres[oc * 128:(oc + 1) * 128, t0:t0 + 512]))(), xb_, writes=[xb_])
                    S.op("vector", (lambda yt=yt, xt=xt, ps=ps: lambda e: e.scalar_tensor_tensor(
                        out=yt[:], in0=xt[:], scalar=ALPHA, in1=ps[:], op0=ALU.mult, op1=ALU.add))(),
                        reads=[xb_, pb], writes=[yb_])
                    S.dma("gpsimd", (lambda yt=yt, oc=oc, t0=t0: lambda e: e.dma_start(
                        out=yT[oc * 128:(oc + 1) * 128, t0:t0 + 512], in_=yt[:]))(), yb_, reads=[yb_])
                return ep_res

            gemm_fm(C, T=NT, TB=2048, acts=[(mT, 16)], weights=[(w_out[l], 16, 0, 0, False)], n_oc=16,
                    epilogue=make_ep_res("r1"))
            phase_ln(C, yT, lnp_t[:, 2 + 4 * l, :], lnp_t[:, 3 + 4 * l, :], xres, xbf, ones_s, eps_t)

            _stop(9 + 20 * l)
            def ep_ffn(oc, t0, psums, st={}):
                if oc == "alloc":
                    st["s"] = C.rot("ffs", 2, [128, 512], F32, dma=False)
                    st["h"] = C.rot("ffh", 3, [128, 512], BF16)
                    return st
                (pg, pgb), (pu, pub) = psums
                s, sb_ = st["s"].next()
                h, hb_ = st["h"].next()
                S.op("scalar", (lambda s=s, pg=pg: lambda e: e.activation(out=s[:], in_=pg[:], func=AF.Silu))(),
                     reads=[pgb], writes=[sb_])
                S.op("vector", (lambda h=h, s=s, pu=pu: lambda e: e.tensor_tensor(out=h[:], in0=s[:], in1=pu[:], op=ALU.mult))(),
                     reads=[sb_, pub], writes=[hb_])
                S.dma("gpsimd", (lambda h=h, oc=oc, t0=t0: lambda e: e.dma_start(
                    out=hT[oc * 128:(oc + 1) * 128, t0:t0 + 512], in_=h[:]))(), hb_, reads=[hb_])

            gemm_fm(C, T=NT, TB=2048, acts=[(xbf, 16)],
                    weights=[(w_fg[l], 16, 0, 0, False), (w_fu[l], 16, 0, 0, False)], n_oc=44, epilogue=ep_ffn)

            _stop(10 + 20 * l)
            gemm_fm(C, T=NT, TB=1024, acts=[(hT, 44)], weights=[(w_fd[l], 44, 0, 0, False)], n_oc=16,
                    epilogue=make_ep_res("r2"))
            _stop(11 + 20 * l)
            if l < NL - 1:
                phase_ln(C, yT, lnp_t[:, 4 + 4 * l, :], lnp_t[:, 5 + 4 * l, :], xres, xbf, ones_s, eps_t)
            else:
                phase_ln(C, yT, lnp_t[:, 4 + 4 * l, :], lnp_t[:, 5 + 4 * l, :], xres, xbf, ones_s, eps_t,
                         final_out=yout, ident=ident)


    try:
        _build_body()
    except _StopBuild:
        pass

    if dump_list:
        C.reset()
        db_ = S.dbuf("dumpcopy")
        for (name, ap, rows, cols, dty) in dump_list:
            dst = nc.dram_tensor(name, [rows, cols], dty, kind="ExternalOutput").ap()
            nchunk = 8
            rr_ = rows // nchunk
            for i in range(nchunk):
                S.dma("sync", (lambda dst=dst, ap=ap, i=i, rr_=rr_: lambda e: e.dma_start(
                    out=dst[i * rr_:(i + 1) * rr_, :], in_=ap[i * rr_:(i + 1) * rr_, :]))(), db_, writes=[db_])
    C.reset()
    with nc.Block() as block:
        S.run(block, prologues=prologues)
    return nc


def _consts_for_core(c):
    g = c % 4
    groupB = c >= 4
    cbf = np.zeros((128, 2560), np.float32)
    cbf[:, 0:128] = 1.0 / 2048.0
    cbf[:, 128:256] = 1.0
    p = np.arange(128)
    k = np.arange(128)
    R = np.zeros((128, 2, 2, 128), np.float64)
    if not groupB:
        ang = 2 * np.pi * np.outer(p, k) / 128.0
        for hf in range(2):
            R[:, 0, hf, :] = np.cos(ang)
            R[:, 1, hf, :] = -np.sin(ang)
    else:
        for hf in range(2):
            rows = np.arange(64) + 64 * hf
            ang = 2 * np.pi * np.outer(np.arange(64), k) / 64.0
            R[rows, 0, hf, :] = np.cos(ang)
            R[rows, 1, hf, :] = -np.sin(ang)
    cbf[:, 256:768] = R.reshape(128, 512)
    M3 = np.zeros((128, 6, 128), np.float64)
    b = np.arange(64)
    for hf in range(2):
        if not groupB:
            ang = 2 * np.pi * np.outer(p, 64 * hf + b) / 128.0
        else:
            ang = 2 * np.pi * np.outer(p, b) / 64.0
        M3[:, 0 + hf, 64 * hf:64 * hf + 64] = np.cos(ang)
        M3[:, 2 + hf, 64 * hf:64 * hf + 64] = np.sin(ang)
        M3[:, 4 + hf, 64 * hf:64 * hf + 64] = -np.sin(ang)
    cbf[:, 768:1536] = M3.reshape(128, 768)
    T = 8192.0 if groupB else 16384.0
    norm = 1.0 / math.sqrt(T * 256.0)
    CSm = np.zeros((128, 2, 2, 256), np.float64)
    cc = np.arange(256)
    for kk in range(2):
        ang = 2 * np.pi * np.outer(kk * 128 + p, cc) / 256.0
        CSm[:, 0, kk, :] = np.cos(ang) * norm
        CSm[:, 1, kk, :] = np.sin(ang) * norm
    cbf[:, 1536:2560] = CSm.reshape(128, 1024)

    cf = np.zeros((128, 1160), np.float32)
    cf[:, 0:128] = np.eye(128)
    if not groupB:
        ang = 2 * np.pi * np.outer(p, k) / 16384.0
    else:
        ang = 2 * np.pi * np.outer(p, k) / 8192.0
    twr = np.cos(ang)
    twi = -np.sin(ang)
    twr2 = np.concatenate([twr, twr, twr, twr], axis=1)
    twi2 = np.concatenate([twi, twi, twi, twi], axis=1)
    cf[:, 128:640] = twr2
    cf[:, 640:1152] = twi2
    cf[:, 1152] = EPS

    top_seq = (c in (0, 4, 6))
    bot_seq = (c in (3, 5, 7))
    meta = np.zeros((1, 16), np.int32)
    meta[0, 0] = g
    meta[0, 1] = g * 4194304
    rs, q = (g, 1) if top_seq else (g - 1, 3)
    meta[0, 2] = (8 * 2048 + rs * 512) * 1024 + q * 256
    meta[0, 4] = ((10 + q // 2) * 2048 + rs * 512 + (q % 2) * 256) * 1024
    rs, q = (g, 2) if bot_seq else (g + 1, 0)
    meta[0, 3] = (8 * 2048 + rs * 512) * 1024 + q * 256
    meta[0, 5] = ((10 + q // 2) * 2048 + rs * 512 + (q % 2) * 256) * 1024
    return cbf.astype(NPBF), cf, meta, top_seq, bot_seq


def _bias_tiles(rpb, top_seq, bot_seq):
    L = rpb.shape[0]
    out = np.full((L, 128, 8, NH, 4, 64), NEG, np.float32)
    pp = np.arange(128)
    kc = pp % 64
    cq = np.arange(64)
    cs = np.clip(cq - 8, 0, 48)
    valid = (kc[:, None] >= cs[None, :]) & (kc[:, None] < cs[None, :] + 16)
    cidx = np.clip(kc[:, None] - cq[None, :] + 15, 0, 30)
    for var in range(8):
        for jj in range(4):
            i = 2 * jj + pp // 64
            ridx = i + 3
            if var >= 1 and var <= 4 and top_seq:
                r = var - 1
                ridx = np.where(r + i < 4, i + 11, i + 3)
            if var >= 5 and bot_seq:
                r = 61 + (var - 5)
                ridx = np.where(r + i >= 68, i - 5, i + 3)
            ridx = np.clip(ridx, 0, 14)
            vals = rpb[:, :, ridx[:, None], cidx]
            vals = np.where(valid[None, None], vals, NEG)
            out[:, :, var, :, jj, :] = np.transpose(vals, (0, 2, 1, 3))
    return out.reshape(L, 128, 8, NH, 256)


_NC_CACHE = {}


def kernel(x_prompt, x_sample, ln_in_g, ln_in_b, w_in, rpb, w_att, w_four, w_gate, b_gate, w_out,
           ln1_g, ln1_b, w_ffn_gate, w_ffn_up, w_ffn_down, ln2_g, ln2_b):
    f32 = lambda a: np.ascontiguousarray(np.asarray(a, dtype=np.float32))
    xall = np.concatenate([f32(x_prompt).reshape(-1, D), f32(x_sample).reshape(-1, D)], axis=0)
    if "nc" not in _NC_CACHE:
        _NC_CACHE["nc"] = build_program()
    nc = _NC_CACHE["nc"]
    shared = {
        "ln_in_g": f32(ln_in_g), "ln_in_b": f32(ln_in_b), "w_in": f32(w_in), "w_att": f32(w_att),
        "w_four": f32(w_four), "w_gate": f32(w_gate), "b_gate": f32(b_gate), "w_out": f32(w_out),
        "ln1_g": f32(ln1_g), "ln1_b": f32(ln1_b), "w_ffn_gate": f32(w_ffn_gate), "w_ffn_up": f32(w_ffn_up),
        "w_ffn_down": f32(w_ffn_down), "ln2_g": f32(ln2_g), "ln2_b": f32(ln2_b),
    }
    rpb = f32(rpb)
    in_maps = []
    for c in range(NCORES):
        cbf, cf, meta, top_seq, bot_seq = _consts_for_core(c)
        m = dict(shared)
        m["x"] = xall[c * NT:(c + 1) * NT]
        m["c_bf"] = cbf
        m["c_f32"] = cf
        m["meta"] = meta
        m["btile"] = _bias_tiles(rpb, top_seq, bot_seq)
        in_maps.append(m)
    res = run_bass_kernel_spmd(nc, in_maps, core_ids=list(range(NCORES)))
    _NC_CACHE["last"] = res
    yall = np.concatenate([res.results[c]["y"] for c in range(NCORES)], axis=0)
    y_prompt = yall[:16384].reshape(1, 16384, D).astype(np.float32)
    y_sample = yall[16384:].reshape(2, 8192, D).astype(np.float32)
    return (y_prompt, y_sample)
```

```python
import math
from contextlib import ExitStack

import numpy as np
import ml_dtypes

import concourse.bass as bass
import concourse.mybir as mybir
from concourse.bass_utils import run_bass_kernel_spmd

F32 = mybir.dt.float32
BF16 = mybir.dt.bfloat16
I32 = mybir.dt.int32
AF = mybir.ActivationFunctionType
ALU = mybir.AluOpType
NPBF = ml_dtypes.bfloat16

NCORES = 8
NT = 4096
D = 2048
DFF = 5632
DEPTH = 2
NH = 8
ALPHA = (2.0 * DEPTH) ** 0.25
EPS = 1e-5
SCALE = 128 ** -0.5
NEG = -30000.0
GT = 16384
AGR = 6144

ENGS = ["tensor", "vector", "scalar", "gpsimd", "sync"]
_DEBUG_STOP = None
_DEBUG_DUMP = ()
_DEBUG_LAYERS = None


class _StopBuild(Exception):
    pass


def _stop(n):
    if _DEBUG_STOP is not None and n >= _DEBUG_STOP:
        raise _StopBuild()


class Buf:
    __slots__ = ("name", "w", "r", "slot", "multi")

    def __init__(self, name, slot=None, multi=False):
        self.name = name
        self.w = {}
        self.r = {}
        self.slot = slot
        self.multi = multi


class Sched:
    def __init__(self, nc, n_dslots=56):
        self.nc = nc
        self.sems = {}
        self.count = {}
        self.ops = {e: [] for e in ENGS}
        self.seen = {e: {} for e in ENGS}
        for e in ENGS:
            self.sems[e] = nc.alloc_semaphore(name="m_" + e)
            self.count[e] = 0
        self.slots = []
        for i in range(n_dslots):
            key = "d%d" % i
            self.sems[key] = nc.alloc_semaphore(name=key)
            self.count[key] = 0
            self.slots.append(key)
        self.next_slot = 0
        self.dyn = {}

    def reset_slots(self):
        self.phase_first = self.next_slot

    def dbuf(self, name, multi=False):
        assert self.next_slot - getattr(self, "phase_first", 0) < len(self.slots), "out of DMA semaphore slots"
        key = self.slots[self.next_slot % len(self.slots)]
        self.next_slot += 1
        return Buf(name, slot=key, multi=multi)

    def buf(self, name, multi=False):
        return Buf(name, multi=multi)

    def _waits(self, eng, reads, writes):
        need = {}

        def add(d):
            for k, v in d.items():
                if need.get(k, 0) < v:
                    need[k] = v
        for b in reads:
            add(b.w)
        for b in writes:
            if not b.multi:
                add(b.w)
            add(b.r)
        out = []
        seen = self.seen[eng]
        for k, v in need.items():
            if seen.get(k, 0) >= v:
                continue
            seen[k] = v
            out.append((k, v))
        return out

    def _record(self, key, val, reads, writes):
        for b in reads:
            if b.r.get(key, 0) < val:
                b.r[key] = val
        for b in writes:
            if b.multi:
                if b.w.get(key, 0) < val:
                    b.w[key] = val
            else:
                b.w = {key: val}
                b.r = {}

    def op(self, eng, fn, reads=(), writes=(), inc=True):
        waits = self._waits(eng, reads, writes)
        if eng == "tensor":
            waits = [(k, v) for (k, v) in waits if k != "tensor"]
        val = self.count[eng] + 1
        if inc:
            self.count[eng] = val
        self._record(eng, val, reads, writes)
        self.ops[eng].append((waits, fn, (eng, 1) if inc else None))

    def dma(self, eng, fn, dbuf, reads=(), writes=()):
        waits = self._waits(eng, reads, writes)
        key = dbuf.slot
        self.count[key] += 16
        self._record(key, self.count[key], reads, writes)
        self.ops[eng].append((waits, fn, (key, 16)))

    def full_barrier(self, skip=()):
        tgt = {k: v for k, v in self.count.items() if v > 0 and k not in skip}
        for e in ENGS:
            waits = []
            for k, v in tgt.items():
                if k == e and e == "tensor":
                    continue
                if self.seen[e].get(k, 0) >= v:
                    continue
                self.seen[e][k] = v
                waits.append((k, v))
            if waits:
                self.ops[e].append((waits, None, None))

    def run(self, block, prologues=None):
        prologues = prologues or {}
        sems = self.sems

        def replay(eng, e):
            for waits, fn, inc in self.ops[eng]:
                for k, v in waits:
                    e.wait_ge(sems[k], v)
                if fn is None:
                    continue
                ins = fn(e)
                if inc is not None:
                    ins.then_inc(sems[inc[0]], inc[1])

        def mk(eng):
            def body(e):
                if eng in prologues:
                    with ExitStack() as st:
                        prologues[eng](e, st)
                        replay(eng, e)
                else:
                    replay(eng, e)
            return body
        block.tensor(mk("tensor"))
        block.vector(mk("vector"))
        block.scalar(mk("scalar"))
        block.gpsimd(mk("gpsimd"))
        block.sync(mk("sync"))


class Rot:
    def __init__(self, items):
        self.items = items
        self.i = 0

    def next(self):
        it = self.items[self.i % len(self.items)]
        self.i += 1
        return it


class Ctx:
    def __init__(self, nc, S):
        self.nc = nc
        self.S = S
        self.off = 17408
        self.uid = 0
        self.base = 17408

    def reset(self, skip=()):
        self.S.full_barrier(skip)
        self.S.reset_slots()
        self.off = self.base

    def tile(self, name, shape, dtype):
        esz = 4 if dtype in (F32, I32) else 2
        n = 1
        for s in shape[1:]:
            n *= s
        nbytes = n * esz
        self.off = (self.off + 63) // 64 * 64
        assert self.off + nbytes <= 224 * 1024, ("SBUF overflow", name, self.off, nbytes)
        self.uid += 1
        t = self.nc.alloc_sbuf_tensor_at("%s_%d" % (name, self.uid), list(shape), dtype, offset=self.off)
        self.off += nbytes
        return t

    def rot(self, name, n, shape, dtype, dma=True):
        items = []
        for i in range(n):
            t = self.tile("%s%d" % (name, i), shape, dtype)
            b = self.S.dbuf("%s%d" % (name, i)) if dma else self.S.buf("%s%d" % (name, i))
            items.append((t, b))
        return Rot(items)


def phase_transpose_in(C, x, xT, ident):
    S = C.S
    C.reset()
    xin = C.rot("xin", 2, [128, D], F32)
    stg = C.rot("tstg", 2, [128, 16, 128], F32)
    xT_v = xT.rearrange("(fc p) t -> p fc t", p=128)
    ev = 0
    for tc in range(NT // 128):
        xt, xb = xin.next()
        S.dma("sync", (lambda xt=xt, tc=tc: lambda e: e.dma_start(out=xt[:], in_=x[tc * 128:(tc + 1) * 128, :]))(),
              xb, writes=[xb])
        st, sb = stg.next()
        for grp in range(4):
            ps, pb = C.psum.next()
            for j in range(4):
                fc = grp * 4 + j
                S.op("tensor", (lambda ps=ps, xt=xt, fc=fc, j=j: lambda e: e.transpose(
                    ps[:, j * 128:(j + 1) * 128], xt[:, fc * 128:(fc + 1) * 128], ident[:]))(),
                    reads=[xb], writes=[pb], inc=(j == 3))
            eng = "vector" if ev % 2 == 0 else "scalar"
            ev += 1
            if eng == "vector":
                fn = (lambda st=st, ps=ps, grp=grp: lambda e: e.tensor_copy(
                    st[:, grp * 4:(grp + 1) * 4, :], ps[:].rearrange("p (a b) -> p a b", a=4)))()
            else:
                fn = (lambda st=st, ps=ps, grp=grp: lambda e: e.activation(
                    out=st[:, grp * 4:(grp + 1) * 4, :], in_=ps[:].rearrange("p (a b) -> p a b", a=4),
                    func=AF.Copy))()
            S.op(eng, fn, reads=[pb], writes=[sb])
        S.dma("gpsimd", (lambda st=st, tc=tc: lambda e: e.dma_start(
            out=xT_v[:, :, tc * 128:(tc + 1) * 128], in_=st[:]))(), sb, reads=[sb])


def phase_ln(C, yT, gcol, bcol, xres, xbf, ones_s, eps_t, final_out=None, ident=None):
    S = C.S
    C.reset()
    yv = yT.rearrange("(kc p) t -> p kc t", p=128)
    ytl = C.rot("lny", 2, [128, 16, 512], F32)
    ybf_t = C.tile("lnybf", [128, 16, 512], BF16)
    ybf_b = S.buf("lnybf")
    ysq_t = C.tile("lnysq", [128, 16, 512], BF16)
    ysq_b = S.buf("lnysq")
    mean_t = C.tile("lnmean", [128, 512], F32)
    mean_b = S.buf("lnmean")
    var_t = C.tile("lnvar", [128, 512], F32)
    var_b = S.buf("lnvar")
    tmp_t = C.tile("lntmp", [128, 512], F32)
    tmp_b = S.buf("lntmp")
    cen = C.rot("lncen", 4, [128, 512], F32, dma=False)
    o32 = C.rot("lno32", 1, [128, 16, 512], F32)
    if final_out is None:
        obf = C.rot("lnobf", 1, [128, 16, 512], BF16)
        xres_v = xres.rearrange("(kc p) t -> p kc t", p=128)
        xbf_v = xbf.rearrange("(kc p) t -> p kc t", p=128)
    else:
        otok = C.rot("lnotok", 2, [128, D], F32)
    for tt in range(NT // 512):
        y, yb = ytl.next()
        S.dma("sync", (lambda y=y, tt=tt: lambda e: e.dma_start(out=y[:], in_=yv[:, :, tt * 512:(tt + 1) * 512]))(),
              yb, writes=[yb])
        S.op("scalar", (lambda y=y: lambda e: e.activation(out=ybf_t[:], in_=y[:], func=AF.Copy))(),
             reads=[yb], writes=[ybf_b])
        S.op("scalar", (lambda y=y: lambda e: e.activation(out=ysq_t[:], in_=y[:], func=AF.Square))(),
             reads=[yb], writes=[ysq_b])
        psm, pmb = C.psum.next()
        psq, pqb = C.psum.next()
        for kc in range(16):
            S.op("tensor", (lambda psm=psm, kc=kc: lambda e: e.matmul(
                psm[:], ones_s[:], ybf_t[:, kc, :], start=(kc == 0), stop=(kc == 15)))(),
                reads=[ybf_b], writes=[pmb], inc=(kc == 15))
        for kc in range(16):
            S.op("tensor", (lambda psq=psq, kc=kc: lambda e: e.matmul(
                psq[:], ones_s[:], ysq_t[:, kc, :], start=(kc == 0), stop=(kc == 15)))(),
                reads=[ysq_b], writes=[pqb], inc=(kc == 15))
        S.op("scalar", (lambda psm=psm: lambda e: e.activation(out=mean_t[:], in_=psm[:], func=AF.Copy))(),
             reads=[pmb], writes=[mean_b])
        S.op("vector", lambda e: e.tensor_tensor(out=tmp_t[:], in0=mean_t[:], in1=mean_t[:], op=ALU.mult),
             reads=[mean_b], writes=[tmp_b])
        S.op("vector", (lambda psq=psq: lambda e: e.tensor_tensor(
            out=var_t[:], in0=psq[:], in1=tmp_t[:], op=ALU.subtract))(),
            reads=[pqb, tmp_b], writes=[var_b])
        S.op("scalar", lambda e: e.activation(out=var_t[:], in_=var_t[:], func=AF.Sqrt, bias=eps_t[:, 0:1], scale=1.0),
             reads=[var_b], writes=[var_b])
        S.op("vector", lambda e: e.reciprocal(out=var_t[:], in_=var_t[:]), reads=[var_b], writes=[var_b])
        o, ob = o32.next()
        prev_ = None
        for kc in range(17):
            cur_ = None
            if kc < 16:
                c, cb = cen.next()
                S.op("vector", (lambda c=c, y=y, kc=kc: lambda e: e.tensor_tensor(
                    out=c[:], in0=y[:, kc, :], in1=mean_t[:], op=ALU.subtract))(),
                    reads=[yb, mean_b], writes=[cb])
                cur_ = (c, cb, kc)
            if prev_ is not None:
                pc, pcb, pkc = prev_
                S.op("vector", (lambda pc=pc: lambda e: e.tensor_tensor(
                    out=pc[:], in0=pc[:], in1=var_t[:], op=ALU.mult))(),
                    reads=[pcb, var_b], writes=[pcb])
                S.op("scalar", (lambda pc=pc, o=o, pkc=pkc: lambda e: e.activation(
                    out=o[:, pkc, :], in_=pc[:], func=AF.Identity, bias=bcol[:, pkc:pkc + 1], scale=gcol[:, pkc:pkc + 1]))(),
                    reads=[pcb], writes=[ob])
            prev_ = cur_
        if final_out is None:
            ob16, ob16b = obf.next()
            S.op("gpsimd", (lambda o=o, ob16=ob16: lambda e: e.tensor_copy(ob16[:], o[:]))(),
                 reads=[ob], writes=[ob16b])
            S.dma("sync", (lambda o=o, tt=tt: lambda e: e.dma_start(
                out=xres_v[:, :, tt * 512:(tt + 1) * 512], in_=o[:]))(), ob, reads=[ob])
            S.dma("sync", (lambda ob16=ob16, tt=tt: lambda e: e.dma_start(
                out=xbf_v[:, :, tt * 512:(tt + 1) * 512], in_=ob16[:]))(), ob16b, reads=[ob16b])
        else:
            ev = 0
            for ts in range(4):
                ot, otb = otok.next()
                for grp in range(4):
                    ps, pb = C.psum.next()
                    for j in range(4):
                        kc = grp * 4 + j
                        S.op("tensor", (lambda ps=ps, o=o, kc=kc, j=j, ts=ts: lambda e: e.transpose(
                            ps[:, j * 128:(j + 1) * 128], o[:, kc, ts * 128:(ts + 1) * 128], ident[:]))(),
                            reads=[ob], writes=[pb], inc=(j == 3))
                    eng = "vector" if ev % 2 == 0 else "scalar"
                    ev += 1
                    if eng == "vector":
                        fn = (lambda ot=ot, ps=ps, grp=grp: lambda e: e.tensor_copy(
                            ot[:, grp * 512:(grp + 1) * 512], ps[:]))()
                    else:
                        fn = (lambda ot=ot, ps=ps, grp=grp: lambda e: e.activation(
                            out=ot[:, grp * 512:(grp + 1) * 512], in_=ps[:], func=AF.Copy))()
                    S.op(eng, fn, reads=[pb], writes=[otb])
                r0 = tt * 512 + ts * 128
                S.dma("sync", (lambda ot=ot, r0=r0: lambda e: e.dma_start(
                    out=final_out[r0:r0 + 128, :], in_=ot[:]))(), otb, reads=[otb])


class WLoader:
    qi = 0

    def __init__(self, C, name, w_ap, KC, stage_rot, n_slab=2, is_bf16=False):
        self.C = C
        self.w = w_ap
        self.KC = KC
        self.stage = stage_rot
        self.is_bf16 = is_bf16
        self.slabs = C.rot(name + "sl", n_slab, [128, KC, 256], BF16, dma=is_bf16)
        self.wv = w_ap.rearrange("(kc p) n -> p kc n", p=128)
        self.cast_i = 0
        if KC <= 16:
            self.pieces = [(0, KC)]
        else:
            assert KC % 11 == 0
            self.pieces = [(i, 11) for i in range(0, KC, 11)]
        self.stages = []
        if not is_bf16:
            for i, (k0, nk) in enumerate(self.pieces):
                t = C.tile("%sst%d" % (name, i), [128, nk, 256], F32)
                self.stages.append((t, C.S.dbuf("%sst%d" % (name, i))))

    def load(self, c0):
        S = self.C.S
        sl, slb = self.slabs.next()
        if self.is_bf16:
            S.dma("sync", (lambda sl=sl, c0=c0: lambda e: e.dma_start(out=sl[:], in_=self.wv[:, :, c0:c0 + 256]))(),
                  slb, writes=[slb])
            return (sl, slb, [])
        parts = []
        for i, (k0, nk) in enumerate(self.pieces):
            st, stb = self.stages[i]
            WLoader.qi += 1
            S.dma("sync" if WLoader.qi % 2 == 0 else "scalar", (lambda st=st, k0=k0, nk=nk, c0=c0: lambda e: e.dma_start(
                out=st[:, 0:nk, :], in_=self.wv[:, k0:k0 + nk, c0:c0 + 256]))(), stb, writes=[stb])
            parts.append((st, stb, k0, nk))
        return (sl, slb, parts)

    def cast(self, h):
        S = self.C.S
        sl, slb, parts = h
        for (st, stb, k0, nk) in parts:
            eng = "vector" if self.cast_i % 2 == 0 else "scalar"
            self.cast_i += 1
            if eng == "vector":
                fn = (lambda sl=sl, st=st, k0=k0, nk=nk: lambda e: e.tensor_copy(sl[:, k0:k0 + nk, :], st[:, 0:nk, :]))()
            else:
                fn = (lambda sl=sl, st=st, k0=k0, nk=nk: lambda e: e.activation(
                    out=sl[:, k0:k0 + nk, :], in_=st[:, 0:nk, :], func=AF.Copy))()
            S.op(eng, fn, reads=[stb], writes=[slb])
        return (sl, slb)


def gemm_fm(C, *, T, TB, acts, weights, n_oc, epilogue, prologue_tb=None, reset_skip=()):
    S = C.S
    C.reset(skip=reset_skip)
    atiles = []
    for i, (a, KC) in enumerate(acts):
        t = C.tile("act%d" % i, [128, KC, TB], BF16)
        b = S.dbuf("act%d" % i)
        atiles.append((t, b, a, KC))
    stage = None
    loaders = []
    for j, (w_ap, KC, ai, col0, isb) in enumerate(weights):
        loaders.append(WLoader(C, "w%d" % j, w_ap, KC, stage, is_bf16=isb))
    ep_state = epilogue("alloc", None, None)
    n_oc2 = n_oc // 2
    for tb in range(T // TB):
        for (t, b, a, KC) in atiles:
            if callable(a):
                fn = (lambda t=t, a=a, tb=tb: lambda e: e.dma_start(out=t[:], in_=a(tb, e)))()
            else:
                src = a.rearrange("(kc p) t -> p kc t", p=128)[:, :, tb * TB:(tb + 1) * TB]
                fn = (lambda t=t, src=src: lambda e: e.dma_start(out=t[:], in_=src))()
            S.dma("scalar" if callable(a) else "sync", fn, b, writes=[b])
        if prologue_tb is not None:
            prologue_tb(tb)
        handles = [ld.load(weights[j][3]) for j, ld in enumerate(loaders)]
        slabs = [ld.cast(h) for ld, h in zip(loaders, handles)]
        for oc2 in range(n_oc2):
            nxt = None
            if oc2 + 1 < n_oc2:
                nxt = [ld.load(weights[j][3] + (oc2 + 1) * 256) for j, ld in enumerate(loaders)]
            for half in range(2):
                oc = oc2 * 2 + half
                for st in range(TB // 512):
                    psums = []
                    for j, (w_ap, KC, ai, col0, isb) in enumerate(weights):
                        ps, pb = C.psum.next()
                        at, ab = atiles[ai][0], atiles[ai][1]
                        sl, slb = slabs[j]
                        for kc in range(KC):
                            S.op("tensor", (lambda ps=ps, sl=sl, at=at, kc=kc, half=half, st=st, KC=KC: lambda e: e.matmul(
                                ps[:], sl[:, kc, half * 128:(half + 1) * 128], at[:, kc, st * 512:(st + 1) * 512],
                                start=(kc == 0), stop=(kc == KC - 1)))(),
                                reads=[slb, ab], writes=[pb], inc=(kc == KC - 1))
                        psums.append((ps, pb))
                    epilogue(oc, tb * TB + st * 512, psums)
                    if half == 0 and st == 0 and nxt is not None:
                        slabs_next = [ld.cast(h) for ld, h in zip(loaders, nxt)]
            if nxt is not None:
                slabs = slabs_next


def gemm_tm(C, *, T, TB, act, KC, w_ap, col0, n_blk, sink):
    S = C.S
    C.reset()
    at = C.tile("tmact", [128, KC, TB], BF16)
    ab = S.dbuf("tmact")
    stage = C.rot("tmst", 3, [128, 8, 512], F32)
    slabs = C.rot("tmsl", 2, [128, KC, 512], BF16, dma=False)
    outs = C.rot("tmout", 3, [128, 512], BF16)
    wv = w_ap.rearrange("(kc p) n -> p kc n", p=128)
    av = act.rearrange("(kc p) t -> p kc t", p=128)
    ci = 0
    ev = 0
    for tb in range(T // TB):
        S.dma("sync", (lambda tb=tb: lambda e: e.dma_start(out=at[:], in_=av[:, :, tb * TB:(tb + 1) * TB]))(),
              ab, writes=[ab])
        for blk in range(n_blk):
            c0 = col0 + blk * 512
            sl, slb = slabs.next()
            for k0 in range(0, KC, 8):
                st, stb = stage.next()
                S.dma("sync", (lambda st=st, k0=k0, c0=c0: lambda e: e.dma_start(
                    out=st[:], in_=wv[:, k0:k0 + 8, c0:c0 + 512]))(), stb, writes=[stb])
                eng = "vector" if ci % 2 == 0 else "scalar"
                ci += 1
                if eng == "vector":
                    fn = (lambda sl=sl, st=st, k0=k0: lambda e: e.tensor_copy(sl[:, k0:k0 + 8, :], st[:]))()
                else:
                    fn = (lambda sl=sl, st=st, k0=k0: lambda e: e.activation(out=sl[:, k0:k0 + 8, :], in_=st[:], func=AF.Copy))()
                S.op(eng, fn, reads=[stb], writes=[slb])
            for tch in range(TB // 128):
                ps, pb = C.psum.next()
                for kc in range(KC):
                    S.op("tensor", (lambda ps=ps, sl=sl, kc=kc, tch=tch: lambda e: e.matmul(
                        ps[:], at[:, kc, tch * 128:(tch + 1) * 128], sl[:, kc, :],
                        start=(kc == 0), stop=(kc == KC - 1)))(),
                        reads=[slb, ab], writes=[pb], inc=(kc == KC - 1))
                o, ob = outs.next()
                eng = "vector" if ev % 2 == 0 else "scalar"
                ev += 1
                if eng == "vector":
                    fn = (lambda o=o, ps=ps: lambda e: e.tensor_copy(o[:], ps[:]))()
                else:
                    fn = (lambda o=o, ps=ps: lambda e: e.activation(out=o[:], in_=ps[:], func=AF.Copy))()
                S.op(eng, fn, reads=[pb], writes=[ob])
                sink(blk, tb * TB + tch * 128, o, ob)


def build_program():
    nc = bass.Bass("TRN2", target_bir_lowering=False)
    def dt(name, shape, dtype, **kw):
        if name in _DEBUG_DUMP and "kind" not in kw:
            kw["kind"] = "ExternalOutput"
        return nc.dram_tensor(name, shape, dtype, **kw)

    def ext_in(name, shape, dtype):
        return dt(name, list(shape), dtype, kind="ExternalInput").ap()

    x = ext_in("x", [NT, D], F32)
    yout = dt("y", [NT, D], F32, kind="ExternalOutput").ap()
    ln_in_g = ext_in("ln_in_g", [D], F32)
    ln_in_b = ext_in("ln_in_b", [D], F32)
    w_in = ext_in("w_in", [DEPTH, D, 4096], F32)
    w_att = ext_in("w_att", [DEPTH, 1024, D], F32)
    w_four = ext_in("w_four", [DEPTH, 1024, D], F32)
    w_gate = ext_in("w_gate", [DEPTH, D, 4096], F32)
    b_gate = ext_in("b_gate", [DEPTH, 4096], F32)
    w_out = ext_in("w_out", [DEPTH, D, D], F32)
    ln1_g = ext_in("ln1_g", [DEPTH, D], F32)
    ln1_b = ext_in("ln1_b", [DEPTH, D], F32)
    w_fg = ext_in("w_ffn_gate", [DEPTH, D, DFF], F32)
    w_fu = ext_in("w_ffn_up", [DEPTH, D, DFF], F32)
    w_fd = ext_in("w_ffn_down", [DEPTH, DFF, D], F32)
    ln2_g = ext_in("ln2_g", [DEPTH, D], F32)
    ln2_b = ext_in("ln2_b", [DEPTH, D], F32)
    c_bf = ext_in("c_bf", [128, 2560], BF16)
    c_f32 = ext_in("c_f32", [128, 1160], F32)
    btile = ext_in("btile", [DEPTH, 128, 8, NH, 256], F32)
    meta = ext_in("meta", [1, 16], I32)

    MB = 1 << 20
    scrA = nc.dram_tensor("scrA", [D * NT], F32).ap()
    nB = (16 + 46 + 48 + 16 + 64) * MB // 2
    scrB = nc.dram_tensor("scrB", [nB], BF16).ap()

    def regB(off_mb, rows, cols):
        o = off_mb * MB // 2
        return scrB[o:o + rows * cols].rearrange("(r c) -> r c", c=cols)

    dump_list = []

    def dbg_or(name, ap, rows, cols, dty):
        if name in _DEBUG_DUMP:
            dump_list.append((name, ap, rows, cols, dty))
        return ap

    xres = dbg_or("xres", scrA.rearrange("(r c) -> r c", c=NT), D, NT, F32)
    xbf = dbg_or("xbf", regB(0, D, NT), D, NT, BF16)
    qT = dbg_or("qT", regB(16, 1024, NT), 1024, NT, BF16)
    kext = dbg_or("kext", regB(24, 1024, 4608), 1024, 4608, BF16)
    vext = dbg_or("vext", regB(33, 4608, 1024), 4608, 1024, BF16)
    agin = regB(42, AGR, 1024)
    w4p = dbg_or("w4p", regB(42, D, D), D, D, BF16)
    attT = dbg_or("attT", regB(54, 1024, NT), 1024, NT, BF16)
    hT = dbg_or("hT", regB(16, DFF, NT), DFF, NT, BF16)
    agout = regB(62, 4 * AGR, 1024)
    mT = dbg_or("mT", regB(62, D, NT), D, NT, BF16)
    agin2 = regB(110, 512, GT)
    agout2 = regB(126, 2048, GT)
    o2 = 126 * MB // 2
    yT = dbg_or("yT", scrB[o2:o2 + 2 * D * NT].bitcast(F32).rearrange("(r c) -> r c", c=NT), D, NT, F32)

    S = Sched(nc)
    C = Ctx(nc, S)

    cbf_t = C.tile("cbf", [128, 2560], BF16)
    cf_t = C.tile("cf32", [128, 1160], F32)
    lnp_t = C.tile("lnp", [128, 10, 16], F32)
    bg_t = C.tile("bgate", [128, DEPTH, 32], F32)
    C.base = (C.off + 63) // 64 * 64
    ones_s = cbf_t[:, 0:128]
    ones1 = cbf_t[:, 128:256]
    Rm = cbf_t[:, 256:768]
    M3 = cbf_t[:, 768:1536].rearrange("p (a b) -> p a b", a=6)
    CS = cbf_t[:, 1536:2560].rearrange("p (r k c) -> p r k c", r=2, k=2)
    ident = cf_t[:, 0:128]
    Twr2 = cf_t[:, 128:640]
    Twi2 = cf_t[:, 640:1152]
    eps_t = cf_t[:, 1152:1153]

    C.psum = Rot([(nc.alloc_psum_tensor("ps%d" % i, [128, 512], F32), S.buf("ps%d" % i)) for i in range(8)])

    cb = S.dbuf("consts")
    S.dma("sync", lambda e: e.dma_start(out=cbf_t[:], in_=c_bf), cb, writes=[cb])
    S.dma("sync", lambda e: e.dma_start(out=cf_t[:], in_=c_f32), cb, writes=[cb])
    lnsrc = [ln_in_g, ln_in_b]
    for l in range(DEPTH):
        lnsrc += [ln1_g[l], ln1_b[l], ln2_g[l], ln2_b[l]]
    for i, src in enumerate(lnsrc):
        S.dma("sync", (lambda i=i, src=src: lambda e: e.dma_start(
            out=lnp_t[:, i, :], in_=src.rearrange("(c p) -> p c", p=128), allow_slow_non_contiguous=True))(),
            cb, writes=[cb])
    for l in range(DEPTH):
        S.dma("sync", (lambda l=l: lambda e: e.dma_start(
            out=bg_t[:, l, :], in_=b_gate[l].rearrange("(c p) -> p c", p=128), allow_slow_non_contiguous=True))(),
            cb, writes=[cb])

    def mk_prologue(items):
        def pro(e, st):
            for (nm, idx, mx) in items:
                r = st.enter_context(e.register("r_" + nm))
                e.reg_load(r, meta[0:1, idx:idx + 1])
                S.dyn[nm] = e.snap(r, donate=True, min_val=0, max_val=mx)
        return pro

    prologues = {
        "sync": mk_prologue([("ex", 1, 3 * 4194304)]),
        "scalar": mk_prologue([("g", 0, 3)]),
        "gpsimd": mk_prologue([("ek_top", 2, (8 * 2048 + 3 * 512) * 1024 + 768), ("ek_bot", 3, (8 * 2048 + 3 * 512) * 1024 + 768),
                               ("ev_top", 4, (11 * 2048 + 3 * 512 + 256) * 1024), ("ev_bot", 5, (11 * 2048 + 3 * 512 + 256) * 1024)]),
    }
    agflat = agout.rearrange("r c -> (r c)")
    aginU = agin[0:4096, :].rearrange("r c -> (r c)").rearrange("(k t c) -> k t c", k=8, c=128)

    def khalo_src(nm):
        return agflat[bass.ds(S.dyn[nm], 2 * 2048 * 1024)].rearrange("(h r w) -> h r w", h=2, w=1024)[:, 0:512, 0:256]

    def vhalo_src(nm):
        return agflat[bass.ds(S.dyn[nm], 256 * 1024)].rearrange("(t c) -> t c", c=1024)

    def _build_body():
        phase_transpose_in(C, x, yT, ident)
        _stop(0)
        phase_ln(C, yT, lnp_t[:, 0, :], lnp_t[:, 1, :], xres, xbf, ones_s, eps_t)
        _stop(1)

        NL = DEPTH if _DEBUG_LAYERS is None else _DEBUG_LAYERS
        for l in range(NL):
            def sink_vu(blk, tok0, o, ob):
                if blk < 2:
                    dst = vext[256 + tok0:256 + tok0 + 128, blk * 512:(blk + 1) * 512]
                else:
                    k0 = (blk - 2) * 4
                    dst = aginU[k0:k0 + 4, tok0:tok0 + 128, :].rearrange("k p c -> p k c")
                    S.dma("gpsimd", (lambda o=o, dst=dst: lambda e: e.dma_start(
                        out=dst, in_=o[:].rearrange("p (k c) -> p k c", k=4)))(), ob, reads=[ob])
                    return
                S.dma("gpsimd", (lambda o=o, dst=dst: lambda e: e.dma_start(out=dst, in_=o[:]))(), ob, reads=[ob])

            gemm_tm(C, T=NT, TB=2048, act=xbf, KC=16, w_ap=w_in[l], col0=2048, n_blk=4, sink=sink_vu)

            C.reset()
            bb = S.dbuf("bnd")
            for qi, r0 in enumerate([0, 4, 56, 60]):
                S.dma("sync", (lambda qi=qi, r0=r0: lambda e: e.dma_start(
                    out=agin[5120 + qi * 256:5120 + (qi + 1) * 256, :], in_=vext[256 + r0 * 64:256 + r0 * 64 + 256, :]))(),
                    bb, writes=[bb])
            agb = S.buf("agout", multi=True)
            for k in (10, 11, 0, 1, 2, 3, 4, 5, 6, 7):
                S.op("gpsimd", (lambda k=k: lambda e: e.collective_compute(
                    "AllGather", ALU.bypass, replica_groups=[[0, 1, 2, 3], [4, 5, 6, 7]],
                    ins=[agin[512 * k:512 * (k + 1), :]], outs=[agout[2048 * k:2048 * (k + 1), :]]))(),
                    reads=[bb], writes=[agb])
            _stop(2 + 20 * l)
            def ep_qk(oc, t0, psums, st={}):
                if oc == "alloc":
                    st["o"] = C.rot("qko", 3, [128, 512], BF16)
                    st["i"] = 0
                    return st
                ps, pb = psums[0]
                o, ob = st["o"].next()
                eng = "vector" if st["i"] % 2 == 0 else "scalar"
                st["i"] += 1
                if eng == "vector":
                    fn = (lambda o=o, ps=ps: lambda e: e.tensor_copy(o[:], ps[:]))()
                else:
                    fn = (lambda o=o, ps=ps: lambda e: e.activation(out=o[:], in_=ps[:], func=AF.Copy))()
                S.op(eng, fn, reads=[pb], writes=[ob])
                if oc < 8:
                    dst = qT[oc * 128:(oc + 1) * 128, t0:t0 + 512]
                else:
                    dst = kext[(oc - 8) * 128:(oc - 7) * 128, 256 + t0:256 + t0 + 512]
                S.dma("gpsimd", (lambda o=o, dst=dst: lambda e: e.dma_start(out=dst, in_=o[:]))(), ob, reads=[ob])

            gemm_fm(C, T=NT, TB=2048, acts=[(xbf, 16)], weights=[(w_in[l], 16, 0, 0, False)], n_oc=16, epilogue=ep_qk,
                    reset_skip=("gpsimd",))

            _stop(3 + 20 * l)
            C.reset(skip=("gpsimd",))
            bb2 = S.dbuf("bndk")
            kb_v = agin[4096:5120, :].rearrange("c (q t) -> c q t", q=4)
            for qi, r0 in enumerate([0, 4, 56, 60]):
                S.dma("sync", (lambda qi=qi, r0=r0: lambda e: e.dma_start(
                    out=kb_v[:, qi, :], in_=kext[:, 256 + r0 * 64:256 + r0 * 64 + 256]))(), bb2, writes=[bb2])
            for k in (8, 9):
                S.op("gpsimd", (lambda k=k: lambda e: e.collective_compute(
                    "AllGather", ALU.bypass, replica_groups=[[0, 1, 2, 3], [4, 5, 6, 7]],
                    ins=[agin[512 * k:512 * (k + 1), :]], outs=[agout[2048 * k:2048 * (k + 1), :]]))(),
                    reads=[bb2], writes=[agb])
            hb = S.dbuf("halo")
            S.dma("gpsimd", lambda e: e.dma_start(
                out=kext[:, 0:256].rearrange("(h c) t -> h c t", h=2), in_=khalo_src("ek_top")),
                hb, reads=[agb], writes=[hb])
            S.dma("gpsimd", lambda e: e.dma_start(
                out=kext[:, 4352:4608].rearrange("(h c) t -> h c t", h=2), in_=khalo_src("ek_bot")),
                hb, reads=[agb], writes=[hb])
            S.dma("gpsimd", lambda e: e.dma_start(out=vext[0:256, :], in_=vhalo_src("ev_top")),
                  hb, reads=[agb], writes=[hb])
            S.dma("gpsimd", lambda e: e.dma_start(out=vext[4352:4608, :], in_=vhalo_src("ev_bot")),
                  hb, reads=[agb], writes=[hb])

            _stop(4 + 20 * l)
            C.reset()
            X = C.tile("fftX", [128, 128, 256], BF16)
            Xb = S.dbuf("fftX")
            for cc in range(2):
                S.dma("sync", (lambda cc=cc: lambda e: e.dma_start(
                    out=X[:, :, cc * 128:(cc + 1) * 128],
                    in_=agflat[cc * 2097152:][bass.ds(S.dyn["ex"], 2097152)].rearrange(
                        "(p t c) -> p t c", p=128, t=128)))(), Xb, writes=[Xb])
            P1r = C.rot("fftp1", 2, [128, 512], F32, dma=False)
            P2r = C.rot("fftp2", 2, [128, 512], F32, dma=False)
            Btr = C.rot("fftB", 2, [128, 32, 2, 256], BF16, dma=False)
            Yr = C.rot("fftY", 2, [128, 32, 2, 128], BF16)
            ag2_v = agin2.rearrange("(ri ch) (t2 t1) -> t2 ri ch t1", ri=2, t2=128)
            ev = 0
            for cbk in range(8):
                Bt, Btb = Btr.next()
                for c in range(32):
                    ch = cbk * 32 + c
                    ps, pb = C.psum.next()
                    S.op("tensor", (lambda ps=ps, ch=ch: lambda e: e.matmul(ps[:], X[:, :, ch], Rm, start=True, stop=True))(),
                         reads=[Xb, cb], writes=[pb])
                    p1, p1b = P1r.next()
                    p2, p2b = P2r.next()
                    S.op("vector", (lambda p1=p1, ps=ps: lambda e: e.tensor_tensor(out=p1[:], in0=ps[:], in1=Twr2, op=ALU.mult))(),
                         reads=[pb], writes=[p1b])
                    S.op("vector", (lambda p2=p2, ps=ps: lambda e: e.tensor_tensor(out=p2[:], in0=ps[:], in1=Twi2, op=ALU.mult))(),
                         reads=[pb], writes=[p2b])
                    S.op("gpsimd", (lambda Bt=Bt, c=c, p1=p1, p2=p2: lambda e: e.tensor_tensor(
                        out=Bt[:, c, 0, :], in0=p1[:, 0:256], in1=p2[:, 256:512], op=ALU.subtract))(),
                        reads=[p1b, p2b], writes=[Btb])
                    S.op("gpsimd", (lambda Bt=Bt, c=c, p1=p1, p2=p2: lambda e: e.tensor_tensor(
                        out=Bt[:, c, 1, :], in0=p2[:, 0:256], in1=p1[:, 256:512], op=ALU.add))(),
                        reads=[p1b, p2b], writes=[Btb])
                if cbk == 0 and "dbgB" in _DEBUG_DUMP and l == 0:
                    dbb = S.dbuf("dbgdma")
                    dbgB = dt("dbgB", [128, 32 * 2 * 256], BF16).ap()
                    dbgX = dt("dbgX", [128, 128 * 256], BF16).ap()
                    S.dma("sync", (lambda Bt=Bt: lambda e: e.dma_start(out=dbgB, in_=Bt[:].rearrange("p a b c -> p (a b c)")))(), dbb, reads=[Btb])
                    S.dma("sync", lambda e: e.dma_start(out=dbgX, in_=X[:].rearrange("p a b -> p (a b)")), dbb, reads=[Xb])
                Y, Yb = Yr.next()
                for q in range(8):
                    psr, prb = C.psum.next()
                    psi, pib = C.psum.next()
                    terms_r = [(0, 0, 0), (1, 0, 1), (2, 1, 0), (3, 1, 1)]
                    terms_i = [(0, 1, 0), (1, 1, 1), (4, 0, 0), (5, 0, 1)]
                    for n, (mi, ri, hf) in enumerate(terms_r):
                        S.op("tensor", (lambda psr=psr, Bt=Bt, q=q, mi=mi, ri=ri, hf=hf, n=n: lambda e: e.matmul(
                            psr[:].rearrange("p (a b) -> p a b", a=4), M3[:, mi, :],
                            Bt[:, 4 * q:4 * q + 4, ri, hf * 128:(hf + 1) * 128], start=(n == 0), stop=(n == 3)))(),
                            reads=[Btb], writes=[prb], inc=(n == 3))
                    for n, (mi, ri, hf) in enumerate(terms_i):
                        S.op("tensor", (lambda psi=psi, Bt=Bt, q=q, mi=mi, ri=ri, hf=hf, n=n: lambda e: e.matmul(
                            psi[:].rearrange("p (a b) -> p a b", a=4), M3[:, mi, :],
                            Bt[:, 4 * q:4 * q + 4, ri, hf * 128:(hf + 1) * 128], start=(n == 0), stop=(n == 3)))(),
                            reads=[Btb], writes=[pib], inc=(n == 3))
                    S.op("scalar", (lambda Y=Y, psr=psr, q=q: lambda e: e.activation(
                        out=Y[:, 4 * q:4 * q + 4, 0, :], in_=psr[:].rearrange("p (a b) -> p a b", a=4), func=AF.Copy))(),
                        reads=[prb], writes=[Yb])
                    S.op("scalar", (lambda Y=Y, psi=psi, q=q: lambda e: e.activation(
                        out=Y[:, 4 * q:4 * q + 4, 1, :], in_=psi[:].rearrange("p (a b) -> p a b", a=4), func=AF.Copy))(),
                        reads=[pib], writes=[Yb])
                if cbk == 0 and "dbgB" in _DEBUG_DUMP and l == 0:
                    dbgY = dt("dbgY", [128, 32 * 2 * 128], BF16).ap()
                    S.dma("sync", (lambda Y=Y: lambda e: e.dma_start(out=dbgY, in_=Y[:].rearrange("p a b c -> p (a b c)")))(), dbb, reads=[Yb])
                for ri in range(2):
                    S.dma("sync", (lambda Y=Y, cbk=cbk, ri=ri: lambda e: e.dma_start(
                        out=ag2_v[:, ri, cbk * 32:(cbk + 1) * 32, :], in_=Y[:, :, ri, :]))(), Yb, reads=[Yb])
            C.reset()
            ag2b = S.buf("agout2", multi=True)
            for k in range(16):
                S.op("gpsimd", (lambda k=k: lambda e: e.collective_compute(
                    "AllGather", ALU.bypass, replica_groups=[[0, 1, 2, 3], [4, 5, 6, 7]],
                    ins=[agin2[32 * k:32 * (k + 1), :]], outs=[agout2[128 * k:128 * (k + 1), :]]))(),
                    writes=[ag2b])

            _stop(5 + 20 * l)
            w4b = C.tile("w4b", [128, 8, D], BF16)
            w4bb = S.buf("w4b")
            w4st = C.rot("w4st", 2, [128, 8, 512], F32)
            w4v = w_four[l].rearrange("(kc p) n -> p kc n", p=128)
            for n4 in range(4):
                st, stb = w4st.next()
                S.dma("sync", (lambda st=st, n4=n4, w4v=w4v: lambda e: e.dma_start(out=st[:], in_=w4v[:, :, n4 * 512:(n4 + 1) * 512]))(),
                      stb, writes=[stb])
                S.op("vector", (lambda st=st, n4=n4, w4b=w4b: lambda e: e.tensor_copy(w4b[:, :, n4 * 512:(n4 + 1) * 512], st[:]))(),
                     reads=[stb], writes=[w4bb])
            w4o = C.rot("w4o", 3, [128, 512], BF16)
            for rr in range(4):
                for ri in range(2):
                    for hc in range(2):
                        R = rr * 4 + ri * 2 + hc
                        for n4 in range(4):
                            ps, pb = C.psum.next()
                            for kk in range(2):
                                S.op("tensor", (lambda ps=ps, ri=ri, kk=kk, hc=hc, rr=rr, n4=n4: lambda e: e.matmul(
                                    ps[:], CS[:, ri, kk, hc * 128:(hc + 1) * 128], w4b[:, rr * 2 + kk, n4 * 512:(n4 + 1) * 512],
                                    start=(kk == 0), stop=(kk == 1)))(), reads=[w4bb, cb], writes=[pb], inc=(kk == 1))
                            o, ob = w4o.next()
                            S.op("scalar", (lambda o=o, ps=ps: lambda e: e.activation(out=o[:], in_=ps[:], func=AF.Copy))(),
                                 reads=[pb], writes=[ob])
                            for pg in range(4):
                                kk2 = 8 * ri + 4 * hc + pg
                                S.dma("sync", (lambda o=o, kk2=kk2, rr=rr, pg=pg, n4=n4: lambda e: e.dma_start(
                                    out=w4p[kk2 * 128 + rr * 32:kk2 * 128 + rr * 32 + 32, n4 * 512:(n4 + 1) * 512],
                                    in_=o[32 * pg:32 * pg + 32, :]))(), ob, reads=[ob])

            _stop(6 + 20 * l)
            C.reset(skip=("gpsimd",))
            aT = C.rot("attaT", 1, [128, 2, NT], BF16)
            qtr = C.rot("attq", 2, [128, 2, NT], BF16)
            ktr = C.rot("attk", 2, [128, 2, 4608], BF16)
            ver = C.rot("attve", 2, [128, 36, 256], BF16)
            vor = C.rot("attvo", 2, [128, 35, 256], BF16)
            btr = C.rot("attbt", 2, [128, 8, 2, 256], F32)
            sbt = C.rot("attsb", 2, [128, 512], F32, dma=False)
            Er = C.rot("attE", 3, [128, 2, 4, 64], BF16, dma=False)
            rdr = C.rot("attrd", 2, [128, 128], F32, dma=False)
            ve_v = vext[0:4608, :].rearrange("(j p) c -> p j c", p=128)
            vo_v = vext[64:64 + 35 * 128, :].rearrange("(j p) c -> p j c", p=128)
            for hp in range(4):
                h0 = 2 * hp
                q_t, q_b = qtr.next()
                k_t, k_b = ktr.next()
                ve_t, ve_b = ver.next()
                vo_t, vo_b = vor.next()
                bt_t, bt_b = btr.next()
                a_t, a_b = aT.next()
                S.dma("sync", (lambda q_t=q_t, h0=h0: lambda e: e.dma_start(
                    out=q_t[:], in_=qT[h0 * 128:(h0 + 2) * 128, :].rearrange("(h p) t -> p h t", p=128)))(), q_b, writes=[q_b])
                S.dma("sync", (lambda k_t=k_t, h0=h0: lambda e: e.dma_start(
                    out=k_t[:], in_=kext[h0 * 128:(h0 + 2) * 128, :].rearrange("(h p) t -> p h t", p=128)))(), k_b, writes=[k_b])
                S.dma("sync", (lambda ve_t=ve_t, h0=h0: lambda e: e.dma_start(
                    out=ve_t[:], in_=ve_v[:, :, h0 * 128:(h0 + 2) * 128]))(), ve_b, writes=[ve_b])
                S.dma("sync", (lambda vo_t=vo_t, h0=h0: lambda e: e.dma_start(
                    out=vo_t[:], in_=vo_v[:, :, h0 * 128:(h0 + 2) * 128]))(), vo_b, writes=[vo_b])
                S.dma("sync", (lambda bt_t=bt_t, h0=h0, l=l: lambda e: e.dma_start(
                    out=bt_t[:], in_=btile[l, :, :, h0:h0 + 2, :]))(), bt_b, writes=[bt_b])
                def stage_a(r):
                    var = 0
                    if r < 4:
                        var = 1 + r
                    elif r > 60:
                        var = 5 + (r - 61)
                    pss, psb = C.psum.next()
                    for hh in range(2):
                        for jj in range(4):
                            S.op("tensor", (lambda pss=pss, hh=hh, r=r, jj=jj, k_t=k_t, q_t=q_t: lambda e: e.matmul(
                                pss[:, hh * 256 + jj * 64:hh * 256 + (jj + 1) * 64],
                                k_t[:, hh, 64 * r + 128 * jj:64 * r + 128 * jj + 128],
                                q_t[:, hh, 64 * r:64 * r + 64], start=True, stop=True))(),
                                reads=[k_b, q_b], writes=[psb], inc=(hh == 1 and jj == 3))
                    sb, sbb = sbt.next()
                    S.op("vector", (lambda sb=sb, pss=pss, var=var, bt_t=bt_t: lambda e: e.scalar_tensor_tensor(
                        out=sb[:], in0=pss[:], scalar=SCALE, in1=bt_t[:, var, :, :].rearrange("p a b -> p (a b)"),
                        op0=ALU.mult, op1=ALU.add))(), reads=[psb, bt_b], writes=[sbb])
                    E, Eb = Er.next()
                    S.op("scalar", (lambda E=E, sb=sb: lambda e: e.activation(
                        out=E[:].rearrange("p a b c -> p (a b c)"), in_=sb[:], func=AF.Exp))(),
                        reads=[sbb], writes=[Eb])
                    return (r, E, Eb)

                def stage_b(st_):
                    r, E, Eb = st_
                    pso, pob = C.psum.next()
                    psd, pdb = C.psum.next()
                    for hh in range(2):
                        for jj in range(4):
                            if r % 2 == 0:
                                vch = ve_t[:, r // 2 + jj, hh * 128:(hh + 1) * 128]
                                vb = ve_b
                            else:
                                vch = vo_t[:, (r - 1) // 2 + jj, hh * 128:(hh + 1) * 128]
                                vb = vo_b
                            S.op("tensor", (lambda pso=pso, vch=vch, E=E, jj=jj, hh=hh: lambda e: e.matmul(
                                pso[:, hh * 64:(hh + 1) * 64], vch, E[:, hh, jj, :], start=(jj == 0), stop=(jj == 3)))(),
                                reads=[vb, Eb], writes=[pob], inc=(hh == 1 and jj == 3))
                    for jj in range(4):
                        S.op("tensor", (lambda psd=psd, E=E, jj=jj: lambda e: e.matmul(
                            psd[:, 0:128].rearrange("p (a b) -> p a b", a=2), ones1, E[:, :, jj, :],
                            start=(jj == 0), stop=(jj == 3)))(),
                            reads=[Eb, cb], writes=[pdb], inc=(jj == 3))
                    rd, rdb = rdr.next()
                    S.op("vector", (lambda rd=rd, psd=psd: lambda e: e.reciprocal(out=rd[:], in_=psd[:, 0:128]))(),
                         reads=[pdb], writes=[rdb])
                    S.op("vector", (lambda pso=pso, rd=rd, r=r, a_t=a_t: lambda e: e.tensor_tensor(
                        out=a_t[:, :, 64 * r:64 * r + 64], in0=pso[:, 0:128].rearrange("p (a b) -> p a b", a=2),
                        in1=rd[:].rearrange("p (a b) -> p a b", a=2), op=ALU.mult))(),
                        reads=[pob, rdb], writes=[a_b])

                prev_ = None
                for r in range(65):
                    cur_ = stage_a(r) if r < 64 else None
                    if prev_ is not None:
                        stage_b(prev_)
                    prev_ = cur_
                S.dma("sync", (lambda a_t=a_t, h0=h0: lambda e: e.dma_start(
                    out=attT[h0 * 128:(h0 + 2) * 128, :].rearrange("(h p) t -> p h t", p=128), in_=a_t[:]))(), a_b, reads=[a_b])

            _stop(7 + 20 * l)
            def ep_mix(oc, t0, psums, st={}, l=l):
                if oc == "alloc":
                    st["ga"] = C.rot("mxga", 2, [128, 512], F32, dma=False)
                    st["gf"] = C.rot("mxgf", 2, [128, 512], F32, dma=False)
                    st["m1"] = C.rot("mxm1", 2, [128, 512], F32, dma=False)
                    st["m"] = C.rot("mxm", 3, [128, 512], BF16)
                    return st
                (pa, pab), (pf, pfb), (pga, pgab), (pgf, pgfb) = psums
                ga, gab = st["ga"].next()
                gf, gfb = st["gf"].next()
                m1, m1b = st["m1"].next()
                m, mb = st["m"].next()
                S.op("scalar", (lambda ga=ga, pga=pga, oc=oc: lambda e: e.activation(
                    out=ga[:], in_=pga[:], func=AF.Sigmoid, bias=bg_t[:, l, oc:oc + 1], scale=1.0))(), reads=[pgab, cb], writes=[gab])
                S.op("scalar", (lambda gf=gf, pgf=pgf, oc=oc: lambda e: e.activation(
                    out=gf[:], in_=pgf[:], func=AF.Sigmoid, bias=bg_t[:, l, 16 + oc:17 + oc], scale=1.0))(), reads=[pgfb, cb], writes=[gfb])
                S.op("vector", (lambda ga=ga, pa=pa: lambda e: e.tensor_tensor(out=ga[:], in0=ga[:], in1=pa[:], op=ALU.mult))(),
                     reads=[gab, pab], writes=[gab])
                S.op("vector", (lambda gf=gf, pf=pf, m1=m1: lambda e: e.tensor_tensor(out=m1[:], in0=gf[:], in1=pf[:], op=ALU.mult))(),
                     reads=[gfb, pfb], writes=[m1b])
                S.op("vector", (lambda m=m, ga=ga, m1=m1: lambda e: e.tensor_tensor(out=m[:], in0=ga[:], in1=m1[:], op=ALU.add))(),
                     reads=[gab, m1b], writes=[mb])
                S.dma("gpsimd", (lambda m=m, oc=oc, t0=t0: lambda e: e.dma_start(
                    out=mT[oc * 128:(oc + 1) * 128, t0:t0 + 512], in_=m[:]))(), mb, reads=[mb])

            def vc_src(tb, e=None):
                return agout2.rearrange("(kc p) (g t) -> p kc g t", p=128, g=4)[
                    :, :, bass.ds(S.dyn["g"], 1), tb * 1024:(tb + 1) * 1024].rearrange("p kc g t -> p kc (g t)")

            gemm_fm(C, T=NT, TB=1024,
                    acts=[(attT, 8), (vc_src, 16), (xbf, 16)],
                    weights=[(w_att[l], 8, 0, 0, False), (w4p, 16, 1, 0, True),
                             (w_gate[l], 16, 2, 0, False), (w_gate[l], 16, 2, 2048, False)],
                    n_oc=16, epilogue=ep_mix)

            _stop(8 + 20 * l)
            def make_ep_res(srcname):
                def ep_res(oc, t0, psums, st={}):
                    if oc == "alloc":
                        st["x"] = C.rot(srcname + "x", 3, [128, 512], F32)
                        st["y"] = C.rot(srcname + "y", 3, [128, 512], F32)
                        return st
                    ps, pb = psums[0]
                    xt, xb_ = st["x"].next()
                    yt, yb_ = st["y"].next()
                    S.dma("sync", (lambda xt=xt, oc=oc, t0=t0: lambda e: e.dma_start(
                        out=xt[:], in_=xres[oc * 128:(oc + 1) * 128, t0:t0 + 512]))(), xb_, writes=[xb_])
                    S.op("vector", (lambda yt=yt, xt=xt, ps=ps: lambda e: e.scalar_tensor_tensor(
                        out=yt[:], in0=xt[:], scalar=ALPHA, in1=ps[:], op0=ALU.mult, op1=ALU.add))(),
                        reads=[xb_, pb], writes=[yb_])
                    S.dma("gpsimd", (lambda yt=yt, oc=oc, t0=t0: lambda e: e.dma_start(
                        out=yT[oc * 128:(oc + 1) * 128, t0:t0 + 512], in_=yt[:]))(), yb_, reads=[yb_])
                return ep_res

            gemm_fm(C, T=NT, TB=2048, acts=[(mT, 16)], weights=[(w_out[l], 16, 0, 0, False)], n_oc=16,
                    epilogue=make_ep_res("r1"))
            phase_ln(C, yT, lnp_t[:, 2 + 4 * l, :], lnp_t[:, 3 + 4 * l, :], xres, xbf, ones_s, eps_t)

            _stop(9 + 20 * l)
            def ep_ffn(oc, t0, psums, st={}):
                if oc == "alloc":
                    st["s"] = C.rot("ffs", 2, [128, 512], F32, dma=False)
                    st["h"] = C.rot("ffh", 3, [128, 512], BF16)
                    return st
                (pg, pgb), (pu, pub) = psums
                s, sb_ = st["s"].next()
                h, hb_ = st["h"].next()
                S.op("scalar", (lambda s=s, pg=pg: lambda e: e.activation(out=s[:], in_=pg[:], func=AF.Silu))(),
                     reads=[pgb], writes=[sb_])
                S.op("vector", (lambda h=h, s=s, pu=pu: lambda e: e.tensor_tensor(out=h[:], in0=s[:], in1=pu[:], op=ALU.mult))(),
                     reads=[sb_, pub], writes=[hb_])
                S.dma("gpsimd", (lambda h=h, oc=oc, t0=t0: lambda e: e.dma_start(
                    out=hT[oc * 128:(oc + 1) * 128, t0:t0 + 512], in_=h[:]))(), hb_, reads=[hb_])

            gemm_fm(C, T=NT, TB=2048, acts=[(xbf, 16)],
                    weights=[(w_fg[l], 16, 0, 0, False), (w_fu[l], 16, 0, 0, False)], n_oc=44, epilogue=ep_ffn)

            _stop(10 + 20 * l)
            gemm_fm(C, T=NT, TB=1024, acts=[(hT, 44)], weights=[(w_fd[l], 44, 0, 0, False)], n_oc=16,
                    epilogue=make_ep_res("r2"))
            _stop(11 + 20 * l)
            if l < NL - 1:
                phase_ln(C, yT, lnp_t[:, 4 + 4 * l, :], lnp_t[:, 5 + 4 * l, :], xres, xbf, ones_s, eps_t)
            else:
                phase_ln(C, yT, lnp_t[:, 4 + 4 * l, :], lnp_t[:, 5 + 4 * l, :], xres, xbf, ones_s, eps_t,
                         final_out=yout, ident=ident)


    try:
        _build_body()
    except _StopBuild:
        pass

    if dump_list:
        C.reset()
        db_ = S.dbuf("dumpcopy")
        for (name, ap, rows, cols, dty) in dump_list:
            dst = nc.dram_tensor(name, [rows, cols], dty, kind="ExternalOutput").ap()
            nchunk = 8
            rr_ = rows // nchunk
            for i in range(nchunk):
                S.dma("sync", (lambda dst=dst, ap=ap, i=i, rr_=rr_: lambda e: e.dma_start(
                    out=dst[i * rr_:(i + 1) * rr_, :], in_=ap[i * rr_:(i + 1) * rr_, :]))(), db_, writes=[db_])
    C.reset()
    with nc.Block() as block:
        S.run(block, prologues=prologues)
    return nc


def _consts_for_core(c):
    g = c % 4
    groupB = c >= 4
    cbf = np.zeros((128, 2560), np.float32)
    cbf[:, 0:128] = 1.0 / 2048.0
    cbf[:, 128:256] = 1.0
    p = np.arange(128)
    k = np.arange(128)
    R = np.zeros((128, 2, 2, 128), np.float64)
    if not groupB:
        ang = 2 * np.pi * np.outer(p, k) / 128.0
        for hf in range(2):
            R[:, 0, hf, :] = np.cos(ang)
            R[:, 1, hf, :] = -np.sin(ang)
    else:
        for hf in range(2):
            rows = np.arange(64) + 64 * hf
            ang = 2 * np.pi * np.outer(np.arange(64), k) / 64.0
            R[rows, 0, hf, :] = np.cos(ang)
            R[rows, 1, hf, :] = -np.sin(ang)
    cbf[:, 256:768] = R.reshape(128, 512)
    M3 = np.zeros((128, 6, 128), np.float64)
    b = np.arange(64)
    for hf in range(2):
        if not groupB:
            ang = 2 * np.pi * np.outer(p, 64 * hf + b) / 128.0
        else:
            ang = 2 * np.pi * np.outer(p, b) / 64.0
        M3[:, 0 + hf, 64 * hf:64 * hf + 64] = np.cos(ang)
        M3[:, 2 + hf, 64 * hf:64 * hf + 64] = np.sin(ang)
        M3[:, 4 + hf, 64 * hf:64 * hf + 64] = -np.sin(ang)
    cbf[:, 768:1536] = M3.reshape(128, 768)
    T = 8192.0 if groupB else 16384.0
    norm = 1.0 / math.sqrt(T * 256.0)
    CSm = np.zeros((128, 2, 2, 256), np.float64)
    cc = np.arange(256)
    for kk in range(2):
        ang = 2 * np.pi * np.outer(kk * 128 + p, cc) / 256.0
        CSm[:, 0, kk, :] = np.cos(ang) * norm
        CSm[:, 1, kk, :] = np.sin(ang) * norm
    cbf[:, 1536:2560] = CSm.reshape(128, 1024)

    cf = np.zeros((128, 1160), np.float32)
    cf[:, 0:128] = np.eye(128)
    if not groupB:
        ang = 2 * np.pi * np.outer(p, k) / 16384.0
    else:
        ang = 2 * np.pi * np.outer(p, k) / 8192.0
    twr = np.cos(ang)
    twi = -np.sin(ang)
    twr2 = np.concatenate([twr, twr, twr, twr], axis=1)
    twi2 = np.concatenate([twi, twi, twi, twi], axis=1)
    cf[:, 128:640] = twr2
    cf[:, 640:1152] = twi2
    cf[:, 1152] = EPS

    top_seq = (c in (0, 4, 6))
    bot_seq = (c in (3, 5, 7))
    meta = np.zeros((1, 16), np.int32)
    meta[0, 0] = g
    meta[0, 1] = g * 4194304
    rs, q = (g, 1) if top_seq else (g - 1, 3)
    meta[0, 2] = (8 * 2048 + rs * 512) * 1024 + q * 256
    meta[0, 4] = ((10 + q // 2) * 2048 + rs * 512 + (q % 2) * 256) * 1024
    rs, q = (g, 2) if bot_seq else (g + 1, 0)
    meta[0, 3] = (8 * 2048 + rs * 512) * 1024 + q * 256
    meta[0, 5] = ((10 + q // 2) * 2048 + rs * 512 + (q % 2) * 256) * 1024
    return cbf.astype(NPBF), cf, meta, top_seq, bot_seq


def _bias_tiles(rpb, top_seq, bot_seq):
    L = rpb.shape[0]
    out = np.full((L, 128, 8, NH, 4, 64), NEG, np.float32)
    pp = np.arange(128)
    kc = pp % 64
    cq = np.arange(64)
    cs = np.clip(cq - 8, 0, 48)
    valid = (kc[:, None] >= cs[None, :]) & (kc[:, None] < cs[None, :] + 16)
    cidx = np.clip(kc[:, None] - cq[None, :] + 15, 0, 30)
    for var in range(8):
        for jj in range(4):
            i = 2 * jj + pp // 64
            ridx = i + 3
            if var >= 1 and var <= 4 and top_seq:
                r = var - 1
                ridx = np.where(r + i < 4, i + 11, i + 3)
            if var >= 5 and bot_seq:
                r = 61 + (var - 5)
                ridx = np.where(r + i >= 68, i - 5, i + 3)
            ridx = np.clip(ridx, 0, 14)
            vals = rpb[:, :, ridx[:, None], cidx]
            vals = np.where(valid[None, None], vals, NEG)
            out[:, :, var, :, jj, :] = np.transpose(vals, (0, 2, 1, 3))
    return out.reshape(L, 128, 8, NH, 256)


_NC_CACHE = {}


def kernel(x_prompt, x_sample, ln_in_g, ln_in_b, w_in, rpb, w_att, w_four, w_gate, b_gate, w_out,
           ln1_g, ln1_b, w_ffn_gate, w_ffn_up, w_ffn_down, ln2_g, ln2_b):
    f32 = lambda a: np.ascontiguousarray(np.asarray(a, dtype=np.float32))
    xall = np.concatenate([f32(x_prompt).reshape(-1, D), f32(x_sample).reshape(-1, D)], axis=0)
    if "nc" not in _NC_CACHE:
        _NC_CACHE["nc"] = build_program()
    nc = _NC_CACHE["nc"]
    shared = {
        "ln_in_g": f32(ln_in_g), "ln_in_b": f32(ln_in_b), "w_in": f32(w_in), "w_att": f32(w_att),
        "w_four": f32(w_four), "w_gate": f32(w_gate), "b_gate": f32(b_gate), "w_out": f32(w_out),
        "ln1_g": f32(ln1_g), "ln1_b": f32(ln1_b), "w_ffn_gate": f32(w_ffn_gate), "w_ffn_up": f32(w_ffn_up),
        "w_ffn_down": f32(w_ffn_down), "ln2_g": f32(ln2_g), "ln2_b": f32(ln2_b),
    }
    rpb = f32(rpb)
    in_maps = []
    for c in range(NCORES):
        cbf, cf, meta, top_seq, bot_seq = _consts_for_core(c)
        m = dict(shared)
        m["x"] = xall[c * NT:(c + 1) * NT]
        m["c_bf"] = cbf
        m["c_f32"] = cf
        m["meta"] = meta
        m["btile"] = _bias_tiles(rpb, top_seq, bot_seq)
        in_maps.append(m)
    res = run_bass_kernel_spmd(nc, in_maps, core_ids=list(range(NCORES)))
    _NC_CACHE["last"] = res
    yall = np.concatenate([res.results[c]["y"] for c in range(NCORES)], axis=0)
    y_prompt = yall[:16384].reshape(1, 16384, D).astype(np.float32)
    y_sample = yall[16384:].reshape(2, 8192, D).astype(np.float32)
    return (y_prompt, y_sample)
```

```python
import math
from contextlib import ExitStack

import numpy as np
import ml_dtypes

import concourse.bass as bass
import concourse.mybir as mybir
from concourse.bass_utils import run_bass_kernel_spmd

F32 = mybir.dt.float32
BF16 = mybir.dt.bfloat16
I32 = mybir.dt.int32
AF = mybir.ActivationFunctionType
ALU = mybir.AluOpType
NPBF = ml_dtypes.bfloat16

NCORES = 8
NT = 4096
D = 2048
DFF = 5632
DEPTH = 2
NH = 8
ALPHA = (2.0 * DEPTH) ** 0.25
EPS = 1e-5
SCALE = 128 ** -0.5
NEG = -30000.0
GT = 16384
AGR = 6144

ENGS = ["tensor", "vector", "scalar", "gpsimd", "sync"]
_DEBUG_STOP = None
_DEBUG_DUMP = ()
_DEBUG_LAYERS = None


class _StopBuild(Exception):
    pass


def _stop(n):
    if _DEBUG_STOP is not None and n >= _DEBUG_STOP:
        raise _StopBuild()


class Buf:
    __slots__ = ("name", "w", "r", "slot", "multi")

    def __init__(self, name, slot=None, multi=False):
        self.name = name
        self.w = {}
        self.r = {}
        self.slot = slot
        self.multi = multi


class Sched:
    def __init__(self, nc, n_dslots=56):
        self.nc = nc
        self.sems = {}
        self.count = {}
        self.ops = {e: [] for e in ENGS}
        self.seen = {e: {} for e in ENGS}
        for e in ENGS:
            self.sems[e] = nc.alloc_semaphore(name="m_" + e)
            self.count[e] = 0
        self.slots = []
        for i in range(n_dslots):
            key = "d%d" % i
            self.sems[key] = nc.alloc_semaphore(name=key)
            self.count[key] = 0
            self.slots.append(key)
        self.next_slot = 0
        self.dyn = {}

    def reset_slots(self):
        self.phase_first = self.next_slot

    def dbuf(self, name, multi=False):
        assert self.next_slot - getattr(self, "phase_first", 0) < len(self.slots), "out of DMA semaphore slots"
        key = self.slots[self.next_slot % len(self.slots)]
        self.next_slot += 1
        return Buf(name, slot=key, multi=multi)

    def buf(self, name, multi=False):
        return Buf(name, multi=multi)

    def _waits(self, eng, reads, writes):
        need = {}

        def add(d):
            for k, v in d.items():
                if need.get(k, 0) < v:
                    need[k] = v
        for b in reads:
            add(b.w)
        for b in writes:
            if not b.multi:
                add(b.w)
            add(b.r)
        out = []
        seen = self.seen[eng]
        for k, v in need.items():
            if seen.get(k, 0) >= v:
                continue
            seen[k] = v
            out.append((k, v))
        return out

    def _record(self, key, val, reads, writes):
        for b in reads:
            if b.r.get(key, 0) < val:
                b.r[key] = val
        for b in writes:
            if b.multi:
                if b.w.get(key, 0) < val:
                    b.w[key] = val
            else:
                b.w = {key: val}
                b.r = {}

    def op(self, eng, fn, reads=(), writes=(), inc=True):
        waits = self._waits(eng, reads, writes)
        if eng == "tensor":
            waits = [(k, v) for (k, v) in waits if k != "tensor"]
        val = self.count[eng] + 1
        if inc:
            self.count[eng] = val
        self._record(eng, val, reads, writes)
        self.ops[eng].append((waits, fn, (eng, 1) if inc else None))

    def dma(self, eng, fn, dbuf, reads=(), writes=()):
        waits = self._waits(eng, reads, writes)
        key = dbuf.slot
        self.count[key] += 16
        self._record(key, self.count[key], reads, writes)
        self.ops[eng].append((waits, fn, (key, 16)))

    def full_barrier(self, skip=()):
        tgt = {k: v for k, v in self.count.items() if v > 0 and k not in skip}
        for e in ENGS:
            waits = []
            for k, v in tgt.items():
                if k == e and e == "tensor":
                    continue
                if self.seen[e].get(k, 0) >= v:
                    continue
                self.seen[e][k] = v
                waits.append((k, v))
            if waits:
                self.ops[e].append((waits, None, None))

    def run(self, block, prologues=None):
        prologues = prologues or {}
        sems = self.sems

        def replay(eng, e):
            for waits, fn, inc in self.ops[eng]:
                for k, v in waits:
                    e.wait_ge(sems[k], v)
                if fn is None:
                    continue
                ins = fn(e)
                if inc is not None:
                    ins.then_inc(sems[inc[0]], inc[1])

        def mk(eng):
            def body(e):
                if eng in prologues:
                    with ExitStack() as st:
                        prologues[eng](e, st)
                        replay(eng, e)
                else:
                    replay(eng, e)
            return body
        block.tensor(mk("tensor"))
        block.vector(mk("vector"))
        block.scalar(mk("scalar"))
        block.gpsimd(mk("gpsimd"))
        block.sync(mk("sync"))


class Rot:
    def __init__(self, items):
        self.items = items
        self.i = 0

    def next(self):
        it = self.items[self.i % len(self.items)]
        self.i += 1
        return it


class Ctx:
    def __init__(self, nc, S):
        self.nc = nc
        self.S = S
        self.off = 17408
        self.uid = 0
        self.base = 17408

    def reset(self, skip=()):
        self.S.full_barrier(skip)
        self.S.reset_slots()
        self.off = self.base

    def tile(self, name, shape, dtype):
        esz = 4 if dtype in (F32, I32) else 2
        n = 1
        for s in shape[1:]:
            n *= s
        nbytes = n * esz
        self.off = (self.off + 63) // 64 * 64
        assert self.off + nbytes <= 224 * 1024, ("SBUF overflow", name, self.off, nbytes)
        self.uid += 1
        t = self.nc.alloc_sbuf_tensor_at("%s_%d" % (name, self.uid), list(shape), dtype, offset=self.off)
        self.off += nbytes
        return t

    def rot(self, name, n, shape, dtype, dma=True):
        items = []
        for i in range(n):
            t = self.tile("%s%d" % (name, i), shape, dtype)
            b = self.S.dbuf("%s%d" % (name, i)) if dma else self.S.buf("%s%d" % (name, i))
            items.append((t, b))
        return Rot(items)


def phase_transpose_in(C, x, xT, ident):
    S = C.S
    C.reset()
    xin = C.rot("xin", 2, [128, D], F32)
    stg = C.rot("tstg", 2, [128, 16, 128], F32)
    xT_v = xT.rearrange("(fc p) t -> p fc t", p=128)
    ev = 0
    for tc in range(NT // 128):
        xt, xb = xin.next()
        S.dma("sync", (lambda xt=xt, tc=tc: lambda e: e.dma_start(out=xt[:], in_=x[tc * 128:(tc + 1) * 128, :]))(),
              xb, writes=[xb])
        st, sb = stg.next()
        for grp in range(4):
            ps, pb = C.psum.next()
            for j in range(4):
                fc = grp * 4 + j
                S.op("tensor", (lambda ps=ps, xt=xt, fc=fc, j=j: lambda e: e.transpose(
                    ps[:, j * 128:(j + 1) * 128], xt[:, fc * 128:(fc + 1) * 128], ident[:]))(),
                    reads=[xb], writes=[pb], inc=(j == 3))
            eng = "vector" if ev % 2 == 0 else "scalar"
            ev += 1
            if eng == "vector":
                fn = (lambda st=st, ps=ps, grp=grp: lambda e: e.tensor_copy(
                    st[:, grp * 4:(grp + 1) * 4, :], ps[:].rearrange("p (a b) -> p a b", a=4)))()
            else:
                fn = (lambda st=st, ps=ps, grp=grp: lambda e: e.activation(
                    out=st[:, grp * 4:(grp + 1) * 4, :], in_=ps[:].rearrange("p (a b) -> p a b", a=4),
                    func=AF.Copy))()
            S.op(eng, fn, reads=[pb], writes=[sb])
        S.dma("gpsimd", (lambda st=st, tc=tc: lambda e: e.dma_start(
            out=xT_v[:, :, tc * 128:(tc + 1) * 128], in_=st[:]))(), sb, reads=[sb])


def phase_ln(C, yT, gcol, bcol, xres, xbf, ones_s, eps_t, final_out=None, ident=None):
    S = C.S
    C.reset()
    yv = yT.rearrange("(kc p) t -> p kc t", p=128)
    ytl = C.rot("lny", 2, [128, 16, 512], F32)
    ybf_t = C.tile("lnybf", [128, 16, 512], BF16)
    ybf_b = S.buf("lnybf")
    ysq_t = C.tile("lnysq", [128, 16, 512], BF16)
    ysq_b = S.buf("lnysq")
    mean_t = C.tile("lnmean", [128, 512], F32)
    mean_b = S.buf("lnmean")
    var_t = C.tile("lnvar", [128, 512], F32)
    var_b = S.buf("lnvar")
    tmp_t = C.tile("lntmp", [128, 512], F32)
    tmp_b = S.buf("lntmp")
    cen = C.rot("lncen", 4, [128, 512], F32, dma=False)
    o32 = C.rot("lno32", 1, [128, 16, 512], F32)
    if final_out is None:
        obf = C.rot("lnobf", 1, [128, 16, 512], BF16)
        xres_v = xres.rearrange("(kc p) t -> p kc t", p=128)
        xbf_v = xbf.rearrange("(kc p) t -> p kc t", p=128)
    else:
        otok = C.rot("lnotok", 2, [128, D], F32)
    for tt in range(NT // 512):
        y, yb = ytl.next()
        S.dma("sync", (lambda y=y, tt=tt: lambda e: e.dma_start(out=y[:], in_=yv[:, :, tt * 512:(tt + 1) * 512]))(),
              yb, writes=[yb])
        S.op("scalar", (lambda y=y: lambda e: e.activation(out=ybf_t[:], in_=y[:], func=AF.Copy))(),
             reads=[yb], writes=[ybf_b])
        S.op("scalar", (lambda y=y: lambda e: e.activation(out=ysq_t[:], in_=y[:], func=AF.Square))(),
             reads=[yb], writes=[ysq_b])
        psm, pmb = C.psum.next()
        psq, pqb = C.psum.next()
        for kc in range(16):
            S.op("tensor", (lambda psm=psm, kc=kc: lambda e: e.matmul(
                psm[:], ones_s[:], ybf_t[:, kc, :], start=(kc == 0), stop=(kc == 15)))(),
                reads=[ybf_b], writes=[pmb], inc=(kc == 15))
        for kc in range(16):
            S.op("tensor", (lambda psq=psq, kc=kc: lambda e: e.matmul(
                psq[:], ones_s[:], ysq_t[:, kc, :], start=(kc == 0), stop=(kc == 15)))(),
                reads=[ysq_b], writes=[pqb], inc=(kc == 15))
        S.op("scalar", (lambda psm=psm: lambda e: e.activation(out=mean_t[:], in_=psm[:], func=AF.Copy))(),
             reads=[pmb], writes=[mean_b])
        S.op("vector", lambda e: e.tensor_tensor(out=tmp_t[:], in0=mean_t[:], in1=mean_t[:], op=ALU.mult),
             reads=[mean_b], writes=[tmp_b])
        S.op("vector", (lambda psq=psq: lambda e: e.tensor_tensor(
            out=var_t[:], in0=psq[:], in1=tmp_t[:], op=ALU.subtract))(),
            reads=[pqb, tmp_b], writes=[var_b])
        S.op("scalar", lambda e: e.activation(out=var_t[:], in_=var_t[:], func=AF.Sqrt, bias=eps_t[:, 0:1], scale=1.0),
             reads=[var_b], writes=[var_b])
        S.op("vector", lambda e: e.reciprocal(out=var_t[:], in_=var_t[:]), reads=[var_b], writes=[var_b])
        o, ob = o32.next()
        prev_ = None
        for kc in range(17):
            cur_ = None
            if kc < 16:
                c, cb = cen.next()
                S.op("vector", (lambda c=c, y=y, kc=kc: lambda e: e.tensor_tensor(
                    out=c[:], in0=y[:, kc, :], in1=mean_t[:], op=ALU.subtract))(),
                    reads=[yb, mean_b], writes=[cb])
                cur_ = (c, cb, kc)
            if prev_ is not None:
                pc, pcb, pkc = prev_
                S.op("vector", (lambda pc=pc: lambda e: e.tensor_tensor(
                    out=pc[:], in0=pc[:], in1=var_t[:], op=ALU.mult))(),
                    reads=[pcb, var_b], writes=[pcb])
                S.op("scalar", (lambda pc=pc, o=o, pkc=pkc: lambda e: e.activation(
                    out=o[:, pkc, :], in_=pc[:], func=AF.Identity, bias=bcol[:, pkc:pkc + 1], scale=gcol[:, pkc:pkc + 1]))(),
                    reads=[pcb], writes=[ob])
            prev_ = cur_
        if final_out is None:
            ob16, ob16b = obf.next()
            S.op("gpsimd", (lambda o=o, ob16=ob16: lambda e: e.tensor_copy(ob16[:], o[:]))(),
                 reads=[ob], writes=[ob16b])
            S.dma("sync", (lambda o=o, tt=tt: lambda e: e.dma_start(
                out=xres_v[:, :, tt * 512:(tt + 1) * 512], in_=o[:]))(), ob, reads=[ob])
            S.dma("sync", (lambda ob16=ob16, tt=tt: lambda e: e.dma_start(
                out=xbf_v[:, :, tt * 512:(tt + 1) * 512], in_=ob16[:]))(), ob16b, reads=[ob16b])
        else:
            ev = 0
            for ts in range(4):
                ot, otb = otok.next()
                for grp in range(4):
                    ps, pb = C.psum.next()
                    for j in range(4):
                        kc = grp * 4 + j
                        S.op("tensor", (lambda ps=ps, o=o, kc=kc, j=j, ts=ts: lambda e: e.transpose(
                            ps[:, j * 128:(j + 1) * 128], o[:, kc, ts * 128:(ts + 1) * 128], ident[:]))(),
                            reads=[ob], writes=[pb], inc=(j == 3))
                    eng = "vector" if ev % 2 == 0 else "scalar"
                    ev += 1
                    if eng == "vector":
                        fn = (lambda ot=ot, ps=ps, grp=grp: lambda e: e.tensor_copy(
                            ot[:, grp * 512:(grp + 1) * 512], ps[:]))()
                    else:
                        fn = (lambda ot=ot, ps=ps, grp=grp: lambda e: e.activation(
                            out=ot[:, grp * 512:(grp + 1) * 512], in_=ps[:], func=AF.Copy))()
                    S.op(eng, fn, reads=[pb], writes=[otb])
                r0 = tt * 512 + ts * 128
                S.dma("sync", (lambda ot=ot, r0=r0: lambda e: e.dma_start(
                    out=final_out[r0:r0 + 128, :], in_=ot[:]))(), otb, reads=[otb])


class WLoader:
    qi = 0

    def __init__(self, C, name, w_ap, KC, stage_rot, n_slab=2, is_bf16=False):
        self.C = C
        self.w = w_ap
        self.KC = KC
        self.stage = stage_rot
        self.is_bf16 = is_bf16
        self.slabs = C.rot(name + "sl", n_slab, [128, KC, 256], BF16, dma=is_bf16)
        self.wv = w_ap.rearrange("(kc p) n -> p kc n", p=128)
        self.cast_i = 0
        if KC <= 16:
            self.pieces = [(0, KC)]
        else:
            assert KC % 11 == 0
            self.pieces = [(i, 11) for i in range(0, KC, 11)]
        self.stages = []
        if not is_bf16:
            for i, (k0, nk) in enumerate(self.pieces):
                t = C.tile("%sst%d" % (name, i), [128, nk, 256], F32)
                self.stages.append((t, C.S.dbuf("%sst%d" % (name, i))))

    def load(self, c0):
        S = self.C.S
        sl, slb = self.slabs.next()
        if self.is_bf16:
            S.dma("sync", (lambda sl=sl, c0=c0: lambda e: e.dma_start(out=sl[:], in_=self.wv[:, :, c0:c0 + 256]))(),
                  slb, writes=[slb])
            return (sl, slb, [])
        parts = []
        for i, (k0, nk) in enumerate(self.pieces):
            st, stb = self.stages[i]
            WLoader.qi += 1
            S.dma("sync" if WLoader.qi % 2 == 0 else "scalar", (lambda st=st, k0=k0, nk=nk, c0=c0: lambda e: e.dma_start(
                out=st[:, 0:nk, :], in_=self.wv[:, k0:k0 + nk, c0:c0 + 256]))(), stb, writes=[stb])
            parts.append((st, stb, k0, nk))
        return (sl, slb, parts)

    def cast(self, h):
        S = self.C.S
        sl, slb, parts = h
        for (st, stb, k0, nk) in parts:
            eng = "vector" if self.cast_i % 2 == 0 else "scalar"
            self.cast_i += 1
            if eng == "vector":
                fn = (lambda sl=sl, st=st, k0=k0, nk=nk: lambda e: e.tensor_copy(sl[:, k0:k0 + nk, :], st[:, 0:nk, :]))()
            else:
                fn = (lambda sl=sl, st=st, k0=k0, nk=nk: lambda e: e.activation(
                    out=sl[:, k0:k0 + nk, :], in_=st[:, 0:nk, :], func=AF.Copy))()
            S.op(eng, fn, reads=[stb], writes=[slb])
        return (sl, slb)


def gemm_fm(C, *, T, TB, acts, weights, n_oc, epilogue, prologue_tb=None, reset_skip=()):
    S = C.S
    C.reset(skip=reset_skip)
    atiles = []
    nst = TB // 512
    for i, (a, KC) in enumerate(acts):
        t = C.tile("act%d" % i, [128, KC, TB], BF16)
        if callable(a):
            b0 = S.dbuf("act%d" % i)
            bl = [b0] * nst
        else:
            bl = [S.dbuf("act%d_%d" % (i, j)) for j in range(nst)]
        atiles.append((t, bl, a, KC))
    stage = None
    loaders = []
    for j, (w_ap, KC, ai, col0, isb) in enumerate(weights):
        loaders.append(WLoader(C, "w%d" % j, w_ap, KC, stage, is_bf16=isb))
    ep_state = epilogue("alloc", None, None)
    n_oc2 = n_oc // 2
    for tb in range(T // TB):
        for (t, bl, a, KC) in atiles:
            if callable(a):
                fn = (lambda t=t, a=a, tb=tb: lambda e: e.dma_start(out=t[:], in_=a(tb, e)))()
                S.dma("scalar", fn, bl[0], writes=[bl[0]])
            else:
                src = a.rearrange("(kc p) t -> p kc t", p=128)
                for j in range(nst):
                    t0_ = tb * TB + j * 512
                    fn = (lambda t=t, src=src, j=j, t0_=t0_: lambda e: e.dma_start(
                        out=t[:, :, j * 512:(j + 1) * 512], in_=src[:, :, t0_:t0_ + 512]))()
                    S.dma("sync", fn, bl[j], writes=[bl[j]])
        if prologue_tb is not None:
            prologue_tb(tb)
        handles = [ld.load(weights[j][3]) for j, ld in enumerate(loaders)]
        slabs = [ld.cast(h) for ld, h in zip(loaders, handles)]
        for oc2 in range(n_oc2):
            nxt = None
            if oc2 + 1 < n_oc2:
                nxt = [ld.load(weights[j][3] + (oc2 + 1) * 256) for j, ld in enumerate(loaders)]
            for half in range(2):
                oc = oc2 * 2 + half
                for st in range(TB // 512):
                    psums = []
                    for j, (w_ap, KC, ai, col0, isb) in enumerate(weights):
                        ps, pb = C.psum.next()
                        at, ab = atiles[ai][0], atiles[ai][1][st]
                        sl, slb = slabs[j]
                        for kc in range(KC):
                            S.op("tensor", (lambda ps=ps, sl=sl, at=at, kc=kc, half=half, st=st, KC=KC: lambda e: e.matmul(
                                ps[:], sl[:, kc, half * 128:(half + 1) * 128], at[:, kc, st * 512:(st + 1) * 512],
                                start=(kc == 0), stop=(kc == KC - 1)))(),
                                reads=[slb, ab], writes=[pb], inc=(kc == KC - 1))
                        psums.append((ps, pb))
                    epilogue(oc, tb * TB + st * 512, psums)
                    if half == 0 and st == 0 and nxt is not None:
                        slabs_next = [ld.cast(h) for ld, h in zip(loaders, nxt)]
            if nxt is not None:
                slabs = slabs_next


def gemm_tm(C, *, T, TB, act, KC, w_ap, col0, n_blk, sink):
    S = C.S
    C.reset()
    at = C.tile("tmact", [128, KC, TB], BF16)
    abl = [S.dbuf("tmact%d" % j) for j in range(TB // 512)]
    stage = C.rot("tmst", 3, [128, 8, 512], F32)
    slabs = C.rot("tmsl", 2, [128, KC, 512], BF16, dma=False)
    outs = C.rot("tmout", 3, [128, 512], BF16)
    wv = w_ap.rearrange("(kc p) n -> p kc n", p=128)
    av = act.rearrange("(kc p) t -> p kc t", p=128)
    ci = 0
    ev = 0
    for tb in range(T // TB):
        for j in range(TB // 512):
            t0_ = tb * TB + j * 512
            S.dma("sync", (lambda j=j, t0_=t0_: lambda e: e.dma_start(
                out=at[:, :, j * 512:(j + 1) * 512], in_=av[:, :, t0_:t0_ + 512]))(), abl[j], writes=[abl[j]])
        for blk in range(n_blk):
            c0 = col0 + blk * 512
            sl, slb = slabs.next()
            for k0 in range(0, KC, 8):
                st, stb = stage.next()
                S.dma("sync", (lambda st=st, k0=k0, c0=c0: lambda e: e.dma_start(
                    out=st[:], in_=wv[:, k0:k0 + 8, c0:c0 + 512]))(), stb, writes=[stb])
                eng = "vector" if ci % 2 == 0 else "scalar"
                ci += 1
                if eng == "vector":
                    fn = (lambda sl=sl, st=st, k0=k0: lambda e: e.tensor_copy(sl[:, k0:k0 + 8, :], st[:]))()
                else:
                    fn = (lambda sl=sl, st=st, k0=k0: lambda e: e.activation(out=sl[:, k0:k0 + 8, :], in_=st[:], func=AF.Copy))()
                S.op(eng, fn, reads=[stb], writes=[slb])
            for tch in range(TB // 128):
                ps, pb = C.psum.next()
                for kc in range(KC):
                    S.op("tensor", (lambda ps=ps, sl=sl, kc=kc, tch=tch: lambda e: e.matmul(
                        ps[:], at[:, kc, tch * 128:(tch + 1) * 128], sl[:, kc, :],
                        start=(kc == 0), stop=(kc == KC - 1)))(),
                        reads=[slb, abl[tch // 4]], writes=[pb], inc=(kc == KC - 1))
                o, ob = outs.next()
                eng = "vector" if ev % 2 == 0 else "scalar"
                ev += 1
                if eng == "vector":
                    fn = (lambda o=o, ps=ps: lambda e: e.tensor_copy(o[:], ps[:]))()
                else:
                    fn = (lambda o=o, ps=ps: lambda e: e.activation(out=o[:], in_=ps[:], func=AF.Copy))()
                S.op(eng, fn, reads=[pb], writes=[ob])
                sink(blk, tb * TB + tch * 128, o, ob)


def build_program():
    nc = bass.Bass("TRN2", target_bir_lowering=False)
    def dt(name, shape, dtype, **kw):
        if name in _DEBUG_DUMP and "kind" not in kw:
            kw["kind"] = "ExternalOutput"
        return nc.dram_tensor(name, shape, dtype, **kw)

    def ext_in(name, shape, dtype):
        return dt(name, list(shape), dtype, kind="ExternalInput").ap()

    x = ext_in("x", [NT, D], F32)
    yout = dt("y", [NT, D], F32, kind="ExternalOutput").ap()
    ln_in_g = ext_in("ln_in_g", [D], F32)
    ln_in_b = ext_in("ln_in_b", [D], F32)
    w_in = ext_in("w_in", [DEPTH, D, 4096], F32)
    w_att = ext_in("w_att", [DEPTH, 1024, D], F32)
    w_four = ext_in("w_four", [DEPTH, 1024, D], F32)
    w_gate = ext_in("w_gate", [DEPTH, D, 4096], F32)
    b_gate = ext_in("b_gate", [DEPTH, 4096], F32)
    w_out = ext_in("w_out", [DEPTH, D, D], F32)
    ln1_g = ext_in("ln1_g", [DEPTH, D], F32)
    ln1_b = ext_in("ln1_b", [DEPTH, D], F32)
    w_fg = ext_in("w_ffn_gate", [DEPTH, D, DFF], F32)
    w_fu = ext_in("w_ffn_up", [DEPTH, D, DFF], F32)
    w_fd = ext_in("w_ffn_down", [DEPTH, DFF, D], F32)
    ln2_g = ext_in("ln2_g", [DEPTH, D], F32)
    ln2_b = ext_in("ln2_b", [DEPTH, D], F32)
    c_bf = ext_in("c_bf", [128, 2560], BF16)
    c_f32 = ext_in("c_f32", [128, 1160], F32)
    btile = ext_in("btile", [DEPTH, 128, 8, NH, 256], F32)
    meta = ext_in("meta", [1, 16], I32)

    MB = 1 << 20
    scrA = nc.dram_tensor("scrA", [D * NT], F32).ap()
    nB = (16 + 46 + 48 + 16 + 64) * MB // 2
    scrB = nc.dram_tensor("scrB", [nB], BF16).ap()

    def regB(off_mb, rows, cols):
        o = off_mb * MB // 2
        return scrB[o:o + rows * cols].rearrange("(r c) -> r c", c=cols)

    dump_list = []

    def dbg_or(name, ap, rows, cols, dty):
        if name in _DEBUG_DUMP:
            dump_list.append((name, ap, rows, cols, dty))
        return ap

    xres = dbg_or("xres", scrA.rearrange("(r c) -> r c", c=NT), D, NT, F32)
    xbf = dbg_or("xbf", regB(0, D, NT), D, NT, BF16)
    qT = dbg_or("qT", regB(16, 1024, NT), 1024, NT, BF16)
    kext = dbg_or("kext", regB(24, 1024, 4608), 1024, 4608, BF16)
    vext = dbg_or("vext", regB(33, 4608, 1024), 4608, 1024, BF16)
    agin = regB(42, AGR, 1024)
    w4p = dbg_or("w4p", regB(42, D, D), D, D, BF16)
    attT = dbg_or("attT", regB(54, 1024, NT), 1024, NT, BF16)
    hT = dbg_or("hT", regB(16, DFF, NT), DFF, NT, BF16)
    agout = regB(62, 4 * AGR, 1024)
    mT = dbg_or("mT", regB(62, D, NT), D, NT, BF16)
    agin2 = regB(110, 512, GT)
    agout2 = regB(126, 2048, GT)
    o2 = 126 * MB // 2
    yT = dbg_or("yT", scrB[o2:o2 + 2 * D * NT].bitcast(F32).rearrange("(r c) -> r c", c=NT), D, NT, F32)

    S = Sched(nc)
    C = Ctx(nc, S)

    cbf_t = C.tile("cbf", [128, 2560], BF16)
    cf_t = C.tile("cf32", [128, 1160], F32)
    lnp_t = C.tile("lnp", [128, 10, 16], F32)
    bg_t = C.tile("bgate", [128, DEPTH, 32], F32)
    C.base = (C.off + 63) // 64 * 64
    ones_s = cbf_t[:, 0:128]
    ones1 = cbf_t[:, 128:256]
    Rm = cbf_t[:, 256:768]
    M3 = cbf_t[:, 768:1536].rearrange("p (a b) -> p a b", a=6)
    CS = cbf_t[:, 1536:2560].rearrange("p (r k c) -> p r k c", r=2, k=2)
    ident = cf_t[:, 0:128]
    Twr2 = cf_t[:, 128:640]
    Twi2 = cf_t[:, 640:1152]
    eps_t = cf_t[:, 1152:1153]

    C.psum = Rot([(nc.alloc_psum_tensor("ps%d" % i, [128, 512], F32), S.buf("ps%d" % i)) for i in range(8)])

    cb = S.dbuf("consts")
    S.dma("sync", lambda e: e.dma_start(out=cbf_t[:], in_=c_bf), cb, writes=[cb])
    S.dma("sync", lambda e: e.dma_start(out=cf_t[:], in_=c_f32), cb, writes=[cb])
    lnsrc = [ln_in_g, ln_in_b]
    for l in range(DEPTH):
        lnsrc += [ln1_g[l], ln1_b[l], ln2_g[l], ln2_b[l]]
    for i, src in enumerate(lnsrc):
        S.dma("sync", (lambda i=i, src=src: lambda e: e.dma_start(
            out=lnp_t[:, i, :], in_=src.rearrange("(c p) -> p c", p=128), allow_slow_non_contiguous=True))(),
            cb, writes=[cb])
    for l in range(DEPTH):
        S.dma("sync", (lambda l=l: lambda e: e.dma_start(
            out=bg_t[:, l, :], in_=b_gate[l].rearrange("(c p) -> p c", p=128), allow_slow_non_contiguous=True))(),
            cb, writes=[cb])

    def mk_prologue(items):
        def pro(e, st):
            for (nm, idx, mx) in items:
                r = st.enter_context(e.register("r_" + nm))
                e.reg_load(r, meta[0:1, idx:idx + 1])
                S.dyn[nm] = e.snap(r, donate=True, min_val=0, max_val=mx)
        return pro

    prologues = {
        "sync": mk_prologue([("ex", 1, 3 * 4194304)]),
        "scalar": mk_prologue([("g", 0, 3)]),
        "gpsimd": mk_prologue([("ek_top", 2, (8 * 2048 + 3 * 512) * 1024 + 768), ("ek_bot", 3, (8 * 2048 + 3 * 512) * 1024 + 768),
                               ("ev_top", 4, (11 * 2048 + 3 * 512 + 256) * 1024), ("ev_bot", 5, (11 * 2048 + 3 * 512 + 256) * 1024)]),
    }
    agflat = agout.rearrange("r c -> (r c)")
    aginU = agin[0:4096, :].rearrange("r c -> (r c)").rearrange("(k t c) -> k t c", k=8, c=128)

    def khalo_src(nm):
        return agflat[bass.ds(S.dyn[nm], 2 * 2048 * 1024)].rearrange("(h r w) -> h r w", h=2, w=1024)[:, 0:512, 0:256]

    def vhalo_src(nm):
        return agflat[bass.ds(S.dyn[nm], 256 * 1024)].rearrange("(t c) -> t c", c=1024)

    def _build_body():
        phase_transpose_in(C, x, yT, ident)
        _stop(0)
        phase_ln(C, yT, lnp_t[:, 0, :], lnp_t[:, 1, :], xres, xbf, ones_s, eps_t)
        _stop(1)

        NL = DEPTH if _DEBUG_LAYERS is None else _DEBUG_LAYERS
        for l in range(NL):
            def sink_vu(blk, tok0, o, ob):
                if blk < 2:
                    dst = vext[256 + tok0:256 + tok0 + 128, blk * 512:(blk + 1) * 512]
                else:
                    k0 = (blk - 2) * 4
                    dst = aginU[k0:k0 + 4, tok0:tok0 + 128, :].rearrange("k p c -> p k c")
                    S.dma("gpsimd", (lambda o=o, dst=dst: lambda e: e.dma_start(
                        out=dst, in_=o[:].rearrange("p (k c) -> p k c", k=4)))(), ob, reads=[ob])
                    return
                S.dma("gpsimd", (lambda o=o, dst=dst: lambda e: e.dma_start(out=dst, in_=o[:]))(), ob, reads=[ob])

            gemm_tm(C, T=NT, TB=2048, act=xbf, KC=16, w_ap=w_in[l], col0=2048, n_blk=4, sink=sink_vu)

            C.reset()
            bb = S.dbuf("bnd")
            for qi, r0 in enumerate([0, 4, 56, 60]):
                S.dma("sync", (lambda qi=qi, r0=r0: lambda e: e.dma_start(
                    out=agin[5120 + qi * 256:5120 + (qi + 1) * 256, :], in_=vext[256 + r0 * 64:256 + r0 * 64 + 256, :]))(),
                    bb, writes=[bb])
            agb = S.buf("agout", multi=True)
            for k in (10, 11, 0, 1, 2, 3, 4, 5, 6, 7):
                S.op("gpsimd", (lambda k=k: lambda e: e.collective_compute(
                    "AllGather", ALU.bypass, replica_groups=[[0, 1, 2, 3], [4, 5, 6, 7]],
                    ins=[agin[512 * k:512 * (k + 1), :]], outs=[agout[2048 * k:2048 * (k + 1), :]]))(),
                    reads=[bb], writes=[agb])
            _stop(2 + 20 * l)
            def ep_qk(oc, t0, psums, st={}):
                if oc == "alloc":
                    st["o"] = C.rot("qko", 3, [128, 512], BF16)
                    st["i"] = 0
                    return st
                ps, pb = psums[0]
                o, ob = st["o"].next()
                eng = "vector" if st["i"] % 2 == 0 else "scalar"
                st["i"] += 1
                if eng == "vector":
                    fn = (lambda o=o, ps=ps: lambda e: e.tensor_copy(o[:], ps[:]))()
                else:
                    fn = (lambda o=o, ps=ps: lambda e: e.activation(out=o[:], in_=ps[:], func=AF.Copy))()
                S.op(eng, fn, reads=[pb], writes=[ob])
                if oc < 8:
                    dst = qT[oc * 128:(oc + 1) * 128, t0:t0 + 512]
                else:
                    dst = kext[(oc - 8) * 128:(oc - 7) * 128, 256 + t0:256 + t0 + 512]
                S.dma("gpsimd", (lambda o=o, dst=dst: lambda e: e.dma_start(out=dst, in_=o[:]))(), ob, reads=[ob])

            gemm_fm(C, T=NT, TB=2048, acts=[(xbf, 16)], weights=[(w_in[l], 16, 0, 0, False)], n_oc=16, epilogue=ep_qk,
                    reset_skip=("gpsimd",))

            _stop(3 + 20 * l)
            C.reset(skip=("gpsimd",))
            bb2 = S.dbuf("bndk")
            kb_v = agin[4096:5120, :].rearrange("c (q t) -> c q t", q=4)
            for qi, r0 in enumerate([0, 4, 56, 60]):
                S.dma("sync", (lambda qi=qi, r0=r0: lambda e: e.dma_start(
                    out=kb_v[:, qi, :], in_=kext[:, 256 + r0 * 64:256 + r0 * 64 + 256]))(), bb2, writes=[bb2])
            for k in (8, 9):
                S.op("gpsimd", (lambda k=k: lambda e: e.collective_compute(
                    "AllGather", ALU.bypass, replica_groups=[[0, 1, 2, 3], [4, 5, 6, 7]],
                    ins=[agin[512 * k:512 * (k + 1), :]], outs=[agout[2048 * k:2048 * (k + 1), :]]))(),
                    reads=[bb2], writes=[agb])
            hb = S.dbuf("halo")
            S.dma("gpsimd", lambda e: e.dma_start(
                out=kext[:, 0:256].rearrange("(h c) t -> h c t", h=2), in_=khalo_src("ek_top")),
                hb, reads=[agb], writes=[hb])
            S.dma("gpsimd", lambda e: e.dma_start(
                out=kext[:, 4352:4608].rearrange("(h c) t -> h c t", h=2), in_=khalo_src("ek_bot")),
                hb, reads=[agb], writes=[hb])
            S.dma("gpsimd", lambda e: e.dma_start(out=vext[0:256, :], in_=vhalo_src("ev_top")),
                  hb, reads=[agb], writes=[hb])
            S.dma("gpsimd", lambda e: e.dma_start(out=vext[4352:4608, :], in_=vhalo_src("ev_bot")),
                  hb, reads=[agb], writes=[hb])

            _stop(4 + 20 * l)
            C.reset()
            X = C.tile("fftX", [128, 128, 256], BF16)
            Xb = S.dbuf("fftX")
            for cc in range(2):
                S.dma("sync", (lambda cc=cc: lambda e: e.dma_start(
                    out=X[:, :, cc * 128:(cc + 1) * 128],
                    in_=agflat[cc * 2097152:][bass.ds(S.dyn["ex"], 2097152)].rearrange(
                        "(p t c) -> p t c", p=128, t=128)))(), Xb, writes=[Xb])
            P1r = C.rot("fftp1", 2, [128, 512], F32, dma=False)
            P2r = C.rot("fftp2", 2, [128, 512], F32, dma=False)
            Btr = C.rot("fftB", 2, [128, 32, 2, 256], BF16, dma=False)
            Yr = C.rot("fftY", 2, [128, 32, 2, 128], BF16)
            ag2_v = agin2.rearrange("(ri ch) (t2 t1) -> t2 ri ch t1", ri=2, t2=128)
            ev = 0
            for cbk in range(8):
                Bt, Btb = Btr.next()
                for c in range(32):
                    ch = cbk * 32 + c
                    ps, pb = C.psum.next()
                    S.op("tensor", (lambda ps=ps, ch=ch: lambda e: e.matmul(ps[:], X[:, :, ch], Rm, start=True, stop=True))(),
                         reads=[Xb, cb], writes=[pb])
                    p1, p1b = P1r.next()
                    p2, p2b = P2r.next()
                    S.op("vector", (lambda p1=p1, ps=ps: lambda e: e.tensor_tensor(out=p1[:], in0=ps[:], in1=Twr2, op=ALU.mult))(),
                         reads=[pb], writes=[p1b])
                    S.op("vector", (lambda p2=p2, ps=ps: lambda e: e.tensor_tensor(out=p2[:], in0=ps[:], in1=Twi2, op=ALU.mult))(),
                         reads=[pb], writes=[p2b])
                    S.op("gpsimd", (lambda Bt=Bt, c=c, p1=p1, p2=p2: lambda e: e.tensor_tensor(
                        out=Bt[:, c, 0, :], in0=p1[:, 0:256], in1=p2[:, 256:512], op=ALU.subtract))(),
                        reads=[p1b, p2b], writes=[Btb])
                    S.op("gpsimd", (lambda Bt=Bt, c=c, p1=p1, p2=p2: lambda e: e.tensor_tensor(
                        out=Bt[:, c, 1, :], in0=p2[:, 0:256], in1=p1[:, 256:512], op=ALU.add))(),
                        reads=[p1b, p2b], writes=[Btb])
                if cbk == 0 and "dbgB" in _DEBUG_DUMP and l == 0:
                    dbb = S.dbuf("dbgdma")
                    dbgB = dt("dbgB", [128, 32 * 2 * 256], BF16).ap()
                    dbgX = dt("dbgX", [128, 128 * 256], BF16).ap()
                    S.dma("sync", (lambda Bt=Bt: lambda e: e.dma_start(out=dbgB, in_=Bt[:].rearrange("p a b c -> p (a b c)")))(), dbb, reads=[Btb])
                    S.dma("sync", lambda e: e.dma_start(out=dbgX, in_=X[:].rearrange("p a b -> p (a b)")), dbb, reads=[Xb])
                Y, Yb = Yr.next()
                for q in range(8):
                    psr, prb = C.psum.next()
                    psi, pib = C.psum.next()
                    terms_r = [(0, 0, 0), (1, 0, 1), (2, 1, 0), (3, 1, 1)]
                    terms_i = [(0, 1, 0), (1, 1, 1), (4, 0, 0), (5, 0, 1)]
                    for n, (mi, ri, hf) in enumerate(terms_r):
                        S.op("tensor", (lambda psr=psr, Bt=Bt, q=q, mi=mi, ri=ri, hf=hf, n=n: lambda e: e.matmul(
                            psr[:].rearrange("p (a b) -> p a b", a=4), M3[:, mi, :],
                            Bt[:, 4 * q:4 * q + 4, ri, hf * 128:(hf + 1) * 128], start=(n == 0), stop=(n == 3)))(),
                            reads=[Btb], writes=[prb], inc=(n == 3))
                    for n, (mi, ri, hf) in enumerate(terms_i):
                        S.op("tensor", (lambda psi=psi, Bt=Bt, q=q, mi=mi, ri=ri, hf=hf, n=n: lambda e: e.matmul(
                            psi[:].rearrange("p (a b) -> p a b", a=4), M3[:, mi, :],
                            Bt[:, 4 * q:4 * q + 4, ri, hf * 128:(hf + 1) * 128], start=(n == 0), stop=(n == 3)))(),
                            reads=[Btb], writes=[pib], inc=(n == 3))
                    S.op("scalar", (lambda Y=Y, psr=psr, q=q: lambda e: e.activation(
                        out=Y[:, 4 * q:4 * q + 4, 0, :], in_=psr[:].rearrange("p (a b) -> p a b", a=4), func=AF.Copy))(),
                        reads=[prb], writes=[Yb])
                    S.op("scalar", (lambda Y=Y, psi=psi, q=q: lambda e: e.activation(
                        out=Y[:, 4 * q:4 * q + 4, 1, :], in_=psi[:].rearrange("p (a b) -> p a b", a=4), func=AF.Copy))(),
                        reads=[pib], writes=[Yb])
                if cbk == 0 and "dbgB" in _DEBUG_DUMP and l == 0:
                    dbgY = dt("dbgY", [128, 32 * 2 * 128], BF16).ap()
                    S.dma("sync", (lambda Y=Y: lambda e: e.dma_start(out=dbgY, in_=Y[:].rearrange("p a b c -> p (a b c)")))(), dbb, reads=[Yb])
                for ri in range(2):
                    S.dma("sync", (lambda Y=Y, cbk=cbk, ri=ri: lambda e: e.dma_start(
                        out=ag2_v[:, ri, cbk * 32:(cbk + 1) * 32, :], in_=Y[:, :, ri, :]))(), Yb, reads=[Yb])
            C.reset()
            ag2b = S.buf("agout2", multi=True)
            for k in range(16):
                S.op("gpsimd", (lambda k=k: lambda e: e.collective_compute(
                    "AllGather", ALU.bypass, replica_groups=[[0, 1, 2, 3], [4, 5, 6, 7]],
                    ins=[agin2[32 * k:32 * (k + 1), :]], outs=[agout2[128 * k:128 * (k + 1), :]]))(),
                    writes=[ag2b])

            _stop(5 + 20 * l)
            w4b = C.tile("w4b", [128, 8, D], BF16)
            w4bb = S.buf("w4b")
            w4st = C.rot("w4st", 2, [128, 8, 512], F32)
            w4v = w_four[l].rearrange("(kc p) n -> p kc n", p=128)
            for n4 in range(4):
                st, stb = w4st.next()
                S.dma("sync", (lambda st=st, n4=n4, w4v=w4v: lambda e: e.dma_start(out=st[:], in_=w4v[:, :, n4 * 512:(n4 + 1) * 512]))(),
                      stb, writes=[stb])
                S.op("vector", (lambda st=st, n4=n4, w4b=w4b: lambda e: e.tensor_copy(w4b[:, :, n4 * 512:(n4 + 1) * 512], st[:]))(),
                     reads=[stb], writes=[w4bb])
            w4o = C.rot("w4o", 3, [128, 512], BF16)
            for rr in range(4):
                for ri in range(2):
                    for hc in range(2):
                        R = rr * 4 + ri * 2 + hc
                        for n4 in range(4):
                            ps, pb = C.psum.next()
                            for kk in range(2):
                                S.op("tensor", (lambda ps=ps, ri=ri, kk=kk, hc=hc, rr=rr, n4=n4: lambda e: e.matmul(
                                    ps[:], CS[:, ri, kk, hc * 128:(hc + 1) * 128], w4b[:, rr * 2 + kk, n4 * 512:(n4 + 1) * 512],
                                    start=(kk == 0), stop=(kk == 1)))(), reads=[w4bb, cb], writes=[pb], inc=(kk == 1))
                            o, ob = w4o.next()
                            S.op("scalar", (lambda o=o, ps=ps: lambda e: e.activation(out=o[:], in_=ps[:], func=AF.Copy))(),
                                 reads=[pb], writes=[ob])
                            for pg in range(4):
                                kk2 = 8 * ri + 4 * hc + pg
                                S.dma("sync", (lambda o=o, kk2=kk2, rr=rr, pg=pg, n4=n4: lambda e: e.dma_start(
                                    out=w4p[kk2 * 128 + rr * 32:kk2 * 128 + rr * 32 + 32, n4 * 512:(n4 + 1) * 512],
                                    in_=o[32 * pg:32 * pg + 32, :]))(), ob, reads=[ob])

            _stop(6 + 20 * l)
            C.reset(skip=("gpsimd",))
            aT = C.rot("attaT", 1, [128, 2, NT], BF16)
            qtr = C.rot("attq", 2, [128, 2, NT], BF16)
            ktr = C.rot("attk", 2, [128, 2, 4608], BF16)
            ver = C.rot("attve", 2, [128, 36, 256], BF16)
            vor = C.rot("attvo", 2, [128, 35, 256], BF16)
            btr = C.rot("attbt", 2, [128, 8, 2, 256], F32)
            sbt = C.rot("attsb", 2, [128, 512], F32, dma=False)
            Er = C.rot("attE", 3, [128, 2, 4, 64], BF16, dma=False)
            rdr = C.rot("attrd", 2, [128, 128], F32, dma=False)
            ve_v = vext[0:4608, :].rearrange("(j p) c -> p j c", p=128)
            vo_v = vext[64:64 + 35 * 128, :].rearrange("(j p) c -> p j c", p=128)
            for hp in range(4):
                h0 = 2 * hp
                q_t, q_b = qtr.next()
                k_t, k_b = ktr.next()
                ve_t, ve_b = ver.next()
                vo_t, vo_b = vor.next()
                bt_t, bt_b = btr.next()
                a_t, a_b = aT.next()
                S.dma("sync", (lambda q_t=q_t, h0=h0: lambda e: e.dma_start(
                    out=q_t[:], in_=qT[h0 * 128:(h0 + 2) * 128, :].rearrange("(h p) t -> p h t", p=128)))(), q_b, writes=[q_b])
                S.dma("sync", (lambda k_t=k_t, h0=h0: lambda e: e.dma_start(
                    out=k_t[:], in_=kext[h0 * 128:(h0 + 2) * 128, :].rearrange("(h p) t -> p h t", p=128)))(), k_b, writes=[k_b])
                S.dma("sync", (lambda ve_t=ve_t, h0=h0: lambda e: e.dma_start(
                    out=ve_t[:], in_=ve_v[:, :, h0 * 128:(h0 + 2) * 128]))(), ve_b, writes=[ve_b])
                S.dma("sync", (lambda vo_t=vo_t, h0=h0: lambda e: e.dma_start(
                    out=vo_t[:], in_=vo_v[:, :, h0 * 128:(h0 + 2) * 128]))(), vo_b, writes=[vo_b])
                S.dma("sync", (lambda bt_t=bt_t, h0=h0, l=l: lambda e: e.dma_start(
                    out=bt_t[:], in_=btile[l, :, :, h0:h0 + 2, :]))(), bt_b, writes=[bt_b])
                def stage_a(r):
                    var = 0
                    if r < 4:
                        var = 1 + r
                    elif r > 60:
                        var = 5 + (r - 61)
                    pss, psb = C.psum.next()
                    for hh in range(2):
                        for jj in range(4):
                            S.op("tensor", (lambda pss=pss, hh=hh, r=r, jj=jj, k_t=k_t, q_t=q_t: lambda e: e.matmul(
                                pss[:, hh * 256 + jj * 64:hh * 256 + (jj + 1) * 64],
                                k_t[:, hh, 64 * r + 128 * jj:64 * r + 128 * jj + 128],
                                q_t[:, hh, 64 * r:64 * r + 64], start=True, stop=True))(),
                                reads=[k_b, q_b], writes=[psb], inc=(hh == 1 and jj == 3))
                    sb, sbb = sbt.next()
                    S.op("vector", (lambda sb=sb, pss=pss, var=var, bt_t=bt_t: lambda e: e.scalar_tensor_tensor(
                        out=sb[:], in0=pss[:], scalar=SCALE, in1=bt_t[:, var, :, :].rearrange("p a b -> p (a b)"),
                        op0=ALU.mult, op1=ALU.add))(), reads=[psb, bt_b], writes=[sbb])
                    E, Eb = Er.next()
                    S.op("scalar", (lambda E=E, sb=sb: lambda e: e.activation(
                        out=E[:].rearrange("p a b c -> p (a b c)"), in_=sb[:], func=AF.Exp))(),
                        reads=[sbb], writes=[Eb])
                    return (r, E, Eb)

                def stage_b(st_):
                    r, E, Eb = st_
                    pso, pob = C.psum.next()
                    psd, pdb = C.psum.next()
                    for hh in range(2):
                        for jj in range(4):
                            if r % 2 == 0:
                                vch = ve_t[:, r // 2 + jj, hh * 128:(hh + 1) * 128]
                                vb = ve_b
                            else:
                                vch = vo_t[:, (r - 1) // 2 + jj, hh * 128:(hh + 1) * 128]
                                vb = vo_b
                            S.op("tensor", (lambda pso=pso, vch=vch, E=E, jj=jj, hh=hh: lambda e: e.matmul(
                                pso[:, hh * 64:(hh + 1) * 64], vch, E[:, hh, jj, :], start=(jj == 0), stop=(jj == 3)))(),
                                reads=[vb, Eb], writes=[pob], inc=(hh == 1 and jj == 3))
                    for jj in range(4):
                        S.op("tensor", (lambda psd=psd, E=E, jj=jj: lambda e: e.matmul(
                            psd[:, 0:128].rearrange("p (a b) -> p a b", a=2), ones1, E[:, :, jj, :],
                            start=(jj == 0), stop=(jj == 3)))(),
                            reads=[Eb, cb], writes=[pdb], inc=(jj == 3))
                    rd, rdb = rdr.next()
                    S.op("vector", (lambda rd=rd, psd=psd: lambda e: e.reciprocal(out=rd[:], in_=psd[:, 0:128]))(),
                         reads=[pdb], writes=[rdb])
                    S.op("vector", (lambda pso=pso, rd=rd, r=r, a_t=a_t: lambda e: e.tensor_tensor(
                        out=a_t[:, :, 64 * r:64 * r + 64], in0=pso[:, 0:128].rearrange("p (a b) -> p a b", a=2),
                        in1=rd[:].rearrange("p (a b) -> p a b", a=2), op=ALU.mult))(),
                        reads=[pob, rdb], writes=[a_b])

                prev_ = None
                for r in range(65):
                    cur_ = stage_a(r) if r < 64 else None
                    if prev_ is not None:
                        stage_b(prev_)
                    prev_ = cur_
                S.dma("sync", (lambda a_t=a_t, h0=h0: lambda e: e.dma_start(
                    out=attT[h0 * 128:(h0 + 2) * 128, :].rearrange("(h p) t -> p h t", p=128), in_=a_t[:]))(), a_b, reads=[a_b])

            _stop(7 + 20 * l)
            def ep_mix(oc, t0, psums, st={}, l=l):
                if oc == "alloc":
                    st["ga"] = C.rot("mxga", 2, [128, 512], F32, dma=False)
                    st["gf"] = C.rot("mxgf", 2, [128, 512], F32, dma=False)
                    st["m1"] = C.rot("mxm1", 2, [128, 512], F32, dma=False)
                    st["m"] = C.rot("mxm", 3, [128, 512], BF16)
                    return st
                (pa, pab), (pf, pfb), (pga, pgab), (pgf, pgfb) = psums
                ga, gab = st["ga"].next()
                gf, gfb = st["gf"].next()
                m1, m1b = st["m1"].next()
                m, mb = st["m"].next()
                S.op("scalar", (lambda ga=ga, pga=pga, oc=oc: lambda e: e.activation(
                    out=ga[:], in_=pga[:], func=AF.Sigmoid, bias=bg_t[:, l, oc:oc + 1], scale=1.0))(), reads=[pgab, cb], writes=[gab])
                S.op("scalar", (lambda gf=gf, pgf=pgf, oc=oc: lambda e: e.activation(
                    out=gf[:], in_=pgf[:], func=AF.Sigmoid, bias=bg_t[:, l, 16 + oc:17 + oc], scale=1.0))(), reads=[pgfb, cb], writes=[gfb])
                S.op("vector", (lambda ga=ga, pa=pa: lambda e: e.tensor_tensor(out=ga[:], in0=ga[:], in1=pa[:], op=ALU.mult))(),
                     reads=[gab, pab], writes=[gab])
                S.op("vector", (lambda gf=gf, pf=pf, m1=m1: lambda e: e.tensor_tensor(out=m1[:], in0=gf[:], in1=pf[:], op=ALU.mult))(),
                     reads=[gfb, pfb], writes=[m1b])
                S.op("vector", (lambda m=m, ga=ga, m1=m1: lambda e: e.tensor_tensor(out=m[:], in0=ga[:], in1=m1[:], op=ALU.add))(),
                     reads=[gab, m1b], writes=[mb])
                S.dma("gpsimd", (lambda m=m, oc=oc, t0=t0: lambda e: e.dma_start(
                    out=mT[oc * 128:(oc + 1) * 128, t0:t0 + 512], in_=m[:]))(), mb, reads=[mb])

            def vc_src(tb, e=None):
                return agout2.rearrange("(kc p) (g t) -> p kc g t", p=128, g=4)[
                    :, :, bass.ds(S.dyn["g"], 1), tb * 1024:(tb + 1) * 1024].rearrange("p kc g t -> p kc (g t)")

            gemm_fm(C, T=NT, TB=1024,
                    acts=[(attT, 8), (vc_src, 16), (xbf, 16)],
                    weights=[(w_att[l], 8, 0, 0, False), (w4p, 16, 1, 0, True),
                             (w_gate[l], 16, 2, 0, False), (w_gate[l], 16, 2, 2048, False)],
                    n_oc=16, epilogue=ep_mix)

            _stop(8 + 20 * l)
            def make_ep_res(srcname):
                def ep_res(oc, t0, psums, st={}):
                    if oc == "alloc":
                        st["x"] = C.rot(srcname + "x", 3, [128, 512], F32)
                        st["y"] = C.rot(srcname + "y", 3, [128, 512], F32)
                        return st
                    ps, pb = psums[0]
                    xt, xb_ = st["x"].next()
                    yt, yb_ = st["y"].next()
                    S.dma("sync", (lambda xt=xt, oc=oc, t0=t0: lambda e: e.dma_start(
                        out=xt[:], in_=xres[oc * 128:(oc + 1) * 128, t0:t0 + 512]))(), xb_, writes=[xb_])
                    S.op("vector", (lambda yt=yt, xt=xt, ps=ps: lambda e: e.scalar_tensor_tensor(
                        out=yt[:], in0=xt[:], scalar=ALPHA, in1=ps[:], op0=ALU.mult, op1=ALU.add))(),
                        reads=[xb_, pb], writes=[yb_])
                    S.dma("gpsimd", (lambda yt=yt, oc=oc, t0=t0: lambda e: e.dma_start(
                        out=yT[oc * 128:(oc + 1) * 128, t0:t0 + 512], in_=yt[:]))(), yb_, reads=[yb_])
                return ep_res

            gemm_fm(C, T=NT, TB=2048, acts=[(mT, 16)], weights=[(w_out[l], 16, 0, 0, False)], n_oc=16,
                    epilogue=make_ep_res("r1"))
            phase_ln(C, yT, lnp_t[:, 2 + 4 * l, :], lnp_t[:, 3 + 4 * l, :], xres, xbf, ones_s, eps_t)

            _stop(9 + 20 * l)
            def ep_ffn(oc, t0, psums, st={}):
                if oc == "alloc":
                    st["s"] = C.rot("ffs", 2, [128, 512], F32, dma=False)
                    st["h"] = C.rot("ffh", 3, [128, 512], BF16)
                    return st
                (pg, pgb), (pu, pub) = psums
                s, sb_ = st["s"].next()
                h, hb_ = st["h"].next()
                S.op("scalar", (lambda s=s, pg=pg: lambda e: e.activation(out=s[:], in_=pg[:], func=AF.Silu))(),
                     reads=[pgb], writes=[sb_])
                S.op("vector", (lambda h=h, s=s, pu=pu: lambda e: e.tensor_tensor(out=h[:], in0=s[:], in1=pu[:], op=ALU.mult))(),
                     reads=[sb_, pub], writes=[hb_])
                S.dma("gpsimd", (lambda h=h, oc=oc, t0=t0: lambda e: e.dma_start(
                    out=hT[oc * 128:(oc + 1) * 128, t0:t0 + 512], in_=h[:]))(), hb_, reads=[hb_])

            gemm_fm(C, T=NT, TB=2048, acts=[(xbf, 16)],
                    weights=[(w_fg[l], 16, 0, 0, False), (w_fu[l], 16, 0, 0, False)], n_oc=44, epilogue=ep_ffn)

            _stop(10 + 20 * l)
            gemm_fm(C, T=NT, TB=1024, acts=[(hT, 44)], weights=[(w_fd[l], 44, 0, 0, False)], n_oc=16,
                    epilogue=make_ep_res("r2"))
            _stop(11 + 20 * l)
            if l < NL - 1:
                phase_ln(C, yT, lnp_t[:, 4 + 4 * l, :], lnp_t[:, 5 + 4 * l, :], xres, xbf, ones_s, eps_t)
            else:
                phase_ln(C, yT, lnp_t[:, 4 + 4 * l, :], lnp_t[:, 5 + 4 * l, :], xres, xbf, ones_s, eps_t,
                         final_out=yout, ident=ident)


    try:
        _build_body()
    except _StopBuild:
        pass

    if dump_list:
        C.reset()
        db_ = S.dbuf("dumpcopy")
        for (name, ap, rows, cols, dty) in dump_list:
            dst = nc.dram_tensor(name, [rows, cols], dty, kind="ExternalOutput").ap()
            nchunk = 8
            rr_ = rows // nchunk
            for i in range(nchunk):
                S.dma("sync", (lambda dst=dst, ap=ap, i=i, rr_=rr_: lambda e: e.dma_start(
                    out=dst[i * rr_:(i + 1) * rr_, :], in_=ap[i * rr_:(i + 1) * rr_, :]))(), db_, writes=[db_])
    C.reset()
    with nc.Block() as block:
        S.run(block, prologues=prologues)
    return nc


def _consts_for_core(c):
    g = c % 4
    groupB = c >= 4
    cbf = np.zeros((128, 2560), np.float32)
    cbf[:, 0:128] = 1.0 / 2048.0
    cbf[:, 128:256] = 1.0
    p = np.arange(128)
    k = np.arange(128)
    R = np.zeros((128, 2, 2, 128), np.float64)
    if not groupB:
        ang = 2 * np.pi * np.outer(p, k) / 128.0
        for hf in range(2):
            R[:, 0, hf, :] = np.cos(ang)
            R[:, 1, hf, :] = -np.sin(ang)
    else:
        for hf in range(2):
            rows = np.arange(64) + 64 * hf
            ang = 2 * np.pi * np.outer(np.arange(64), k) / 64.0
            R[rows, 0, hf, :] = np.cos(ang)
            R[rows, 1, hf, :] = -np.sin(ang)
    cbf[:, 256:768] = R.reshape(128, 512)
    M3 = np.zeros((128, 6, 128), np.float64)
    b = np.arange(64)
    for hf in range(2):
        if not groupB:
            ang = 2 * np.pi * np.outer(p, 64 * hf + b) / 128.0
        else:
            ang = 2 * np.pi * np.outer(p, b) / 64.0
        M3[:, 0 + hf, 64 * hf:64 * hf + 64] = np.cos(ang)
        M3[:, 2 + hf, 64 * hf:64 * hf + 64] = np.sin(ang)
        M3[:, 4 + hf, 64 * hf:64 * hf + 64] = -np.sin(ang)
    cbf[:, 768:1536] = M3.reshape(128, 768)
    T = 8192.0 if groupB else 16384.0
    norm = 1.0 / math.sqrt(T * 256.0)
    CSm = np.zeros((128, 2, 2, 256), np.float64)
    cc = np.arange(256)
    for kk in range(2):
        ang = 2 * np.pi * np.outer(kk * 128 + p, cc) / 256.0
        CSm[:, 0, kk, :] = np.cos(ang) * norm
        CSm[:, 1, kk, :] = np.sin(ang) * norm
    cbf[:, 1536:2560] = CSm.reshape(128, 1024)

    cf = np.zeros((128, 1160), np.float32)
    cf[:, 0:128] = np.eye(128)
    if not groupB:
        ang = 2 * np.pi * np.outer(p, k) / 16384.0
    else:
        ang = 2 * np.pi * np.outer(p, k) / 8192.0
    twr = np.cos(ang)
    twi = -np.sin(ang)
    twr2 = np.concatenate([twr, twr, twr, twr], axis=1)
    twi2 = np.concatenate([twi, twi, twi, twi], axis=1)
    cf[:, 128:640] = twr2
    cf[:, 640:1152] = twi2
    cf[:, 1152] = EPS

    top_seq = (c in (0, 4, 6))
    bot_seq = (c in (3, 5, 7))
    meta = np.zeros((1, 16), np.int32)
    meta[0, 0] = g
    meta[0, 1] = g * 4194304
    rs, q = (g, 1) if top_seq else (g - 1, 3)
    meta[0, 2] = (8 * 2048 + rs * 512) * 1024 + q * 256
    meta[0, 4] = ((10 + q // 2) * 2048 + rs * 512 + (q % 2) * 256) * 1024
    rs, q = (g, 2) if bot_seq else (g + 1, 0)
    meta[0, 3] = (8 * 2048 + rs * 512) * 1024 + q * 256
    meta[0, 5] = ((10 + q // 2) * 2048 + rs * 512 + (q % 2) * 256) * 1024
    return cbf.astype(NPBF), cf, meta, top_seq, bot_seq


def _bias_tiles(rpb, top_seq, bot_seq):
    L = rpb.shape[0]
    out = np.full((L, 128, 8, NH, 4, 64), NEG, np.float32)
    pp = np.arange(128)
    kc = pp % 64
    cq = np.arange(64)
    cs = np.clip(cq - 8, 0, 48)
    valid = (kc[:, None] >= cs[None, :]) & (kc[:, None] < cs[None, :] + 16)
    cidx = np.clip(kc[:, None] - cq[None, :] + 15, 0, 30)
    for var in range(8):
        for jj in range(4):
            i = 2 * jj + pp // 64
            ridx = i + 3
            if var >= 1 and var <= 4 and top_seq:
                r = var - 1
                ridx = np.where(r + i < 4, i + 11, i + 3)
            if var >= 5 and bot_seq:
                r = 61 + (var - 5)
                ridx = np.where(r + i >= 68, i - 5, i + 3)
            ridx = np.clip(ridx, 0, 14)
            vals = rpb[:, :, ridx[:, None], cidx]
            vals = np.where(valid[None, None], vals, NEG)
            out[:, :, var, :, jj, :] = np.transpose(vals, (0, 2, 1, 3))
    return out.reshape(L, 128, 8, NH, 256)


_NC_CACHE = {}


def kernel(x_prompt, x_sample, ln_in_g, ln_in_b, w_in, rpb, w_att, w_four, w_gate, b_gate, w_out,
           ln1_g, ln1_b, w_ffn_gate, w_ffn_up, w_ffn_down, ln2_g, ln2_b):
    f32 = lambda a: np.ascontiguousarray(np.asarray(a, dtype=np.float32))
    xall = np.concatenate([f32(x_prompt).reshape(-1, D), f32(x_sample).reshape(-1, D)], axis=0)
    if "nc" not in _NC_CACHE:
        _NC_CACHE["nc"] = build_program()
    nc = _NC_CACHE["nc"]
    shared = {
        "ln_in_g": f32(ln_in_g), "ln_in_b": f32(ln_in_b), "w_in": f32(w_in), "w_att": f32(w_att),
        "w_four": f32(w_four), "w_gate": f32(w_gate), "b_gate": f32(b_gate), "w_out": f32(w_out),
        "ln1_g": f32(ln1_g), "ln1_b": f32(ln1_b), "w_ffn_gate": f32(w_ffn_gate), "w_ffn_up": f32(w_ffn_up),
        "w_ffn_down": f32(w_ffn_down), "ln2_g": f32(ln2_g), "ln2_b": f32(ln2_b),
    }
    rpb = f32(rpb)
    in_maps = []
    for c in range(NCORES):
        cbf, cf, meta, top_seq, bot_seq = _consts_for_core(c)
        m = dict(shared)
        m["x"] = xall[c * NT:(c + 1) * NT]
        m["c_bf"] = cbf
        m["c_f32"] = cf
        m["meta"] = meta
        m["btile"] = _bias_tiles(rpb, top_seq, bot_seq)
        in_maps.append(m)
    res = run_bass_kernel_spmd(nc, in_maps, core_ids=list(range(NCORES)))
    _NC_CACHE["last"] = res
    yall = np.concatenate([res.results[c]["y"] for c in range(NCORES)], axis=0)
    y_prompt = yall[:16384].reshape(1, 16384, D).astype(np.float32)
    y_sample = yall[16384:].reshape(2, 8192, D).astype(np.float32)
    return (y_prompt, y_sample)
```
